# Optimizing a Trainium2 kernel written in Bass

```python
import jax, jax.numpy as jnp
from jax import lax
import numpy as np

D_MODEL = 1024
BATCH = 4
SEQ = 8192
DEPTH = 4

CTX_LEN = 256
GRID_W = 64

NA_HEADS = D_MODEL // 128
HEAD_DIM = 64
NA_WIDTH = NA_HEADS * HEAD_DIM
NA_KH = 8
NA_KW = 16
ROPE_AXIS_DIM = HEAD_DIM // 2
ROPE_THETA = 10000.0
POOL_WIDTH = D_MODEL // 4
POOL_GROUPS = 4
POOL_GROUP_DIM = POOL_WIDTH // POOL_GROUPS
POOL_WINDOWS = (2, 4, 8, 16)
FOURIER_WIDTH = D_MODEL // 4
FOURIER_GROUPS = 4
FOURIER_GROUP_DIM = FOURIER_WIDTH // FOURIER_GROUPS
CONV_WIDTH = D_MODEL // 4
CONV_K = 31
N_BRANCH = 4
D_FF = 4 * D_MODEL
EPS = 1e-6

Q_OFF = 0
K_OFF = NA_WIDTH
V_OFF = 2 * NA_WIDTH
POOL_OFF = 3 * NA_WIDTH
FOUR_OFF = POOL_OFF + POOL_WIDTH
CONV_OFF = FOUR_OFF + FOURIER_WIDTH
GATE_OFF = CONV_OFF + 2 * CONV_WIDTH
IN_WIDTH = GATE_OFF + N_BRANCH * D_MODEL

kernel_name = "hybrid_na_pool_fourier_conv_dit"


def rmsnorm(x, g):
    xf = x.astype(jnp.float32)
    y = xf * lax.rsqrt(jnp.mean(xf * xf, axis=-1, keepdims=True) + EPS)
    return (y * g.astype(jnp.float32)).astype(x.dtype)


def ada(cvec, w_mod, b_mod, n_chunks):
    m = jax.nn.silu(cvec) @ w_mod[:, :n_chunks * D_MODEL] + b_mod[:n_chunks * D_MODEL]
    m = m[..., None, :]
    return jnp.split(m, n_chunks, axis=-1)


def modulate(h, shift, scale):
    return h * (1.0 + scale) + shift


def split_qkv(p):
    hs = p.shape[:-1] + (NA_HEADS, HEAD_DIM)
    return (p[..., Q_OFF:K_OFF].reshape(hs), p[..., K_OFF:V_OFF].reshape(hs), p[..., V_OFF:POOL_OFF].reshape(hs))


def axial_rope_tables(n, dtype):
    t = jnp.arange(n)
    row = (t // GRID_W).astype(jnp.float32)
    col = (t % GRID_W).astype(jnp.float32)
    inv = ROPE_THETA ** (-jnp.arange(0, ROPE_AXIS_DIM, 2, dtype=jnp.float32) / ROPE_AXIS_DIM)
    ang = jnp.concatenate([row[:, None] * inv, col[:, None] * inv], axis=-1)
    return jnp.cos(ang).astype(dtype), jnp.sin(ang).astype(dtype)


def _rotate(xh, cos, sin):
    half = xh.shape[-1] // 2
    x1, x2 = xh[..., :half], xh[..., half:]
    return jnp.concatenate([x1 * cos - x2 * sin, x2 * cos + x1 * sin], axis=-1)


def apply_axial_rope(x, cos_t, sin_t):
    nf = ROPE_AXIS_DIM // 2
    cr, sr = cos_t[:, None, :nf], sin_t[:, None, :nf]
    cc, sc = cos_t[:, None, nf:], sin_t[:, None, nf:]
    return jnp.concatenate([_rotate(x[..., :ROPE_AXIS_DIM], cr, sr), _rotate(x[..., ROPE_AXIS_DIM:], cc, sc)], axis=-1)


def neighborhood_attention(q, k, v, k_ctx, v_ctx, rpb):
    bsz, n, nh, hd = q.shape
    rows = n // GRID_W
    kh = min(NA_KH, rows)
    scale = hd ** -0.5
    qg = q.reshape(bsz, rows, GRID_W, nh, hd).transpose(1, 0, 2, 3, 4)
    kg = k.reshape(bsz, rows, GRID_W, nh, hd)
    vg = v.reshape(bsz, rows, GRID_W, nh, hd)
    cq = jnp.arange(GRID_W)
    col_start = jnp.clip(cq - NA_KW // 2, 0, GRID_W - NA_KW)
    col_idx = col_start[:, None] + jnp.arange(NA_KW)[None, :]
    dcol = col_idx - cq[:, None] + (NA_KW - 1)
    rpb_col = rpb[:, :, dcol]
    n_win = kh * NA_KW

    def one_row(args):
        r, q_r = args
        rs = jnp.clip(r - kh // 2, 0, rows - kh)
        k_blk = lax.dynamic_slice_in_dim(kg, rs, kh, axis=1)
        v_blk = lax.dynamic_slice_in_dim(vg, rs, kh, axis=1)
        k_win = k_blk[:, :, col_idx]
        v_win = v_blk[:, :, col_idx]
        drow = rs + jnp.arange(kh) - r + (NA_KH - 1)
        bias = rpb_col[:, drow].transpose(0, 2, 1, 3)
        s_win = jnp.einsum('bqhd,brqwhd->bhqrw', q_r, k_win).astype(jnp.float32) * scale + bias.astype(jnp.float32)
        s_ctx = jnp.einsum('bqhd,bchd->bhqc', q_r, k_ctx).astype(jnp.float32) * scale
        s = jnp.concatenate([s_win.reshape(bsz, nh, GRID_W, n_win), s_ctx], axis=-1)
        p = jax.nn.softmax(s, axis=-1).astype(v.dtype)
        p_win = p[..., :n_win].reshape(bsz, nh, GRID_W, kh, NA_KW)
        p_ctx = p[..., n_win:]
        return (jnp.einsum('bhqrw,brqwhd->bqhd', p_win, v_win)
                + jnp.einsum('bhqc,bchd->bqhd', p_ctx, v_ctx))

    out = lax.map(one_row, (jnp.arange(rows), qg))
    return out.transpose(1, 0, 2, 3, 4).reshape(bsz, n, nh * hd)


def context_attention(q, k, v):
    bsz, l = q.shape[0], q.shape[1]
    s = jnp.einsum('blhd,bmhd->bhlm', q, k).astype(jnp.float32) * (HEAD_DIM ** -0.5)
    p = jax.nn.softmax(s, axis=-1).astype(v.dtype)
    return jnp.einsum('bhlm,bmhd->blhd', p, v).reshape(bsz, l, NA_WIDTH)


def pool_mix(u, w_pool, pool_scale):
    bsz, n, _ = u.shape
    uf = u.astype(jnp.float32).reshape(bsz, n, POOL_GROUPS, POOL_GROUP_DIM)
    cs = jnp.concatenate([jnp.zeros_like(uf[:, :1]), jnp.cumsum(uf, axis=1)], axis=1)
    t = jnp.arange(n)
    outs = []
    for gi, w in enumerate(POOL_WINDOWS):
        lo = w // 2
        hi = w - lo - 1
        start = jnp.clip(t - lo, 0, n)
        end = jnp.clip(t + hi + 1, 0, n)
        csg = cs[:, :, gi]
        cnt = (end - start).astype(jnp.float32)[None, :, None]
        outs.append((csg[:, end] - csg[:, start]) / cnt - uf[:, :, gi])
    d = jnp.stack(outs, axis=2)
    y = jnp.einsum('bngc,gcd->bngd', d, w_pool.astype(jnp.float32)).reshape(bsz, n, POOL_WIDTH)
    return (y * pool_scale.astype(jnp.float32)).astype(u.dtype)


def fourier_mix(u):
    bsz, n, _ = u.shape
    uf = u.astype(jnp.float32).reshape(bsz, n, FOURIER_GROUPS, FOURIER_GROUP_DIM)
    y = jnp.fft.fft2(uf, axes=(1, 3), norm='ortho').real
    return y.reshape(bsz, n, FOURIER_WIDTH).astype(u.dtype)


def conv_module(u2, w_dw, b_dw, ln_g, ln_b):
    a, g = u2[..., :CONV_WIDTH], u2[..., CONV_WIDTH:]
    z = a * jax.nn.sigmoid(g)
    z = lax.conv_general_dilated(z, w_dw.reshape(CONV_K, 1, CONV_WIDTH).astype(z.dtype), window_strides=(1,),
                                 padding=[(CONV_K // 2, CONV_K // 2)], dimension_numbers=('NWC', 'WIO', 'NWC'),
                                 feature_group_count=CONV_WIDTH) + b_dw
    zf = z.astype(jnp.float32)
    mu = jnp.mean(zf, axis=-1, keepdims=True)
    var = jnp.mean(jnp.square(zf - mu), axis=-1, keepdims=True)
    zf = (zf - mu) * lax.rsqrt(var + EPS) * ln_g.astype(jnp.float32) + ln_b.astype(jnp.float32)
    return jax.nn.silu(zf).astype(u2.dtype)


def mixer_output(attn, p, w_pool, pool_scale, w_dw, b_dw, ln_g, ln_b,
                 w_br_attn, w_br_pool, w_br_fourier, w_br_conv, w_out):
    y_attn = attn @ w_br_attn
    y_pool = pool_mix(p[..., POOL_OFF:FOUR_OFF], w_pool, pool_scale) @ w_br_pool
    y_four = fourier_mix(p[..., FOUR_OFF:CONV_OFF]) @ w_br_fourier
    y_conv = conv_module(p[..., CONV_OFF:GATE_OFF], w_dw, b_dw, ln_g, ln_b) @ w_br_conv
    g = jax.nn.sigmoid(p[..., GATE_OFF:])
    merged = (g[..., :D_MODEL] * y_attn + g[..., D_MODEL:2 * D_MODEL] * y_pool
              + g[..., 2 * D_MODEL:3 * D_MODEL] * y_four + g[..., 3 * D_MODEL:] * y_conv)
    return merged @ w_out


def sq_relu_mlp(h, w1, w2):
    return jnp.square(jax.nn.relu(h @ w1)) @ w2


def setup_inputs(seed: int = 0) -> dict:
    key = jax.random.key(seed)
    ks = jax.random.split(key, 24)
    f32 = jnp.float32
    L, D = DEPTH, D_MODEL

    def nrm(k, shape, s):
        return jax.random.normal(k, shape, f32) * s

    return {
        "x": nrm(ks[0], (BATCH, SEQ, D), 1.0),
        "c": nrm(ks[1], (BATCH, D), 1.0),
        "ctx": nrm(ks[2], (BATCH, CTX_LEN, D), 1.0),
        "c_ctx": nrm(ks[3], (D,), 1.0),
        "w_mod": nrm(ks[4], (L, D, 6 * D), 0.5 * D ** -0.5),
        "b_mod": nrm(ks[5], (L, 6 * D), 0.02),
        "g_mix": 1.0 + nrm(ks[6], (L, D), 0.02),
        "g_ff": 1.0 + nrm(ks[7], (L, D), 0.02),
        "w_in": nrm(ks[8], (L, D, IN_WIDTH), D ** -0.5),
        "rpb": nrm(ks[9], (L, NA_HEADS, 2 * NA_KH - 1, 2 * NA_KW - 1), 0.1),
        "w_pool": nrm(ks[10], (L, POOL_GROUPS, POOL_GROUP_DIM, POOL_GROUP_DIM), POOL_GROUP_DIM ** -0.5),
        "pool_scale": 1.0 + nrm(ks[11], (L, POOL_WIDTH), 0.02),
        "w_dw": nrm(ks[12], (L, CONV_K, CONV_WIDTH), CONV_K ** -0.5),
        "b_dw": nrm(ks[13], (L, CONV_WIDTH), 0.02),
        "conv_ln_g": 1.0 + nrm(ks[14], (L, CONV_WIDTH), 0.02),
        "conv_ln_b": nrm(ks[15], (L, CONV_WIDTH), 0.02),
        "w_br_attn": nrm(ks[16], (L, NA_WIDTH, D), NA_WIDTH ** -0.5),
        "w_br_pool": nrm(ks[17], (L, POOL_WIDTH, D), POOL_WIDTH ** -0.5),
        "w_br_fourier": nrm(ks[18], (L, FOURIER_WIDTH, D), FOURIER_WIDTH ** -0.5),
        "w_br_conv": nrm(ks[19], (L, CONV_WIDTH, D), CONV_WIDTH ** -0.5),
        "w_out": nrm(ks[20], (L, D, D), D ** -0.5),
        "w_ff1": nrm(ks[21], (L, D, D_FF), D ** -0.5),
        "w_ff2": nrm(ks[22], (L, D_FF, D), D_FF ** -0.5),
        "g_final": 1.0 + nrm(ks[23], (D,), 0.02),
    }


def reference(x, c, ctx, c_ctx, w_mod, b_mod, g_mix, g_ff, w_in, rpb, w_pool, pool_scale, w_dw, b_dw,
              conv_ln_g, conv_ln_b, w_br_attn, w_br_pool, w_br_fourier, w_br_conv, w_out, w_ff1, w_ff2, g_final):
    n = x.shape[1]
    cos_t, sin_t = axial_rope_tables(n, x.dtype)
    h_ctx = ctx
    for l in range(DEPTH):
        last = l == DEPTH - 1
        tails = (w_pool[l], pool_scale[l], w_dw[l], b_dw[l], conv_ln_g[l], conv_ln_b[l],
                 w_br_attn[l], w_br_pool[l], w_br_fourier[l], w_br_conv[l], w_out[l])
        if last:
            sh1c, sc1c = ada(c_ctx, w_mod[l], b_mod[l], 2)
            hc = modulate(rmsnorm(h_ctx, g_mix[l]), sh1c, sc1c)
            kv = hc @ w_in[l][:, K_OFF:POOL_OFF]
            hs = kv.shape[:-1] + (NA_HEADS, HEAD_DIM)
            k_c = kv[..., :NA_WIDTH].reshape(hs)
            v_c = kv[..., NA_WIDTH:].reshape(hs)
        else:
            sh1c, sc1c, gt1c, sh2c, sc2c, gt2c = ada(c_ctx, w_mod[l], b_mod[l], 6)
            hc = modulate(rmsnorm(h_ctx, g_mix[l]), sh1c, sc1c)
            pc = hc @ w_in[l]
            q_c, k_c, v_c = split_qkv(pc)
            ctx_next = h_ctx + gt1c * mixer_output(context_attention(q_c, k_c, v_c), pc, *tails)
            hc2 = modulate(rmsnorm(ctx_next, g_ff[l]), sh2c, sc2c)
            ctx_next = ctx_next + gt2c * sq_relu_mlp(hc2, w_ff1[l], w_ff2[l])
        sh1, sc1, gt1, sh2, sc2, gt2 = ada(c, w_mod[l], b_mod[l], 6)
        h = modulate(rmsnorm(x, g_mix[l]), sh1, sc1)
        p = h @ w_in[l]
        q, k, v = split_qkv(p)
        q = apply_axial_rope(q, cos_t, sin_t)
        k = apply_axial_rope(k, cos_t, sin_t)
        attn = neighborhood_attention(q, k, v, k_c, v_c, rpb[l])
        x = x + gt1 * mixer_output(attn, p, *tails)
        h2 = modulate(rmsnorm(x, g_ff[l]), sh2, sc2)
        x = x + gt2 * sq_relu_mlp(h2, w_ff1[l], w_ff2[l])
        if not last:
            h_ctx = ctx_next
    return rmsnorm(x, g_final)
```

```python
import numpy as np
from contextlib import ExitStack
import concourse.bass as bass
import concourse.mybir as mybir
from concourse.bass_utils import run_bass_kernel_spmd

F32 = mybir.dt.float32
BF16 = mybir.dt.bfloat16
AF = mybir.ActivationFunctionType
ALU = mybir.AluOpType

D = 1024
NTOK = 8192
NCTX = 256
NT = NTOK + NCTX
GRID_W = 64
EPS = 1e-6
Q_OFF, K_OFF, V_OFF, POOL_OFF, FOUR_OFF, CONV_OFF, GATE_OFF = 0, 512, 1024, 1536, 1792, 2048, 2560
IN_WIDTH = 6656
NV = 134
GROUPS = [(g * 512, 512, 0) for g in range(16)] + [(NTOK, 256, 1)]
NBT = 21

ENGS = ("pe", "act", "dve", "pool", "sp")
DMAQ = ("sp", "act", "pool")
NDSEM = 10


class Buf:
    __slots__ = ("w", "r")

    def __init__(self):
        self.w = {}
        self.r = {}


class Prog:
    def __init__(self, nc):
        self.nc = nc
        self.es = ExitStack()
        self.sem = {e: self.es.enter_context(nc.semaphore("c_" + e)) for e in ENGS}
        self.base = {e: 0 for e in ENGS}
        self.dsem = {q: [self.es.enter_context(nc.semaphore("d_%s%d" % (q, i))) for i in range(NDSEM)] for q in DMAQ}
        self.dtot = {q: [0] * NDSEM for q in DMAQ}
        self.drr = {q: 0 for q in DMAQ}
        self.ops = None
        self.touched = None
        self.ninst = 0

    def begin(self):
        self.ops = {e: [] for e in ENGS}
        self.touched = set()

    def _deps(self, eng, reads, writes, wadd=()):
        deps = {}

        def add(evs, raw):
            for k, v in evs.items():
                if k[0] == "c" and k[1] == eng and (eng == "pe" or not raw):
                    continue
                if deps.get(k, -1) < v:
                    deps[k] = v
        for b in reads:
            add(b.w, True)
        for b in writes:
            add(b.w, False)
            add(b.r, False)
        for b in wadd:
            add(b.r, False)
        return deps

    def _mark(self, key, val, reads, writes, wadd=()):
        for b in wadd:
            self.touched.add(b)
            if b.w.get(key, -1) < val:
                b.w[key] = val
        for b in reads:
            self.touched.add(b)
            if b.r.get(key, -1) < val:
                b.r[key] = val
        for b in writes:
            self.touched.add(b)
            b.w = {key: val}
            b.r = {}

    def op(self, eng, fn, reads=(), writes=()):
        deps = self._deps(eng, reads, writes)
        idx = len(self.ops[eng])
        self.ops[eng].append([fn, deps, None, False])
        self._mark(("c", eng), idx, reads, writes)

    def dma(self, q, out, in_, reads=(), writes=(), wadd=()):
        self.dma_multi([(q, out, in_)], reads, writes, wadd)

    def dma_multi(self, parts, reads=(), writes=(), wadd=()):
        deps0 = self._deps(None, reads, writes, wadd)
        evs = []
        for (q, out, in_) in parts:
            deps = dict(deps0)
            i = self.drr[q]
            self.drr[q] = (i + 1) % NDSEM
            prev = self.dtot[q][i]
            self.dtot[q][i] = prev + 16
            if prev > 0 and deps.get(("d", q, i), -1) < prev:
                deps[("d", q, i)] = prev
            self.ops[q].append([lambda e, out=out, in_=in_: e.dma_start(out=out, in_=in_), deps, (q, i), False])
            evs.append((("d", q, i), prev + 16))
        for b in reads:
            self.touched.add(b)
            for key, val in evs:
                if b.r.get(key, -1) < val:
                    b.r[key] = val
        for b in writes:
            self.touched.add(b)
            b.w = {key: val for key, val in evs}
            b.r = {}
        for b in wadd:
            self.touched.add(b)
            for key, val in evs:
                if b.w.get(key, -1) < val:
                    b.w[key] = val

    def end(self):
        nc = self.nc
        tail = {q: {("d", q, i): self.dtot[q][i] for i in range(NDSEM) if self.dtot[q][i] > 0} for q in DMAQ}
        for e in ENGS:
            for o in self.ops[e]:
                for k, v in o[1].items():
                    if k[0] == "c":
                        self.ops[k[1]][v][3] = True
        semval = {}
        for e in ENGS:
            c = self.base[e]
            vals = []
            for o in self.ops[e]:
                if o[3]:
                    c += 1
                vals.append(c)
            semval[e] = vals
            self.base[e] = c
            self.ninst += len(vals)
        ops, sem, dsem = self.ops, self.sem, self.dsem

        def emit(ename):
            def body(e):
                waited = {}
                for fn, deps, dma, sig in ops[ename]:
                    for k, v in deps.items():
                        if k[0] == "c":
                            s, val = sem[k[1]], semval[k[1]][v]
                        else:
                            s, val = dsem[k[1]][k[2]], v
                        if waited.get(k, -1) >= val:
                            continue
                        waited[k] = val
                        e.wait_ge(s, val)
                    ins = fn(e)
                    if dma is not None:
                        ins.then_inc(dsem[dma[0]][dma[1]], 16)
                    elif sig:
                        ins.then_inc(sem[ename], 1)
                if ename in tail:
                    for k, v in tail[ename].items():
                        if waited.get(k, -1) < v:
                            e.wait_ge(dsem[k[1]][k[2]], v)
            return body
        with nc.Block() as block:
            if ops["pe"]:
                block.tensor(emit("pe"))
            if ops["act"]:
                block.scalar(emit("act"))
            if ops["dve"]:
                block.vector(emit("dve"))
            if ops["pool"]:
                block.gpsimd(emit("pool"))
            if ops["sp"]:
                block.sync(emit("sp"))
        for b in self.touched:
            b.w = {}
            b.r = {}
        self.ops = None

    def close(self):
        self.es.close()


class T:
    __slots__ = ("t", "b")

    def __init__(self, t):
        self.t = t
        self.b = Buf()


class Ring:
    def __init__(self, items):
        self.items = items
        self.i = 0

    def next(self):
        it = self.items[self.i % len(self.items)]
        self.i += 1
        return it


def _consts():
    c = {}
    t = np.arange(NTOK)
    row = (t // GRID_W).astype(np.float32)
    col = (t % GRID_W).astype(np.float32)
    inv = (np.float32(10000.0) ** (-np.arange(0, 32, 2, dtype=np.float32) / np.float32(32))).astype(np.float32)
    p = np.arange(128)
    d = p % 64
    blk = d // 16
    f = d % 16
    pos = np.where((blk < 2)[:, None], row[None, :], col[None, :]).astype(np.float32)
    ang = (pos * inv[f][:, None]).astype(np.float32)
    c["rope_cos"] = np.cos(ang).astype(np.float32)
    sgn = np.where((blk % 2) == 0, -1.0, 1.0).astype(np.float32)
    c["rope_sin"] = (np.sin(ang) * sgn[:, None]).astype(np.float32)
    partner = np.where((blk % 2) == 0, p + 16, p - 16)
    perm = np.zeros((128, 128), np.float32)
    perm[partner, p] = 1.0
    c["perm"] = perm
    k1 = np.arange(64)
    a = 2 * np.pi * np.outer(k1, k1) / 64.0
    C, S = np.cos(a), np.sin(a)
    w64 = np.zeros((128, 128), np.float64)
    w64[0:64, 0:64] = C.T
    w64[64:128, 0:64] = S.T
    w64[0:64, 64:128] = -S.T
    w64[64:128, 64:128] = C.T
    c["w64"] = w64.astype(np.float32)
    n2 = np.arange(128, dtype=np.int64)
    kk = (np.arange(64)[:, None] + 64 * np.arange(128)[None, :]).astype(np.int64)
    ph = (n2[:, None, None] * kk[None]) % 8192
    a3 = 2 * np.pi * ph.astype(np.float64) / 8192.0
    c["cf"] = np.cos(a3).astype(np.float32).reshape(128, 64 * 128)
    c["sf"] = np.sin(a3).astype(np.float32).reshape(128, 64 * 128)
    cd = 2 * np.pi * np.outer(np.arange(64), np.arange(64)) / 64.0
    cs = np.zeros((256, 512), np.float64)
    for g in range(4):
        cs[g * 64:(g + 1) * 64, g * 64:(g + 1) * 64] = np.cos(cd)
        cs[g * 64:(g + 1) * 64, 256 + g * 64:256 + (g + 1) * 64] = -np.sin(cd)
    c["cs"] = cs.astype(np.float32)
    a256 = 2 * np.pi * (np.outer(np.arange(256), np.arange(256)) % 256) / 256.0
    c["c256"] = np.cos(a256).astype(np.float32)
    c["s256"] = np.sin(a256).astype(np.float32)
    pin = np.zeros((2, 128, 4, 512), np.float32)
    wins = (2, 4, 8, 16)

    def cnt(n, w):
        tt = np.arange(n)
        lo = w // 2
        hi = w - lo - 1
        st = np.clip(tt - lo, 0, n)
        en = np.clip(tt + hi + 1, 0, n)
        return (en - st).astype(np.float32)
    for ch in range(2):
        for half in range(2):
            w = wins[ch * 2 + half]
            cl = np.float32(1.0) / cnt(NTOK, w)
            cc = np.float32(1.0) / cnt(NCTX, w)
            sl = slice(half * 64, (half + 1) * 64)
            pin[ch, sl, 0, :] = cl[:512]
            pin[ch, sl, 1, :] = np.float32(1.0) / np.float32(w)
            pin[ch, sl, 2, :] = cl[-512:]
            pin[ch, sl, 3, :256] = cc
            pin[ch, sl, 3, 256:] = 1.0
    c["poolinv"] = pin
    c["ident"] = np.eye(128, dtype=np.float32)
    return c


ATT_VARIANTS = [("int", 10, [8, 9, 10, 11, 12]), ("t0", 0, [0, 1, 2, 3]), ("t1", 1, [0, 1, 2, 3]),
                ("t62", 62, [60, 61, 62, 63]), ("t63", 63, [60, 61, 62, 63])]


def _bias_tables(rpb_l):
    out = np.full((128, 8, NBT, 128), -30000.0, np.float32)
    kr_l = (np.arange(128) // 64)[:, None]
    kc = (np.arange(128) % 64)[:, None]
    qr_l = (np.arange(128) // 64)[None, :]
    qc = (np.arange(128) % 64)[None, :]
    ci = 0
    for _, tq, keys in ATT_VARIANTS:
        for u in keys:
            r = 2 * tq + qr_l
            rs = np.clip(r - 4, 0, 120)
            kr = 2 * u + kr_l
            vrow = (kr >= rs) & (kr < rs + 8)
            drow = np.clip(kr - r + 7, 0, 14)
            cs_ = np.clip(qc - 8, 0, 48)
            vcol = (kc >= cs_) & (kc < cs_ + 16)
            dcol = np.clip(kc - qc + 15, 0, 30)
            valid = vrow & vcol
            drow_b = np.broadcast_to(drow, (128, 128))
            dcol_b = np.broadcast_to(dcol, (128, 128))
            for h in range(8):
                g = rpb_l[h][drow_b, dcol_b]
                out[:, h, ci, :] = np.where(valid, g, np.float32(-30000.0))
            ci += 1
    return out


def _vec(inp, l):
    v = np.zeros((128, NV), np.float32)
    v[:, 0:8] = inp["g_mix"][l].reshape(8, 128).T
    v[:, 8:16] = inp["g_ff"][l].reshape(8, 128).T
    v[:, 16:64] = inp["b_mod"][l].reshape(48, 128).T
    v[:, 64:66] = inp["pool_scale"][l].reshape(2, 128).T
    v[:, 66:68] = inp["b_dw"][l].reshape(2, 128).T
    v[:, 68:70] = inp["conv_ln_g"][l].reshape(2, 128).T
    v[:, 70:72] = inp["conv_ln_b"][l].reshape(2, 128).T
    wd = inp["w_dw"][l]
    for c in range(2):
        v[:, 72 + c * 31:72 + (c + 1) * 31] = wd[:, c * 128:(c + 1) * 128].T
    return v


def _wpool_bd(wp):
    o = np.zeros((2, 128, 128), np.float32)
    for g in range(4):
        c, h = g // 2, g % 2
        o[c, h * 64:(h + 1) * 64, h * 64:(h + 1) * 64] = wp[g]
    return o


def build(NL, dbg=()):
    nc = bass.Bass("TRN2", target_bir_lowering=False)
    P = Prog(nc)
    uid = {"n": 0}

    def sbt(name, shape, dt):
        uid["n"] += 1
        return nc.sbuf_tensor("%s_u%d" % (name, uid["n"]), shape, dt)

    def pst(name, shape, dt):
        uid["n"] += 1
        return nc.psum_tensor("%s_u%d" % (name, uid["n"]), shape, dt)

    def din(name, shape, dt=F32):
        return nc.dram_tensor(name, list(shape), dt, kind="ExternalInput").ap()

    def dscr(name, shape, dt):
        if name in dbg:
            return nc.dram_tensor(name, list(shape), dt, kind="ExternalOutput").ap()
        return nc.dram_tensor(name, list(shape), dt).ap()

    xin = din("xT", [8, 128, NT])
    cT = din("cT", [128, 8, 2])
    gfin = din("gfin", [128, 8])
    L = []
    for l in range(NL):
        L.append(dict(
            wmod=din("wmod%d" % l, [D, 6 * D]), vec=din("vec%d" % l, [128, NV]), win=din("win%d" % l, [D, IN_WIDTH]),
            wbr=din("wbr%d" % l, [1280, D]), wout=din("wout%d" % l, [D, D]), wff1=din("wff1%d" % l, [D, 4 * D]),
            wff2=din("wff2%d" % l, [4 * D, D]), wpool=din("wpool%d" % l, [2, 128, 128]),
            btab=din("btab%d" % l, [128, 8, NBT, 128])))
    K = dict(rope_cos=din("rope_cos", [128, NTOK]), rope_sin=din("rope_sin", [128, NTOK]), perm=din("perm", [128, 128]),
             w64=din("w64", [128, 128]), cf=din("cf", [128, 8192]), sf=din("sf", [128, 8192]), cs=din("cs", [256, 512]),
             c256=din("c256", [256, 256]), s256=din("s256", [256, 256]), poolinv=din("poolinv", [2, 128, 4, 512]), ident=din("ident", [128, 128]))
    outT = nc.dram_tensor("outT", [8, 128, NTOK], F32, kind="ExternalOutput").ap()

    xs = dscr("xs", [8, 128, NT], F32)
    qk_d = dscr("qk_d", [2, 4, 128, NT], BF16)
    v_d = dscr("v_d", [NT, 512], BF16)
    poolu_d = dscr("poolu_d", [2, 128, NT], F32)
    z_d = dscr("z_d", [NT, 512], BF16)
    convz_d = dscr("convz_d", [2, 128, NT], BF16)
    br_d = dscr("br_d", [10, 128, NT], BF16)
    zs_d = dscr("zs_d", [128, 128, 256], BF16)
    cfb_d = dscr("cfb_d", [128, 8192], BF16)
    sfb_d = dscr("sfb_d", [128, 8192], BF16)
    hT_d = dscr("hT_d", [8, 128, NT], BF16)
    h2_d = dscr("h2_d", [8, 128, NT], BF16)
    B_hT, B_h2, B_m = Buf(), Buf(), Buf()
    m_d = dscr("m_d", [8, 128, NT], BF16)
    B_qk, B_v, B_poolu, B_z, B_convz, B_br, B_zs, B_cfb, B_out = [Buf() for _ in range(9)]
    B_xsg = [Buf() for _ in range(17)]

    alt = {"i": 0}

    def evac_copy(out_ap, in_ap, reads, writes):
        alt["i"] += 1
        if alt["i"] % 2:
            P.op("act", lambda e: e.activation(out=out_ap, in_=in_ap, func=AF.Identity), reads=reads, writes=writes)
        else:
            P.op("dve", lambda e: e.tensor_copy(out=out_ap, in_=in_ap), reads=reads, writes=writes)

    with ExitStack() as gs:
        def gsb(name, shape, dt):
            return T(gs.enter_context(sbt(name, list(shape), dt)))
        ones_bf = gsb("ones_bf", [128, 128], BF16)
        ones_f = gsb("ones_f", [128, 128], F32)
        perm_bf = gsb("perm_bf", [128, 128], BF16)
        cs_bf = gsb("cs_bf", [128, 2, 512], BF16)
        w64_bf = gsb("w64_bf", [128, 128], BF16)
        ident_bf = gsb("ident_bf", [128, 128], BF16)
        c256_bf = gsb("c256_bf", [128, 2, 256], BF16)
        s256_bf = gsb("s256_bf", [128, 2, 256], BF16)
        gfin_t = gsb("gfin_t", [128, 8], F32)
        sc_t = gsb("sc_t", [128, 8, 2], F32)
        vec_t = gsb("vec_t", [128, NV], F32)
        mod_t = gsb("mod_t", [128, 2, 48], F32)
        gs1_t = gsb("gs1_t", [128, 2, 8], F32)
        gs2_t = gsb("gs2_t", [128, 2, 8], F32)
        zero_t = gsb("zero_t", [128, 1], F32)
        eps_t = gsb("eps_t", [128, 1], F32)
        wpool_bf = gsb("wpool_bf", [128, 2, 128], BF16)

        P.begin()
        with ExitStack() as es:
            def sb(name, shape, dt):
                return T(es.enter_context(sbt(name, list(shape), dt)))
            st = sb("st_a", [128, 2048], F32)
            st2 = sb("st_b", [128, 2048], F32)
            stb = sb("st_c", [128, 2048], BF16)
            stb2 = sb("st_d", [128, 2048], BF16)
            P.op("pool", lambda e: e.memset(ones_bf.t[:], 1.0), writes=[ones_bf.b])
            P.op("pool", lambda e: e.memset(ones_f.t[:], 1.0), writes=[ones_f.b])
            P.op("pool", lambda e: e.memset(zero_t.t[:], 0.0), writes=[zero_t.b])
            P.op("pool", lambda e: e.memset(eps_t.t[:], EPS), writes=[eps_t.b])
            for k in range(8):
                P.dma(("sp", "act")[k % 2], xs[k], xin[k], wadd=B_xsg)
            P.dma("sp", gfin_t.t[:], gfin, writes=[gfin_t.b])
            P.dma("sp", sc_t.t[:], cT, writes=[sc_t.b])
            P.op("act", lambda e: e.activation(out=sc_t.t[:], in_=sc_t.t[:], func=AF.Silu), reads=[sc_t.b], writes=[sc_t.b])
            P.dma("sp", st.t[:, 0:128], K["perm"], writes=[st.b])
            P.dma("sp", st.t[:, 128:256], K["w64"], writes=[st.b])
            stc = sb("st_cs", [128, 2, 512], F32)
            stq = sb("st_c256", [128, 2, 256], F32)
            sts = sb("st_s256", [128, 2, 256], F32)
            P.dma("sp", stc.t[:], K["cs"].rearrange("(c p) n -> p c n", p=128), writes=[stc.b])
            P.op("dve", lambda e: e.tensor_copy(out=perm_bf.t[:], in_=st.t[:, 0:128]), reads=[st.b], writes=[perm_bf.b])
            P.op("dve", lambda e: e.tensor_copy(out=w64_bf.t[:], in_=st.t[:, 128:256]), reads=[st.b], writes=[w64_bf.b])
            P.dma("sp", st.t[:, 256:384], K["ident"], writes=[st.b])
            P.op("dve", lambda e: e.tensor_copy(out=ident_bf.t[:], in_=st.t[:, 256:384]), reads=[st.b], writes=[ident_bf.b])
            P.op("dve", lambda e: e.tensor_copy(out=cs_bf.t[:], in_=stc.t[:]), reads=[stc.b], writes=[cs_bf.b])
            P.dma("act", stq.t[:], K["c256"].rearrange("(c p) n -> p c n", p=128), writes=[stq.b])
            P.dma("act", sts.t[:], K["s256"].rearrange("(c p) n -> p c n", p=128), writes=[sts.b])
            P.op("dve", lambda e: e.tensor_copy(out=c256_bf.t[:], in_=stq.t[:]), reads=[stq.b], writes=[c256_bf.b])
            P.op("dve", lambda e: e.tensor_copy(out=s256_bf.t[:], in_=sts.t[:]), reads=[sts.b], writes=[s256_bf.b])
            i = 0
            for src, dst in ((K["cf"], cfb_d), (K["sf"], sfb_d)):
                for j in range(4):
                    s32, s16 = (st, stb) if i % 2 == 0 else (st2, stb2)
                    sl = slice(j * 2048, (j + 1) * 2048)
                    P.dma("sp", s32.t[:], src[:, sl], writes=[s32.b])
                    eng = "dve" if i % 2 == 0 else "pool"
                    P.op(eng, lambda e, s32=s32, s16=s16: e.tensor_copy(out=s16.t[:], in_=s32.t[:]), reads=[s32.b], writes=[s16.b])
                    P.dma("act", dst[:, sl], s16.t[:], reads=[s16.b], wadd=[B_cfb])
                    i += 1
        P.end()

        def norm_groups(glist, gsf, shf, out_fn, store_fn=None):
            with ExitStack() as es:
                def sb(name, shape, dt):
                    return T(es.enter_context(sbt(name, list(shape), dt)))
                xg = Ring([sb("n_xg%d" % i, [128, 8, 512], F32) for i in range(2)])
                sq = Ring([sb("n_sq%d" % i, [128, 8, 512], BF16) for i in range(2)])
                rs = Ring([sb("n_rs%d" % i, [128, 512], F32) for i in range(2)])
                tm = Ring([sb("n_tm%d" % i, [128, 512], F32) for i in range(4)])
                psn = Ring([T(es.enter_context(pst("n_ps%d" % i, [128, 512], F32))) for i in range(2)])
                st = {}

                def stage1(i):
                    g, (t0, W, mi) = glist[i]
                    x_, s_, r_, p_ = xg.next(), sq.next(), rs.next(), psn.next()
                    P.dma_multi([("sp", x_.t[:, 0:4, 0:W], xs[0:4, :, t0:t0 + W].rearrange("k p t -> p k t")),
                                 ("sp", x_.t[:, 4:8, 0:W], xs[4:8, :, t0:t0 + W].rearrange("k p t -> p k t"))], reads=[B_xsg[g]], writes=[x_.b])
                    P.op("act", lambda e: e.activation(out=s_.t[:, :, 0:W], in_=x_.t[:, :, 0:W], func=AF.Square), reads=[x_.b], writes=[s_.b])
                    for k in range(8):
                        P.op("pe", lambda e, k=k: e.matmul(p_.t[:, 0:W], lhsT=ones_bf.t[:], rhs=s_.t[:, k, 0:W], start=(k == 0), stop=(k == 7)),
                             reads=[ones_bf.b, s_.b], writes=[p_.b])
                    P.op("act", lambda e: e.activation(out=r_.t[:, 0:W], in_=p_.t[:, 0:W], func=AF.Ln, bias=eps_t.t[:, 0:1], scale=1.0 / D),
                         reads=[p_.b, eps_t.b], writes=[r_.b])
                    P.op("act", lambda e: e.activation(out=r_.t[:, 0:W], in_=r_.t[:, 0:W], func=AF.Exp, scale=-0.5), reads=[r_.b], writes=[r_.b])
                    st[i] = (x_, r_)

                def stage2(i):
                    g, (t0, W, mi) = glist[i]
                    x_, r_ = st.pop(i)
                    for k in range(8):
                        t_ = tm.next()
                        P.op("dve", lambda e, k=k, t_=t_: e.scalar_tensor_tensor(
                            out=t_.t[:, 0:W], in0=x_.t[:, k, 0:W], scalar=gsf(mi, k), in1=r_.t[:, 0:W], op0=ALU.mult, op1=ALU.mult),
                            reads=[x_.b, r_.b], writes=[t_.b])
                        oap, ob = out_fn(g, k)
                        P.op("act", lambda e, k=k, t_=t_, oap=oap: e.activation(out=oap, in_=t_.t[:, 0:W], func=AF.Identity, bias=shf(mi, k), scale=1.0),
                             reads=[t_.b], writes=[ob])
                    if store_fn is not None:
                        store_fn(g)
                stage1(0)
                for i in range(len(glist)):
                    if i + 1 < len(glist):
                        stage1(i + 1)
                    stage2(i)

        def load_w(src2d, r0, c0, nk, ncol, st_ring, dst, dk0=0, dc0=0):
            s_ = st_ring.next()
            h = max(1, nk // 2)
            parts = [("sp", s_.t[:, 0:h, 0:ncol], src2d[r0:r0 + h * 128, c0:c0 + ncol].rearrange("(k p) c -> p k c", p=128))]
            if nk > h:
                parts.append(("act", s_.t[:, h:nk, 0:ncol], src2d[r0 + h * 128:r0 + nk * 128, c0:c0 + ncol].rearrange("(k p) c -> p k c", p=128)))
            P.dma_multi(parts, writes=[s_.b])

            def cast():
                P.op("dve", lambda e: e.tensor_copy(out=dst.t[:, dk0:dk0 + h, dc0:dc0 + ncol], in_=s_.t[:, 0:h, 0:ncol]), reads=[s_.b], writes=[dst.b])
                if nk > h:
                    P.op("act", lambda e: e.activation(out=dst.t[:, dk0 + h:dk0 + nk, dc0:dc0 + ncol], in_=s_.t[:, h:nk, 0:ncol], func=AF.Identity), reads=[s_.b], writes=[dst.b])
            return cast

        def load_cast_w(src2d, r0, c0, nk, ncol, st_ring, dst, dk0=0, dc0=0):
            load_w(src2d, r0, c0, nk, ncol, st_ring, dst, dk0, dc0)()

        def gsf1(mi, k):
            return gs1_t.t[:, mi, k:k + 1]

        def gsf2(mi, k):
            return gs2_t.t[:, mi, k:k + 1]

        def shf1(mi, k):
            return mod_t.t[:, mi, k:k + 1]

        def shf2(mi, k):
            return mod_t.t[:, mi, 24 + k:25 + k]

        for l in range(NL):
            W_ = L[l]
            if "stopS" in dbg:
                break
            P.begin()
            with ExitStack() as es:
                def sb(name, shape, dt):
                    return T(es.enter_context(sbt(name, list(shape), dt)))
                wm = Ring([sb("m_w%d" % i, [128, 8, 512], F32) for i in range(2)])
                psm = T(es.enter_context(pst("m_ps", [128, 48, 2], F32)))
                wp32 = sb("m_wp", [128, 2, 128], F32)
                P.dma("sp", vec_t.t[:], W_["vec"], writes=[vec_t.b])
                P.dma("act", wp32.t[:], W_["wpool"].rearrange("c p m -> p c m"), writes=[wp32.b])
                P.op("dve", lambda e: e.tensor_copy(out=wpool_bf.t[:], in_=wp32.t[:]), reads=[wp32.b], writes=[wpool_bf.b])
                for blk in range(12):
                    w_ = wm.next()
                    P.dma_multi([("sp", w_.t[:, 0:4, :], W_["wmod"][0:512, blk * 512:(blk + 1) * 512].rearrange("(k p) c -> p k c", p=128)),
                                 ("act", w_.t[:, 4:8, :], W_["wmod"][512:1024, blk * 512:(blk + 1) * 512].rearrange("(k p) c -> p k c", p=128))], writes=[w_.b])
                    for jj in range(4):
                        j = blk * 4 + jj
                        for k in range(8):
                            if "M1" in dbg:
                                continue
                            P.op("pe", lambda e, w_=w_, jj=jj, j=j, k=k: e.matmul(psm.t[:, j, :], lhsT=w_.t[:, k, jj * 128:(jj + 1) * 128], rhs=sc_t.t[:, k, :], start=(k == 0), stop=(k == 7)),
                                 reads=[w_.b, sc_t.b], writes=[psm.b])
                for b in range(2):
                    if "M1" in dbg or "M2" in dbg:
                        continue
                    P.op("dve", lambda e, b=b: e.tensor_tensor(out=mod_t.t[:, b, :], in0=psm.t[:, :, b], in1=vec_t.t[:, 16:64], op=ALU.add),
                         reads=[psm.b, vec_t.b], writes=[mod_t.b])
                for b in range(2):
                    if "M1" in dbg or "M2" in dbg or "M3" in dbg:
                        continue
                    P.op("dve", lambda e, b=b: e.scalar_tensor_tensor(out=gs1_t.t[:, b, :], in0=mod_t.t[:, b, 8:16], scalar=1.0, in1=vec_t.t[:, 0:8], op0=ALU.add, op1=ALU.mult),
                         reads=[mod_t.b, vec_t.b], writes=[gs1_t.b])
                    P.op("dve", lambda e, b=b: e.scalar_tensor_tensor(out=gs2_t.t[:, b, :], in0=mod_t.t[:, b, 32:40], scalar=1.0, in1=vec_t.t[:, 8:16], op0=ALU.add, op1=ALU.mult),
                         reads=[mod_t.b, vec_t.b], writes=[gs2_t.b])
            P.end()

            if "stopM" in dbg:
                break
            for hf in range(2):
                gl = [(g, GROUPS[g]) for g in (range(0, 8) if hf == 0 else range(8, 17))]
                tbase = gl[0][1][0]
                with ExitStack() as hs:
                    hT = T(hs.enter_context(sbt("a_hT", [128, 8, 4352], BF16)))

                    def hout(g, k, hT=hT, tbase=tbase):
                        t0, W, _ = GROUPS[g]
                        return hT.t[:, k, t0 - tbase:t0 - tbase + W], hT.b
                    P.begin()
                    norm_groups(gl, gsf1, shf1, hout)
                    P.end()
                    if "A1only" in dbg:
                        break
                    P.begin()
                    ntok_h = sum(GROUPS[g][1] for g, _ in gl)
                    P.dma("pool", hT_d[0:4, :, tbase:tbase + ntok_h].rearrange("k p t -> p k t"), hT.t[:, 0:4, 0:ntok_h], reads=[hT.b], wadd=[B_hT])
                    P.dma("pool", hT_d[4:8, :, tbase:tbase + ntok_h].rearrange("k p t -> p k t"), hT.t[:, 4:8, 0:ntok_h], reads=[hT.b], wadd=[B_hT])
                    with ExitStack() as es:
                        def sb(name, shape, dt):
                            return T(es.enter_context(sbt(name, list(shape), dt)))

                        def ps(name):
                            return T(es.enter_context(pst(name, [128, 512], F32)))
                        wst = Ring([sb("a_wst%d" % i, [128, 8, 512], F32) for i in range(2)])
                        wbf = Ring([sb("a_wbf%d" % i, [128, 8, 512], BF16) for i in range(2)])
                        cosr = Ring([sb("a_cos%d" % i, [128, 512], F32) for i in range(3)])
                        sinr = Ring([sb("a_sin%d" % i, [128, 512], F32) for i in range(3)])
                        qsr = Ring([sb("a_qs%d" % i, [128, 512], BF16) for i in range(3)])
                        qfr = Ring([sb("a_qf%d" % i, [128, 512], F32) for i in range(3)])
                        t1r = Ring([sb("a_t1%d" % i, [128, 512], F32) for i in range(3)])
                        t2r = Ring([sb("a_t2%d" % i, [128, 512], F32) for i in range(3)])
                        obr = Ring([sb("a_ob%d" % i, [128, 512], BF16) for i in range(3)])
                        ofr = Ring([sb("a_of%d" % i, [128, 512], F32) for i in range(3)])
                        ubr = Ring([sb("a_ub%d" % i, [128, 2, 512], BF16) for i in range(2)])
                        afr = Ring([sb("a_af%d" % i, [128, 512], F32) for i in range(2)])
                        sgr = Ring([sb("a_sg%d" % i, [128, 512], F32) for i in range(2)])
                        psA = Ring([ps("a_psA%d" % i) for i in range(3)])
                        psP = Ring([ps("a_psP%d" % i) for i in range(2)])
                        psZ = Ring([ps("a_psZ%d" % i) for i in range(2)])
                        rope_pend = []
                        wb_next = None
                        for blk in range(5):
                            if any(x.startswith("Ablk") for x in dbg) and ("Ablk%d" % blk) not in dbg:
                                continue
                            if blk == 0 or any(x.startswith("Ablk") for x in dbg):
                                wb = wbf.next()
                                load_cast_w(W_["win"], 0, blk * 512, 8, 512, wst, wb)
                            else:
                                wb = wb_next
                            pend_cast = None
                            for gidx, (g, (t0, W, mi)) in enumerate(gl):
                                if blk < 4 and not any(x.startswith("Ablk") for x in dbg):
                                    if gidx == 0:
                                        wb_next = wbf.next()
                                        pend_cast = load_w(W_["win"], 0, (blk + 1) * 512, 8, 512, wst, wb_next)
                                    elif gidx == 2:
                                        pend_cast()
                                lo = t0 - tbase
                                if blk == 2:
                                    for tt in range(W // 128):
                                        pz = psZ.next()
                                        for k in range(8):
                                            P.op("pe", lambda e, pz=pz, k=k, lo=lo, tt=tt, wb=wb: e.matmul(pz.t[:, :], lhsT=hT.t[:, k, lo + tt * 128:lo + (tt + 1) * 128], rhs=wb.t[:, k, :], start=(k == 0), stop=(k == 7)),
                                                 reads=[hT.b, wb.b], writes=[pz.b])
                                        ob = obr.next()
                                        evac_copy(ob.t[:, :], pz.t[:, :], [pz.b], [ob.b])
                                        P.dma("pool", v_d[t0 + tt * 128:t0 + (tt + 1) * 128, :], ob.t[:, :], reads=[ob.b], wadd=[B_v])
                                    continue
                                if "norope" in dbg:
                                    mi = 1
                                if blk < 2 and mi == 0:
                                    cg, sg_ = cosr.next(), sinr.next()
                                    P.dma("sp", cg.t[:, 0:W], K["rope_cos"][:, t0:t0 + W], writes=[cg.b])
                                    P.dma("sp", sg_.t[:, 0:W], K["rope_sin"][:, t0:t0 + W], writes=[sg_.b])
                                order = (0, 2, 1, 3) if blk == 4 else (0, 1, 2, 3)
                                ub = ubr.next() if blk == 3 else None
                                af = None
                                for j in order:
                                    pa = psA.next()
                                    for k in range(8):
                                        P.op("pe", lambda e, pa=pa, k=k, j=j, lo=lo, W=W, wb=wb: e.matmul(pa.t[:, 0:W], lhsT=wb.t[:, k, j * 128:(j + 1) * 128], rhs=hT.t[:, k, lo:lo + W], start=(k == 0), stop=(k == 7)),
                                             reads=[hT.b, wb.b], writes=[pa.b])
                                    while rope_pend:
                                        rope_pend.pop(0)()
                                    if blk < 2:
                                        ob = obr.next()
                                        if mi == 0:
                                            qs, t1, t2, pp = qsr.next(), t1r.next(), t2r.next(), psP.next()
                                            qf = qfr.next()
                                            P.op("act", lambda e, qf=qf, pa=pa, W=W: e.activation(out=qf.t[:, 0:W], in_=pa.t[:, 0:W], func=AF.Identity), reads=[pa.b], writes=[qf.b])
                                            P.op("act", lambda e, qs=qs, pa=pa, W=W: e.activation(out=qs.t[:, 0:W], in_=pa.t[:, 0:W], func=AF.Identity), reads=[pa.b], writes=[qs.b])

                                            def rope_tail(qs=qs, qf=qf, t1=t1, t2=t2, pp=pp, ob=ob, cg=cg, sg_=sg_, W=W, blk=blk, j=j, t0=t0):
                                                P.op("pe", lambda e: e.matmul(pp.t[:, 0:W], lhsT=perm_bf.t[:], rhs=qs.t[:, 0:W], start=True, stop=True),
                                                     reads=[perm_bf.b, qs.b], writes=[pp.b])
                                                P.op("dve", lambda e: e.tensor_tensor(out=t1.t[:, 0:W], in0=qf.t[:, 0:W], in1=cg.t[:, 0:W], op=ALU.mult),
                                                     reads=[qf.b, cg.b], writes=[t1.b])
                                                P.op("dve", lambda e: e.tensor_tensor(out=t2.t[:, 0:W], in0=pp.t[:, 0:W], in1=sg_.t[:, 0:W], op=ALU.mult),
                                                     reads=[pp.b, sg_.b], writes=[t2.b])
                                                P.op("pool", lambda e: e.tensor_tensor(out=ob.t[:, 0:W], in0=t1.t[:, 0:W], in1=t2.t[:, 0:W], op=ALU.add),
                                                     reads=[t1.b, t2.b], writes=[ob.b])
                                                P.dma("pool", qk_d[blk, j, :, t0:t0 + W], ob.t[:, 0:W], reads=[ob.b], wadd=[B_qk])
                                            rope_pend.append(rope_tail)
                                            continue
                                        else:
                                            evac_copy(ob.t[:, 0:W], pa.t[:, 0:W], [pa.b], [ob.b])
                                        P.dma("pool", qk_d[blk, j, :, t0:t0 + W], ob.t[:, 0:W], reads=[ob.b], wadd=[B_qk])
                                    elif blk == 3:
                                        if j < 2:
                                            of = ofr.next()
                                            evac_copy(of.t[:, 0:W], pa.t[:, 0:W], [pa.b], [of.b])
                                            P.dma("pool", poolu_d[j, :, t0:t0 + W], of.t[:, 0:W], reads=[of.b], wadd=[B_poolu])
                                        else:
                                            evac_copy(ub.t[:, j - 2, 0:W], pa.t[:, 0:W], [pa.b], [ub.b])
                                            if j == 3:
                                                for tt in range(W // 128):
                                                    pz = psZ.next()
                                                    for c in range(2):
                                                        P.op("pe", lambda e, pz=pz, c=c, tt=tt, ub=ub: e.matmul(pz.t[:, :], lhsT=ub.t[:, c, tt * 128:(tt + 1) * 128], rhs=cs_bf.t[:, c, :], start=(c == 0), stop=(c == 1)),
                                                             reads=[ub.b, cs_bf.b], writes=[pz.b])
                                                    ob = obr.next()
                                                    evac_copy(ob.t[:, :], pz.t[:, :], [pz.b], [ob.b])
                                                    P.dma("pool", z_d[t0 + tt * 128:t0 + (tt + 1) * 128, :], ob.t[:, :], reads=[ob.b], wadd=[B_z])
                                    else:
                                        if j < 2:
                                            af = afr.next()
                                            P.op("dve", lambda e, af=af, pa=pa, W=W: e.tensor_copy(out=af.t[:, 0:W], in_=pa.t[:, 0:W]), reads=[pa.b], writes=[af.b])
                                        else:
                                            sg2, of = sgr.next(), obr.next()
                                            P.op("act", lambda e, sg2=sg2, pa=pa, W=W: e.activation(out=sg2.t[:, 0:W], in_=pa.t[:, 0:W], func=AF.Sigmoid), reads=[pa.b], writes=[sg2.b])
                                            P.op("pool", lambda e, of=of, af=af, sg2=sg2, W=W: e.tensor_tensor(out=of.t[:, 0:W], in0=af.t[:, 0:W], in1=sg2.t[:, 0:W], op=ALU.mult),
                                                 reads=[af.b, sg2.b], writes=[of.b])
                                            P.dma("pool", convz_d[j - 2, :, t0:t0 + W], of.t[:, 0:W], reads=[of.b], wadd=[B_convz])
                            while rope_pend:
                                rope_pend.pop(0)()
                    P.end()
            if "stopA" in dbg:
                break

            P.begin()
            with ExitStack() as es:
                def sb(name, shape, dt):
                    return T(es.enter_context(sbt(name, list(shape), dt)))

                def ps(name):
                    return T(es.enter_context(pst(name, [128, 512], F32)))
                NKR = 8
                kt = [sb("b_kt%d" % i, [128, 4, 128], BF16) for i in range(NKR)]
                vt = [sb("b_vt%d" % i, [128, 8, 65], BF16) for i in range(NKR)]
                kc = sb("b_kc", [128, 4, 256], BF16)
                vc = [sb("b_vc%d" % i, [128, 8, 65], BF16) for i in range(2)]
                for v_ in vt + vc:
                    P.op("pool", lambda e, v_=v_: e.memset(v_.t[:], 1.0), writes=[v_.b])
                qtr = Ring([sb("b_qt%d" % i, [128, 4, 128], BF16) for i in range(3)])
                bst = sb("b_bst", [128, 8, 640], F32)
                btab2 = W_["btab"].rearrange("p h c q -> p h (c q)")
                bt = {}
                off = 0
                for name, _, keys in ATT_VARIANTS:
                    n = len(keys)
                    bt[name] = sb("b_bt_" + name, [128, 8, n * 128], BF16)
                    P.dma("sp", bst.t[:, :, 0:n * 128], btab2[:, :, off * 128:(off + n) * 128], writes=[bst.b])
                    P.op("act", lambda e, n=n, name=name: e.activation(out=bt[name].t[:], in_=bst.t[:, :, 0:n * 128], func=AF.Exp), reads=[bst.b], writes=[bt[name].b])
                    off += n
                Er = Ring([sb("b_E%d" % i, [128, 896], BF16) for i in range(3)])
                Pr = Ring([sb("b_P%d" % i, [128, 896], BF16) for i in range(3)])
                recr = Ring([sb("b_rec%d" % i, [128, 8], F32) for i in range(3)])
                atr = Ring([sb("b_at%d" % i, [128, 4, 128], BF16) for i in range(3)])
                attr_ = Ring([sb("b_att%d" % i, [128, 512], BF16) for i in range(3)])
                psS = Ring([(ps("b_psSa%d" % i), ps("b_psSb%d" % i)) for i in range(2)])
                psO = [T(es.enter_context(pst("b_psO%d" % i, [128, 4, 65], F32))) for i in range(2)]
                psT = T(es.enter_context(pst("b_psT", [128, 512], BF16)))
                P.dma("sp", kc.t[:], qk_d[1, :, :, NTOK:NT].rearrange("j p t -> p j t"), reads=[B_qk], writes=[kc.b])
                for i in range(2):
                    P.dma("act", vc[i].t[:, :, 0:64], v_d[NTOK + i * 128:NTOK + (i + 1) * 128, :].rearrange("p (h d) -> p h d", h=8), reads=[B_v], writes=[vc[i].b])
                state = {"loaded": -1}
                units = []
                tiles = {}

                def tile_setup(qi):
                    if qi < 64:
                        if qi == 0:
                            var, keys = "t0", [0, 1, 2, 3]
                        elif qi == 1:
                            var, keys = "t1", [0, 1, 2, 3]
                        elif qi == 62:
                            var, keys = "t62", [60, 61, 62, 63]
                        elif qi == 63:
                            var, keys = "t63", [60, 61, 62, 63]
                        else:
                            var, keys = "int", [qi - 2, qi - 1, qi, qi + 1, qi + 2]
                        while state["loaded"] < keys[-1]:
                            state["loaded"] += 1
                            u = state["loaded"]
                            P.dma("sp", kt[u % NKR].t[:], qk_d[1, :, :, u * 128:(u + 1) * 128].rearrange("j p t -> p j t"), reads=[B_qk], writes=[kt[u % NKR].b])
                            P.dma("sp", vt[u % NKR].t[:, :, 0:64], v_d[u * 128:(u + 1) * 128, :].rearrange("p (h d) -> p h d", h=8), reads=[B_v], writes=[vt[u % NKR].b])
                        chunks = [(kt[u % NKR], None, vt[u % NKR], None) for u in keys] + [(kc, 0, vc[0], 0), (kc, 1, vc[1], 1)]
                        nwin = len(keys)
                    else:
                        var, nwin = None, 0
                        chunks = [(kc, 0, vc[0], 0), (kc, 1, vc[1], 1)]
                    q0 = qi * 128
                    qt = qtr.next()
                    P.dma("sp", qt.t[:], qk_d[0, :, :, q0:q0 + 128].rearrange("j p t -> p j t"), reads=[B_qk], writes=[qt.b])
                    tiles[qi] = dict(var=var, nwin=nwin, chunks=chunks, q0=q0, qt=qt, at=atr.next(), rec=recr.next(), att=attr_.next())

                def s_stage(qi, h):
                    if h == 0:
                        tile_setup(qi)
                    tl = tiles[qi]
                    chunks, qt, nwin, var = tl["chunks"], tl["qt"], tl["nwin"], tl["var"]
                    n = len(chunks)
                    hc, p0 = h // 2, (h % 2) * 64
                    sa, sb_ = psS.next()
                    for ci, (kT_, kci, _, _) in enumerate(chunks):
                        dst = sa if ci < 4 else sb_
                        lhs = kT_.t[p0:p0 + 64, hc, :] if kci is None else kT_.t[p0:p0 + 64, hc, kci * 128:(kci + 1) * 128]
                        P.op("pe", lambda e, dst=dst, ci=ci, lhs=lhs: e.matmul(dst.t[:, (ci % 4) * 128:(ci % 4 + 1) * 128], lhsT=lhs, rhs=qt.t[p0:p0 + 64, hc, :], start=True, stop=True),
                             reads=[kT_.b, qt.b], writes=[dst.b])
                    E = Er.next()
                    na = min(n, 4)
                    P.op("act", lambda e: e.activation(out=E.t[:, 0:na * 128], in_=sa.t[:, 0:na * 128], func=AF.Exp, scale=0.125), reads=[sa.b], writes=[E.b])
                    if n > 4:
                        P.op("act", lambda e: e.activation(out=E.t[:, 512:n * 128], in_=sb_.t[:, 0:(n - 4) * 128], func=AF.Exp, scale=0.125), reads=[sb_.b], writes=[E.b])
                    Pm = None
                    if nwin > 0:
                        Pm = Pr.next()
                        btv = bt[var]
                        P.op("dve", lambda e: e.tensor_tensor(out=Pm.t[:, 0:nwin * 128], in0=E.t[:, 0:nwin * 128], in1=btv.t[:, h, 0:nwin * 128], op=ALU.mult),
                             reads=[E.b, btv.b], writes=[Pm.b])
                    tl[("EP", h)] = (E, Pm)

                def pv_stage(qi, h):
                    tl = tiles[qi]
                    chunks, nwin, rec, att, at, q0 = tl["chunks"], tl["nwin"], tl["rec"], tl["att"], tl["at"], tl["q0"]
                    n = len(chunks)
                    E, Pm = tl.pop(("EP", h))
                    po = psO[h // 4]
                    h4 = h % 4
                    for ci, (_, _, v_, vci) in enumerate(chunks):
                        src = Pm if ci < nwin else E
                        P.op("pe", lambda e, v_=v_, src=src, ci=ci: e.matmul(po.t[:, h4, :], lhsT=src.t[:, ci * 128:(ci + 1) * 128], rhs=v_.t[:, h, :], start=(ci == 0), stop=(ci == n - 1)),
                             reads=[v_.b, src.b], writes=[po.b])
                    if h4 == 3:
                        hb = h - 3
                        P.op("dve", lambda e: e.reciprocal(out=rec.t[:, hb:hb + 4], in_=po.t[:, :, 64]), reads=[po.b], writes=[rec.b])
                        for hh in range(4):
                            P.op("dve", lambda e, hh=hh: e.tensor_scalar(out=att.t[:, (hb + hh) * 64:(hb + hh + 1) * 64], in0=po.t[:, hh, 0:64], scalar1=rec.t[:, hb + hh:hb + hh + 1], scalar2=0.0, op0=ALU.mult, op1=ALU.add),
                                 reads=[po.b, rec.b], writes=[att.b])
                    if h == 7:
                        for j in range(4):
                            P.op("pe", lambda e, j=j: e.transpose(out=psT.t[:, j * 128:(j + 1) * 128], in_=att.t[:, j * 128:(j + 1) * 128], identity=ident_bf.t[:]),
                                 reads=[att.b, ident_bf.b], writes=[psT.b])
                        P.op("act", lambda e: e.activation(out=at.t[:], in_=psT.t[:], func=AF.Identity), reads=[psT.b], writes=[at.b])
                        P.dma("pool", br_d[0:4, :, q0:q0 + 128].rearrange("j p t -> p j t"), at.t[:], reads=[at.b], wadd=[B_br])
                        del tiles[qi]
                units = [(qi, h) for qi in range(66) for h in range(8)]
                s_stage(*units[0])
                for i, u_ in enumerate(units):
                    if i + 1 < len(units):
                        s_stage(*units[i + 1])
                    pv_stage(*u_)
            P.end()

            P.begin()
            mix_es = ExitStack()
            def gen_pool(es=mix_es):
                def sb(name, shape, dt):
                    return T(es.enter_context(sbt(name, list(shape), dt)))

                def ps(name):
                    return T(es.enter_context(pst(name, [128, 512], F32)))
                Ur = Ring([sb("p_U%d" % i, [128, 528], F32) for i in range(2)])
                P2r = Ring([sb("p_P2%d" % i, [128, 528], F32) for i in range(2)])
                P4r = Ring([sb("p_P4%d" % i, [128, 528], F32) for i in range(2)])
                invr = Ring([sb("p_inv%d" % i, [128, 512], F32) for i in range(2)])
                tmr = Ring([sb("p_tm%d" % i, [128, 512], F32) for i in range(2)])
                dbr = Ring([sb("p_db%d" % i, [128, 512], BF16) for i in range(3)])
                obr = Ring([sb("p_ob%d" % i, [128, 512], BF16) for i in range(2)])
                psr = Ring([ps("p_ps%d" % i) for i in range(1)])
                pool_pend = []
                blocks = [(g * 512, 512, 0 if g == 0 else (2 if g == 15 else 1), g == 0, g == 15) for g in range(16)] + [(NTOK, 256, 3, True, True)]
                for c in range(2):
                    for (t0, W, var, first, last) in blocks:
                        U, A, Bq = Ur.next(), P2r.next(), P4r.next()
                        lo = 0 if not first else 8
                        hi = W + 16 if not last else W + 8
                        if first:
                            P.op("pool", lambda e, U=U: e.memset(U.t[:, 0:8], 0.0), writes=[U.b])
                        if last:
                            P.op("pool", lambda e, U=U, W=W: e.memset(U.t[:, W + 8:W + 16], 0.0), writes=[U.b])
                        P.dma("sp", U.t[:, lo:hi], poolu_d[c, :, t0 - 8 + lo:t0 - 8 + hi], reads=[B_poolu], writes=[U.b])
                        iv = invr.next()
                        P.dma("sp", iv.t[:, 0:W], K["poolinv"][c, :, var, 0:W], writes=[iv.b])
                        n = W + 16
                        P.op("pool", lambda e, U=U, A=A, n=n: e.tensor_tensor(out=A.t[:, 1:n], in0=U.t[:, 0:n - 1], in1=U.t[:, 1:n], op=ALU.add), reads=[U.b], writes=[A.b])
                        P.op("pool", lambda e, A=A, Bq=Bq, n=n: e.tensor_tensor(out=Bq.t[:, 2:n - 1], in0=A.t[:, 1:n - 2], in1=A.t[:, 3:n], op=ALU.add), reads=[A.b], writes=[Bq.b])
                        if c == 1:
                            A2, B2 = P2r.next(), P4r.next()
                            P.op("pool", lambda e, Bq=Bq, A2=A2, n=n: e.tensor_tensor(out=A2.t[:, 4:n - 3], in0=Bq.t[:, 2:n - 5], in1=Bq.t[:, 6:n - 1], op=ALU.add), reads=[Bq.b], writes=[A2.b])
                            P.op("pool", lambda e, A2=A2, B2=B2, n=n: e.tensor_tensor(out=B2.t[:, 8:n - 7], in0=A2.t[:, 4:n - 11], in1=A2.t[:, 12:n - 3], op=ALU.add), reads=[A2.b], writes=[B2.b])
                            lo_t, hi_t = A2, B2
                        else:
                            lo_t, hi_t = A, Bq
                        tm, db = tmr.next(), dbr.next()
                        for half, src in ((0, lo_t), (1, hi_t)):
                            pp = slice(half * 64, (half + 1) * 64)
                            P.op("dve", lambda e, tm=tm, src=src, iv=iv, pp=pp, W=W: e.tensor_tensor(out=tm.t[pp, 0:W], in0=src.t[pp, 8:8 + W], in1=iv.t[pp, 0:W], op=ALU.mult),
                                 reads=[src.b, iv.b], writes=[tm.b])
                        P.op("dve", lambda e, tm=tm, db=db, U=U, W=W: e.tensor_tensor(out=db.t[:, 0:W], in0=tm.t[:, 0:W], in1=U.t[:, 8:8 + W], op=ALU.subtract),
                             reads=[tm.b, U.b], writes=[db.b])
                        def pool_tail(db=db, c=c, W=W, t0=t0):
                            pp_ = psr.next()
                            P.op("pe", lambda e: e.matmul(pp_.t[:, 0:W], lhsT=wpool_bf.t[:, c, :], rhs=db.t[:, 0:W], start=True, stop=True),
                                 reads=[wpool_bf.b, db.b], writes=[pp_.b])
                            ob = obr.next()
                            P.op("act", lambda e: e.activation(out=ob.t[:, 0:W], in_=pp_.t[:, 0:W], func=AF.Identity, scale=vec_t.t[:, 64 + c:65 + c]),
                                 reads=[pp_.b, vec_t.b], writes=[ob.b])
                            P.dma("pool", br_d[4 + c, :, t0:t0 + W], ob.t[:, 0:W], reads=[ob.b], wadd=[B_br])
                        while pool_pend:
                            pool_pend.pop(0)()
                        pool_pend.append(pool_tail)
                        yield
                while pool_pend:
                    pool_pend.pop(0)()


            def gen_conv(es=mix_es):
                def sb(name, shape, dt):
                    return T(es.enter_context(sbt(name, list(shape), dt)))

                def ps(name):
                    return T(es.enter_context(pst(name, [128, 512], F32)))
                dg = sb("c_dg", [128, 62, 128], BF16)
                for cj in range(62):
                    P.op("dve", lambda e, cj=cj: e.tensor_scalar(out=dg.t[:, cj, :], in0=ident_bf.t[:], scalar1=vec_t.t[:, 72 + cj:73 + cj], scalar2=0.0, op0=ALU.mult, op1=ALU.add),
                         reads=[ident_bf.b, vec_t.b], writes=[dg.b])
                Zr_ = [Ring([sb("c_Z%d_%d" % (c, i), [128, 542], BF16) for i in range(2)]) for c in range(2)]
                accr = [Ring([sb("c_acc%d_%d" % (c, i), [128, 512], F32) for i in range(3)]) for c in range(2)]
                sqr = Ring([sb("c_sq%d" % i, [128, 512], F32) for i in range(2)])
                mr = Ring([sb("c_m%d" % i, [128, 512], F32) for i in range(2)])
                m2r = Ring([sb("c_m2%d" % i, [128, 512], F32) for i in range(2)])
                rsr = Ring([sb("c_rs%d" % i, [128, 512], F32) for i in range(2)])
                xcr = Ring([sb("c_xc%d" % i, [128, 512], F32) for i in range(2)])
                obr = Ring([sb("c_ob%d" % i, [128, 512], BF16) for i in range(2)])
                psC = Ring([ps("c_psC%d" % i) for i in range(2)])
                ps1 = Ring([ps("c_ps1%d" % i) for i in range(1)])
                ps2 = Ring([ps("c_ps2%d" % i) for i in range(1)])
                blocks = [(g * 512, 512, g == 0, g == 15) for g in range(16)] + [(NTOK, 256, True, True)]
                conv_pend = []
                for (t0, W, first, last) in blocks:
                    accs = []
                    for c in range(2):
                        Z = Zr_[c].next()
                        lo = 15 if first else 0
                        hi = W + 15 if last else W + 30
                        if first:
                            P.op("pool", lambda e, Z=Z: e.memset(Z.t[:, 0:15], 0.0), writes=[Z.b])
                        if last:
                            P.op("pool", lambda e, Z=Z, W=W: e.memset(Z.t[:, W + 15:W + 30], 0.0), writes=[Z.b])
                        P.dma("sp", Z.t[:, lo:hi], convz_d[c, :, t0 - 15 + lo:t0 - 15 + hi], reads=[B_convz], writes=[Z.b])
                        pc = psC.next()
                        for j in range(31):
                            P.op("pe", lambda e, pc=pc, Z=Z, c=c, j=j, W=W: e.matmul(pc.t[:, 0:W], lhsT=dg.t[:, c * 31 + j, :], rhs=Z.t[:, j:j + W], start=(j == 0), stop=(j == 30)),
                                 reads=[dg.b, Z.b], writes=[pc.b])
                        acc = accr[c].next()
                        P.op("act", lambda e, acc=acc, pc=pc, c=c, W=W: e.activation(out=acc.t[:, 0:W], in_=pc.t[:, 0:W], func=AF.Identity, bias=vec_t.t[:, 66 + c:67 + c], scale=1.0),
                             reads=[pc.b, vec_t.b], writes=[acc.b])
                        accs.append(acc)
                    def conv_tail(accs=accs, W=W, t0=t0):
                        p1, p2 = ps1.next(), ps2.next()
                        for c in range(2):
                            sq = sqr.next()
                            P.op("act", lambda e, sq=sq, a=accs[c], W=W: e.activation(out=sq.t[:, 0:W], in_=a.t[:, 0:W], func=AF.Square), reads=[accs[c].b], writes=[sq.b])
                            P.op("pe", lambda e, p1=p1, a=accs[c], c=c, W=W: e.matmul(p1.t[:, 0:W], lhsT=ones_f.t[:], rhs=a.t[:, 0:W], start=(c == 0), stop=(c == 1)), reads=[ones_f.b, accs[c].b], writes=[p1.b])
                            P.op("pe", lambda e, p2=p2, sq=sq, c=c, W=W: e.matmul(p2.t[:, 0:W], lhsT=ones_f.t[:], rhs=sq.t[:, 0:W], start=(c == 0), stop=(c == 1)), reads=[ones_f.b, sq.b], writes=[p2.b])
                        m, m2, rs = mr.next(), m2r.next(), rsr.next()
                        P.op("dve", lambda e, m=m, p1=p1, W=W: e.tensor_scalar(out=m.t[:, 0:W], in0=p1.t[:, 0:W], scalar1=1.0 / 256.0, scalar2=0.0, op0=ALU.mult, op1=ALU.add), reads=[p1.b], writes=[m.b])
                        P.op("dve", lambda e, m=m, m2=m2, W=W: e.tensor_tensor(out=m2.t[:, 0:W], in0=m.t[:, 0:W], in1=m.t[:, 0:W], op=ALU.mult), reads=[m.b], writes=[m2.b])
                        P.op("dve", lambda e, rs=rs, p2=p2, m2=m2, W=W: e.scalar_tensor_tensor(out=rs.t[:, 0:W], in0=p2.t[:, 0:W], scalar=1.0 / 256.0, in1=m2.t[:, 0:W], op0=ALU.mult, op1=ALU.subtract),
                             reads=[p2.b, m2.b], writes=[rs.b])
                        P.op("act", lambda e, rs=rs, W=W: e.activation(out=rs.t[:, 0:W], in_=rs.t[:, 0:W], func=AF.Ln, bias=eps_t.t[:, 0:1], scale=1.0), reads=[rs.b, eps_t.b], writes=[rs.b])
                        P.op("act", lambda e, rs=rs, W=W: e.activation(out=rs.t[:, 0:W], in_=rs.t[:, 0:W], func=AF.Exp, scale=-0.5), reads=[rs.b], writes=[rs.b])
                        for c in range(2):
                            xc, ob = xcr.next(), obr.next()
                            eng = "dve" if c == 0 else "pool"
                            P.op(eng, lambda e, xc=xc, a=accs[c], m=m, W=W: e.tensor_tensor(out=xc.t[:, 0:W], in0=a.t[:, 0:W], in1=m.t[:, 0:W], op=ALU.subtract), reads=[accs[c].b, m.b], writes=[xc.b])
                            P.op(eng, lambda e, xc=xc, rs=rs, W=W: e.tensor_tensor(out=xc.t[:, 0:W], in0=xc.t[:, 0:W], in1=rs.t[:, 0:W], op=ALU.mult), reads=[xc.b, rs.b], writes=[xc.b])
                            P.op("act", lambda e, xc=xc, ob=ob, c=c, W=W: e.activation(out=ob.t[:, 0:W], in_=xc.t[:, 0:W], func=AF.Silu, scale=vec_t.t[:, 68 + c:69 + c], bias=vec_t.t[:, 70 + c:71 + c]),
                                 reads=[xc.b, vec_t.b], writes=[ob.b])
                            P.dma("pool", br_d[8 + c, :, t0:t0 + W], ob.t[:, 0:W], reads=[ob.b], wadd=[B_br])
                    while conv_pend:
                        conv_pend.pop(0)()
                    conv_pend.append(conv_tail)
                    yield
                while conv_pend:
                    conv_pend.pop(0)()


            def gen_four(es=mix_es):
                def sb(name, shape, dt):
                    return T(es.enter_context(sbt(name, list(shape), dt)))

                def ps(name):
                    return T(es.enter_context(pst(name, [128, 512], F32)))
                zin = Ring([sb("f_zin%d" % i, [128, 8, 256], BF16) for i in range(2)])
                zso = Ring([sb("f_zso%d" % i, [128, 2048], BF16) for i in range(2)])
                zs2 = zs_d.rearrange("p n d -> p (n d)")
                psF = Ring([ps("f_ps%d" % i) for i in range(2)])
                zv = z_d[0:NTOK, :].rearrange("(a b) f -> a b f", b=128)
                for it in range(16):
                    zi_ = zin.next()
                    P.dma_multi([("sp", zi_.t[0:64, :, :], zv[:, it * 8:(it + 1) * 8, 0:256]),
                                 ("sp", zi_.t[64:128, :, :], zv[:, it * 8:(it + 1) * 8, 256:512])], reads=[B_z], writes=[zi_.b])
                    zo = zso.next()
                    for s in range(4):
                        pf = psF.next()
                        P.op("pe", lambda e, pf=pf, zi_=zi_, s=s: e.matmul(pf.t[:, :], lhsT=w64_bf.t[:], rhs=zi_.t[:, 2 * s:2 * s + 2, :], start=True, stop=True),
                             reads=[w64_bf.b, zi_.b], writes=[pf.b])
                        evac_copy(zo.t[:, s * 512:(s + 1) * 512], pf.t[:, :], [pf.b], [zo.b])
                    P.dma("pool", zs2[:, it * 2048:(it + 1) * 2048], zo.t[:], reads=[zo.b], wadd=[B_zs])
                    yield
                FT = sb("f_FT", [128, 2, 128, 64], BF16)
                zr_r = Ring([sb("f_zr%d" % i, [128, 8, 256], BF16) for i in range(2)])
                zi_r = Ring([sb("f_zi%d" % i, [128, 8, 256], BF16) for i in range(2)])
                cf_r = Ring([sb("f_cf%d" % i, [128, 8, 128], BF16) for i in range(2)])
                sf_r = Ring([sb("f_sf%d" % i, [128, 8, 128], BF16) for i in range(2)])
                FTv = FT.t
                for kb in range(8):
                    zr, zi2, cfb, sfb = zr_r.next(), zi_r.next(), cf_r.next(), sf_r.next()
                    P.dma("sp", zr.t[:], zs_d[kb * 8:(kb + 1) * 8, :, :].rearrange("k n d -> n k d"), reads=[B_zs], writes=[zr.b])
                    P.dma("sp", zi2.t[:], zs_d[64 + kb * 8:64 + (kb + 1) * 8, :, :].rearrange("k n d -> n k d"), reads=[B_zs], writes=[zi2.b])
                    P.dma("sp", cfb.t[:], cfb_d[:, kb * 1024:(kb + 1) * 1024].rearrange("p (k m) -> p k m", k=8), reads=[B_cfb], writes=[cfb.b])
                    P.dma("sp", sfb.t[:], sfb_d[:, kb * 1024:(kb + 1) * 1024].rearrange("p (k m) -> p k m", k=8), reads=[B_cfb], writes=[sfb.b])
                    for dc in range(2):
                        for q4 in range(2):
                            pf = psF.next()
                            for a in range(4):
                                k1l = q4 * 4 + a
                                P.op("pe", lambda e, pf=pf, zr=zr, cfb=cfb, k1l=k1l, a=a, dc=dc: e.matmul(pf.t[:, a * 128:(a + 1) * 128], lhsT=zr.t[:, k1l, dc * 128:(dc + 1) * 128], rhs=cfb.t[:, k1l, :], start=True, stop=False),
                                     reads=[zr.b, cfb.b], writes=[pf.b])
                                P.op("pe", lambda e, pf=pf, zi2=zi2, sfb=sfb, k1l=k1l, a=a, dc=dc: e.matmul(pf.t[:, a * 128:(a + 1) * 128], lhsT=zi2.t[:, k1l, dc * 128:(dc + 1) * 128], rhs=sfb.t[:, k1l, :], start=False, stop=True),
                                     reads=[zi2.b, sfb.b], writes=[pf.b])
                            for a in range(4):
                                k1 = kb * 8 + q4 * 4 + a
                                eng = "act" if (dc + q4) % 2 == 0 else "dve"
                                if eng == "act":
                                    P.op("act", lambda e, pf=pf, a=a, dc=dc, k1=k1: e.activation(out=FTv[:, dc, :, k1], in_=pf.t[:, a * 128:(a + 1) * 128], func=AF.Identity, scale=float(1.0 / np.sqrt(8192.0 * 64.0))),
                                         reads=[pf.b], writes=[FT.b])
                                else:
                                    P.op("dve", lambda e, pf=pf, a=a, dc=dc, k1=k1: e.tensor_scalar(out=FTv[:, dc, :, k1], in0=pf.t[:, a * 128:(a + 1) * 128], scalar1=float(1.0 / np.sqrt(8192.0 * 64.0)), scalar2=0.0, op0=ALU.mult, op1=ALU.add),
                                         reads=[pf.b], writes=[FT.b])
                    yield
                for dc in range(2):
                    P.dma(("sp", "act")[dc], br_d[6 + dc, :, 0:NTOK].rearrange("p (a b) -> p a b", b=64), FT.t[:, dc, :, :], reads=[FT.b], wadd=[B_br])
                zc = sb("f_zc", [128, 2, 512], BF16)
                obc = sb("f_obc", [128, 2, 256], BF16)
                P.dma("sp", zc.t[:], z_d[NTOK:NT, :].rearrange("(c p) f -> p c f", p=128), reads=[B_z], writes=[zc.b])
                for dc in range(2):
                    pf = psF.next()
                    i = 0
                    for tt in range(2):
                        for (o, tab) in ((0, c256_bf), (256, s256_bf)):
                            P.op("pe", lambda e, pf=pf, tt=tt, o=o, tab=tab, dc=dc, i=i: e.matmul(pf.t[:, 0:256], lhsT=zc.t[:, tt, o + dc * 128:o + (dc + 1) * 128], rhs=tab.t[:, tt, :], start=(i == 0), stop=(i == 3)),
                                 reads=[zc.b, tab.b], writes=[pf.b])
                            i += 1
                    P.op("act", lambda e, pf=pf, dc=dc: e.activation(out=obc.t[:, dc, :], in_=pf.t[:, 0:256], func=AF.Identity, scale=1.0 / 128.0), reads=[pf.b], writes=[obc.b])
                P.dma("sp", br_d[6:8, :, NTOK:NT].rearrange("c p t -> p c t"), obc.t[:], reads=[obc.b], wadd=[B_br])
            gens = [(gen_conv(), 1), (gen_pool(), 2), (gen_four(), 2)]
            while gens:
                for item in list(gens):
                    g_, reps = item
                    for _ in range(reps):
                        try:
                            next(g_)
                        except StopIteration:
                            gens.remove(item)
                            break
            P.end()
            mix_es.close()
            if "stopB" in dbg:
                break

            for qi in range(2):
                gl = [(g, GROUPS[g]) for g in (range(0, 8) if qi == 0 else range(8, 17))]
                tbase = gl[0][1][0]

                def lofs(t0, tbase=tbase):
                    return (t0 - tbase) if t0 < NTOK else 4096
                with ExitStack() as hs:
                    hts = ExitStack()
                    hT = T(hts.enter_context(sbt("c_hT", [128, 8, 4352], BF16)))
                    P.begin()
                    parts = [("sp", hT.t[:, 0:4, 0:4096], hT_d[0:4, :, tbase:tbase + 4096].rearrange("k p t -> p k t")),
                             ("act", hT.t[:, 4:8, 0:4096], hT_d[4:8, :, tbase:tbase + 4096].rearrange("k p t -> p k t"))]
                    if qi == 1:
                        parts.append(("sp", hT.t[:, :, 4096:4352], hT_d[:, :, NTOK:NT].rearrange("k p t -> p k t")))
                    P.dma_multi(parts, reads=[B_hT], writes=[hT.b])
                    with ExitStack() as es:
                        def sb(name, shape, dt):
                            return T(es.enter_context(sbt(name, list(shape), dt)))

                        def ps(name):
                            return T(es.enter_context(pst(name, [128, 512], F32)))
                        gst = Ring([sb("c2_gst%d" % i, [128, 8, 512], F32) for i in range(2)])
                        gbf = Ring([sb("c2_gbf%d" % i, [128, 8, 512], BF16) for i in range(2)])
                        bst = Ring([sb("c2_bst%d" % i, [128, 10, 128], F32) for i in range(2)])
                        bbf = Ring([sb("c2_bbf%d" % i, [128, 10, 128], BF16) for i in range(2)])
                        brg = Ring([sb("c2_brg%d" % i, [128, 10, 512], BF16) for i in range(4)])
                        sgr = Ring([sb("c2_sg%d" % i, [128, 512], F32) for i in range(2)])
                        tr = Ring([sb("c2_t%d" % i, [128, 512], F32) for i in range(2)])
                        mar = Ring([sb("c2_ma%d" % i, [128, 512], F32) for i in range(3)])
                        mor = Ring([sb("c2_mo%d" % i, [128, 512], BF16) for i in range(3)])
                        psG = Ring([ps("c2_psG%d" % i) for i in range(2)])
                        psY = Ring([ps("c2_psY%d" % i) for i in range(2)])
                        KB = [(0, 4), (4, 2), (6, 2), (8, 2)]
                        def c2_load(d):
                            gs_, gb = gst.next(), gbf.next()
                            P.dma_multi([(("sp", "act")[b % 2], gs_.t[:, :, b * 128:(b + 1) * 128], W_["win"][:, GATE_OFF + b * 1024 + d * 128:GATE_OFF + b * 1024 + (d + 1) * 128].rearrange("(k p) c -> p k c", p=128)) for b in range(4)], writes=[gs_.b])
                            bs_, bb = bst.next(), bbf.next()
                            P.dma("sp", bs_.t[:], W_["wbr"][:, d * 128:(d + 1) * 128].rearrange("(k p) c -> p k c", p=128), writes=[bs_.b])

                            def cast():
                                P.op("dve", lambda e: e.tensor_copy(out=gb.t[:, 0:4, :], in_=gs_.t[:, 0:4, :]), reads=[gs_.b], writes=[gb.b])
                                P.op("act", lambda e: e.activation(out=gb.t[:, 4:8, :], in_=gs_.t[:, 4:8, :], func=AF.Identity), reads=[gs_.b], writes=[gb.b])
                                P.op("act", lambda e: e.activation(out=bb.t[:], in_=bs_.t[:], func=AF.Identity), reads=[bs_.b], writes=[bb.b])
                            return gb, bb, cast
                        nxt = c2_load(0)
                        nxt[2]()
                        for d in range(8):
                            gb, bb = nxt[0], nxt[1]
                            nxt = None
                            for gidx, (g, (t0, W, mi)) in enumerate(gl):
                                if d < 7 and gidx == 0:
                                    nxt = c2_load(d + 1)
                                if d < 7 and gidx == 2:
                                    nxt[2]()
                                lo = lofs(t0)
                                bg = brg.next()
                                P.dma_multi([("sp", bg.t[:, 0:5, 0:W], br_d[0:5, :, t0:t0 + W].rearrange("j p t -> p j t")),
                                             ("sp", bg.t[:, 5:10, 0:W], br_d[5:10, :, t0:t0 + W].rearrange("j p t -> p j t"))], reads=[B_br], writes=[bg.b])
                                ma = None
                                for b in range(4):
                                    pg, py = psG.next(), psY.next()
                                    for k in range(8):
                                        P.op("pe", lambda e, pg=pg, gb=gb, b=b, k=k, lo=lo, W=W: e.matmul(pg.t[:, 0:W], lhsT=gb.t[:, k, b * 128:(b + 1) * 128], rhs=hT.t[:, k, lo:lo + W], start=(k == 0), stop=(k == 7)),
                                             reads=[gb.b, hT.b], writes=[pg.b])
                                    k0, nk = KB[b]
                                    for k in range(nk):
                                        P.op("pe", lambda e, py=py, bb=bb, bg=bg, k0=k0, k=k, nk=nk, W=W: e.matmul(py.t[:, 0:W], lhsT=bb.t[:, k0 + k, :], rhs=bg.t[:, k0 + k, 0:W], start=(k == 0), stop=(k == nk - 1)),
                                             reads=[bb.b, bg.b], writes=[py.b])
                                    sg_ = sgr.next()
                                    P.op("act", lambda e, sg_=sg_, pg=pg, W=W: e.activation(out=sg_.t[:, 0:W], in_=pg.t[:, 0:W], func=AF.Sigmoid), reads=[pg.b], writes=[sg_.b])
                                    if b == 0:
                                        ma = mar.next()
                                        P.op("dve", lambda e, ma=ma, py=py, sg_=sg_, W=W: e.tensor_tensor(out=ma.t[:, 0:W], in0=py.t[:, 0:W], in1=sg_.t[:, 0:W], op=ALU.mult), reads=[py.b, sg_.b], writes=[ma.b])
                                    else:
                                        t_ = tr.next()
                                        P.op("dve", lambda e, t_=t_, py=py, sg_=sg_, W=W: e.tensor_tensor(out=t_.t[:, 0:W], in0=py.t[:, 0:W], in1=sg_.t[:, 0:W], op=ALU.mult), reads=[py.b, sg_.b], writes=[t_.b])
                                        if b < 3:
                                            nma = mar.next()
                                            P.op("pool", lambda e, nma=nma, ma=ma, t_=t_, W=W: e.tensor_tensor(out=nma.t[:, 0:W], in0=ma.t[:, 0:W], in1=t_.t[:, 0:W], op=ALU.add), reads=[ma.b, t_.b], writes=[nma.b])
                                            ma = nma
                                        else:
                                            mo = mor.next()
                                            P.op("pool", lambda e, ma=ma, t_=t_, mo=mo, W=W: e.tensor_tensor(out=mo.t[:, 0:W], in0=ma.t[:, 0:W], in1=t_.t[:, 0:W], op=ALU.add), reads=[ma.b, t_.b], writes=[mo.b])
                                            P.dma("pool", m_d[d, :, t0:t0 + W], mo.t[:, 0:W], reads=[mo.b], wadd=[B_m])
                    P.end()
                    hts.close()
            P.begin()
            with ExitStack() as es:
                def sb(name, shape, dt):
                    return T(es.enter_context(sbt(name, list(shape), dt)))

                def ps(name):
                    return T(es.enter_context(pst(name, [128, 512], F32)))
                wst = Ring([sb("c3_wst%d" % i, [128, 8, 512], F32) for i in range(2)])
                wo = sb("c3_wo", [128, 8, 1024], BF16)
                mgr = Ring([sb("c3_mg%d" % i, [128, 8, 512], BF16) for i in range(2)])
                gl = [(g, GROUPS[g]) for g in range(17)]
                xgr = Ring([sb("c3_xg%d" % i, [128, 8, 512], F32) for i in range(3)])
                psO = Ring([ps("c3_ps%d" % i) for i in range(3)])
                n_sq = Ring([sb("c3_sq%d" % i, [128, 8, 512], BF16) for i in range(2)])
                n_rs = Ring([sb("c3_rs%d" % i, [128, 512], F32) for i in range(2)])
                n_tm = Ring([sb("c3_tm%d" % i, [128, 512], F32) for i in range(4)])
                n_ps = Ring([ps("c3_psn%d" % i) for i in range(2)])
                h2r = Ring([sb("c3_h2g%d" % i, [128, 8, 512], BF16) for i in range(2)])
                npend = []

                def norm2_tile(x_, g, W, mi):
                    s_, r_, p_ = n_sq.next(), n_rs.next(), n_ps.next()
                    P.op("act", lambda e: e.activation(out=s_.t[:, :, 0:W], in_=x_.t[:, :, 0:W], func=AF.Square), reads=[x_.b], writes=[s_.b])
                    for k in range(8):
                        P.op("pe", lambda e, k=k: e.matmul(p_.t[:, 0:W], lhsT=ones_bf.t[:], rhs=s_.t[:, k, 0:W], start=(k == 0), stop=(k == 7)),
                             reads=[ones_bf.b, s_.b], writes=[p_.b])
                    P.op("act", lambda e: e.activation(out=r_.t[:, 0:W], in_=p_.t[:, 0:W], func=AF.Ln, bias=eps_t.t[:, 0:1], scale=1.0 / D),
                         reads=[p_.b, eps_t.b], writes=[r_.b])
                    P.op("act", lambda e: e.activation(out=r_.t[:, 0:W], in_=r_.t[:, 0:W], func=AF.Exp, scale=-0.5), reads=[r_.b], writes=[r_.b])
                    h2g = h2r.next()
                    for k in range(8):
                        t_ = n_tm.next()
                        P.op("dve", lambda e, k=k, t_=t_: e.scalar_tensor_tensor(
                            out=t_.t[:, 0:W], in0=x_.t[:, k, 0:W], scalar=gsf2(mi, k), in1=r_.t[:, 0:W], op0=ALU.mult, op1=ALU.mult),
                            reads=[x_.b, r_.b], writes=[t_.b])
                        P.op("act", lambda e, k=k, t_=t_: e.activation(out=h2g.t[:, k, 0:W], in_=t_.t[:, 0:W], func=AF.Identity, bias=shf2(mi, k), scale=1.0),
                             reads=[t_.b], writes=[h2g.b])
                    t0_ = GROUPS[g][0]
                    P.dma("pool", h2_d[:, :, t0_:t0_ + W].rearrange("k p t -> p k t"), h2g.t[:, :, 0:W], reads=[h2g.b], wadd=[B_h2])
                for hcol in range(2):
                    load_cast_w(W_["wout"], 0, hcol * 512, 8, 512, wst, wo, 0, hcol * 512)
                for g, (t0, W, mi) in gl:
                    mg = mgr.next()
                    P.dma_multi([("sp", mg.t[:, 0:4, 0:W], m_d[0:4, :, t0:t0 + W].rearrange("k p t -> p k t")),
                                 ("sp", mg.t[:, 4:8, 0:W], m_d[4:8, :, t0:t0 + W].rearrange("k p t -> p k t"))], reads=[B_m], writes=[mg.b])
                    xg = xgr.next()
                    P.dma_multi([("sp", xg.t[:, 0:4, 0:W], xs[0:4, :, t0:t0 + W].rearrange("k p t -> p k t")),
                                 ("sp", xg.t[:, 4:8, 0:W], xs[4:8, :, t0:t0 + W].rearrange("k p t -> p k t"))], reads=[B_xsg[g]], writes=[xg.b])
                    for d in range(8):
                        po = psO.next()
                        for k in range(8):
                            P.op("pe", lambda e, po=po, d=d, k=k, mg=mg, W=W: e.matmul(po.t[:, 0:W], lhsT=wo.t[:, k, d * 128:(d + 1) * 128], rhs=mg.t[:, k, 0:W], start=(k == 0), stop=(k == 7)),
                                 reads=[wo.b, mg.b], writes=[po.b])
                        if d == 3:
                            while npend:
                                npend.pop(0)()
                        P.op("dve", lambda e, po=po, xg=xg, d=d, W=W, mi=mi: e.scalar_tensor_tensor(out=xg.t[:, d, 0:W], in0=po.t[:, 0:W], scalar=mod_t.t[:, mi, 16 + d:17 + d], in1=xg.t[:, d, 0:W], op0=ALU.mult, op1=ALU.add),
                             reads=[po.b, xg.b, mod_t.b], writes=[xg.b])
                    P.dma("pool", xs[:, :, t0:t0 + W].rearrange("k p t -> p k t"), xg.t[:, :, 0:W], reads=[xg.b], writes=[B_xsg[g]])
                    npend.append(lambda xg=xg, g=g, W=W, mi=mi: norm2_tile(xg, g, W, mi))
                while npend:
                    npend.pop(0)()
            P.end()

            P.begin()
            with ExitStack() as es:
                def sb(name, shape, dt):
                    return T(es.enter_context(sbt(name, list(shape), dt)))

                def ps(name):
                    return T(es.enter_context(pst(name, [128, 512], F32)))
                wst = Ring([sb("d_wst%d" % i, [128, 8, 512], F32) for i in range(2)])
                w1r = [sb("d_w1b%d" % i, [128, 8, 1024], BF16) for i in range(2)]
                w2r = [sb("d_w2b%d" % i, [128, 8, 1024], BF16) for i in range(2)]
                hid = Ring([sb("d_hid%d" % i, [128, 8, 512], BF16) for i in range(2)])
                rl = Ring([sb("d_rl%d" % i, [128, 512], BF16) for i in range(3)])
                xgr = Ring([sb("d_xg%d" % i, [128, 8, 512], F32) for i in range(2)])
                h2l = Ring([sb("d_h2%d" % i, [128, 8, 512], BF16) for i in range(2)])
                gl = [(g, GROUPS[g]) for g in range(17)]
                psH = Ring([ps("d_psH%d" % i) for i in range(3)])
                psO = Ring([ps("d_psO%d" % i) for i in range(3)])

                def loads_for(p):
                    a1, a2 = w1r[p % 2], w2r[p % 2]
                    return [lambda: load_w(W_["wff1"], 0, p * 1024, 8, 512, wst, a1, 0, 0),
                            lambda: load_w(W_["wff1"], 0, p * 1024 + 512, 8, 512, wst, a1, 0, 512),
                            lambda: load_w(W_["wff2"], p * 1024, 0, 8, 512, wst, a2, 0, 0),
                            lambda: load_w(W_["wff2"], p * 1024, 512, 8, 512, wst, a2, 0, 512)]
                ld = loads_for(0)
                for i0 in (0, 2):
                    ca, cb = ld[i0](), ld[i0 + 1]()
                    ca()
                    cb()
                for p in range(4):
                    w1b, w2b = w1r[p % 2], w2r[p % 2]
                    nxt = loads_for(p + 1) if p < 3 else None
                    pend = []
                    for gi, (g, (t0, W, mi)) in enumerate(gl):
                        if nxt is not None and gi < 3:
                            for cfn in pend:
                                cfn()
                            pend = [nxt[2 * gi](), nxt[2 * gi + 1]()] if gi < 2 else []
                        h2 = h2l.next()
                        P.dma_multi([("sp", h2.t[:, 0:4, 0:W], h2_d[0:4, :, t0:t0 + W].rearrange("k p t -> p k t")),
                                     ("sp", h2.t[:, 4:8, 0:W], h2_d[4:8, :, t0:t0 + W].rearrange("k p t -> p k t"))], reads=[B_h2], writes=[h2.b])
                        hd = hid.next()
                        for hcn in range(8):
                            ph = psH.next()
                            for k in range(8):
                                P.op("pe", lambda e, ph=ph, hcn=hcn, k=k, h2=h2, W=W, w1b=w1b: e.matmul(ph.t[:, 0:W], lhsT=w1b.t[:, k, hcn * 128:(hcn + 1) * 128], rhs=h2.t[:, k, 0:W], start=(k == 0), stop=(k == 7)),
                                     reads=[w1b.b, h2.b], writes=[ph.b])
                            r_ = rl.next()
                            P.op("act", lambda e, r_=r_, ph=ph, W=W: e.activation(out=r_.t[:, 0:W], in_=ph.t[:, 0:W], func=AF.Relu), reads=[ph.b], writes=[r_.b])
                            P.op("pool", lambda e, r_=r_, hd=hd, hcn=hcn, W=W: e.tensor_tensor(out=hd.t[:, hcn, 0:W], in0=r_.t[:, 0:W], in1=r_.t[:, 0:W], op=ALU.mult), reads=[r_.b], writes=[hd.b])
                        xg = xgr.next()
                        P.dma_multi([("sp", xg.t[:, 0:4, 0:W], xs[0:4, :, t0:t0 + W].rearrange("k p t -> p k t")),
                                     ("sp", xg.t[:, 4:8, 0:W], xs[4:8, :, t0:t0 + W].rearrange("k p t -> p k t"))], reads=[B_xsg[g]], writes=[xg.b])
                        for d in range(8):
                            po = psO.next()
                            for k in range(8):
                                P.op("pe", lambda e, po=po, d=d, k=k, hd=hd, W=W, w2b=w2b: e.matmul(po.t[:, 0:W], lhsT=w2b.t[:, k, d * 128:(d + 1) * 128], rhs=hd.t[:, k, 0:W], start=(k == 0), stop=(k == 7)),
                                     reads=[w2b.b, hd.b], writes=[po.b])
                            P.op("dve", lambda e, po=po, xg=xg, d=d, W=W, mi=mi: e.scalar_tensor_tensor(out=xg.t[:, d, 0:W], in0=po.t[:, 0:W], scalar=mod_t.t[:, mi, 40 + d:41 + d], in1=xg.t[:, d, 0:W], op0=ALU.mult, op1=ALU.add),
                                 reads=[po.b, xg.b, mod_t.b], writes=[xg.b])
                        P.dma("pool", xs[:, :, t0:t0 + W].rearrange("k p t -> p k t"), xg.t[:, :, 0:W], reads=[xg.b], writes=[B_xsg[g]])
            P.end()

        if not any(s in dbg for s in ("stopS", "stopM", "stopA", "stopB")):
            P.begin()
            with ExitStack() as es:
                fo = Ring([T(es.enter_context(sbt("e_fo%d" % i, [128, 8, 512], F32))) for i in range(2)])
                cur = {}

                def fout(g, k):
                    if k == 0:
                        cur["t"] = fo.next()
                    return cur["t"].t[:, k, :], cur["t"].b

                def fstore(g):
                    t0 = GROUPS[g][0]
                    P.dma("pool", outT[:, :, t0:t0 + 512].rearrange("k p t -> p k t"), cur["t"].t[:], reads=[cur["t"].b], wadd=[B_out])
                norm_groups([(g, GROUPS[g]) for g in range(16)], lambda mi, k: gfin_t.t[:, k:k + 1], lambda mi, k: zero_t.t[:, 0:1], fout, fstore)
            P.end()
    P.close()
    return nc, P


_CACHE = {}


def prep_inputs(inp, NL=4, batches=(0, 1, 2, 3, 0, 1, 2, 3)):
    shared = dict(_consts())
    shared["gfin"] = np.ascontiguousarray(inp["g_final"].reshape(8, 128).T)
    for l in range(NL):
        shared["wmod%d" % l] = np.ascontiguousarray(inp["w_mod"][l])
        shared["vec%d" % l] = _vec(inp, l)
        shared["win%d" % l] = np.ascontiguousarray(inp["w_in"][l])
        shared["wbr%d" % l] = np.ascontiguousarray(np.concatenate([inp["w_br_attn"][l], inp["w_br_pool"][l], inp["w_br_fourier"][l], inp["w_br_conv"][l]], 0))
        shared["wout%d" % l] = np.ascontiguousarray(inp["w_out"][l])
        shared["wff1%d" % l] = np.ascontiguousarray(inp["w_ff1"][l])
        shared["wff2%d" % l] = np.ascontiguousarray(inp["w_ff2"][l])
        shared["wpool%d" % l] = _wpool_bd(inp["w_pool"][l])
        shared["btab%d" % l] = _bias_tables(inp["rpb"][l])
    maps = []
    for b in batches:
        m = dict(shared)
        xt = np.concatenate([inp["x"][b], inp["ctx"][b]], 0)
        m["xT"] = np.ascontiguousarray(xt.T.reshape(8, 128, NT))
        ct = np.stack([inp["c"][b], inp["c_ctx"]], -1)
        m["cT"] = np.ascontiguousarray(ct.reshape(8, 128, 2).transpose(1, 0, 2))
        maps.append(m)
    return maps


def kernel(**inputs):
    inp = {k: np.asarray(v) for k, v in inputs.items()}
    if "nc" not in _CACHE:
        _CACHE["nc"] = build(4)[0]
    nc = _CACHE["nc"]
    maps = prep_inputs(inp, 4)
    res = run_bass_kernel_spmd(nc, maps, core_ids=list(range(8)))
    out = np.empty((4, NTOK, D), np.float32)
    for b in range(4):
        o = res.results[b]["outT"]
        out[b] = o.reshape(D, NTOK).T
    return out
```

```python
import numpy as np
from contextlib import ExitStack
import concourse.bass as bass
import concourse.mybir as mybir
from concourse.bass_utils import run_bass_kernel_spmd

F32 = mybir.dt.float32
BF16 = mybir.dt.bfloat16
AF = mybir.ActivationFunctionType
ALU = mybir.AluOpType

D = 1024
NTOK = 8192
NCTX = 256
NT = NTOK + NCTX
GRID_W = 64
EPS = 1e-6
Q_OFF, K_OFF, V_OFF, POOL_OFF, FOUR_OFF, CONV_OFF, GATE_OFF = 0, 512, 1024, 1536, 1792, 2048, 2560
IN_WIDTH = 6656
NV = 134
GROUPS = [(g * 512, 512, 0) for g in range(16)] + [(NTOK, 256, 1)]
NBT = 21

ENGS = ("pe", "act", "dve", "pool", "sp")
DMAQ = ("sp", "act", "pool")
NDSEM = 10


class Buf:
    __slots__ = ("w", "r")

    def __init__(self):
        self.w = {}
        self.r = {}


class Prog:
    def __init__(self, nc):
        self.nc = nc
        self.es = ExitStack()
        self.sem = {e: self.es.enter_context(nc.semaphore("c_" + e)) for e in ENGS}
        self.base = {e: 0 for e in ENGS}
        self.dsem = {q: [self.es.enter_context(nc.semaphore("d_%s%d" % (q, i))) for i in range(NDSEM)] for q in DMAQ}
        self.dtot = {q: [0] * NDSEM for q in DMAQ}
        self.drr = {q: 0 for q in DMAQ}
        self.ops = None
        self.touched = None
        self.ninst = 0

    def begin(self):
        self.ops = {e: [] for e in ENGS}
        self.touched = set()

    def _deps(self, eng, reads, writes, wadd=()):
        deps = {}

        def add(evs, raw):
            for k, v in evs.items():
                if k[0] == "c" and k[1] == eng and (eng == "pe" or not raw):
                    continue
                if deps.get(k, -1) < v:
                    deps[k] = v
        for b in reads:
            add(b.w, True)
        for b in writes:
            add(b.w, False)
            add(b.r, False)
        for b in wadd:
            add(b.r, False)
        return deps

    def _mark(self, key, val, reads, writes, wadd=()):
        for b in wadd:
            self.touched.add(b)
            if b.w.get(key, -1) < val:
                b.w[key] = val
        for b in reads:
            self.touched.add(b)
            if b.r.get(key, -1) < val:
                b.r[key] = val
        for b in writes:
            self.touched.add(b)
            b.w = {key: val}
            b.r = {}

    def op(self, eng, fn, reads=(), writes=()):
        deps = self._deps(eng, reads, writes)
        idx = len(self.ops[eng])
        self.ops[eng].append([fn, deps, None, False])
        self._mark(("c", eng), idx, reads, writes)

    def dma(self, q, out, in_, reads=(), writes=(), wadd=()):
        self.dma_multi([(q, out, in_)], reads, writes, wadd)

    def dma_multi(self, parts, reads=(), writes=(), wadd=()):
        deps0 = self._deps(None, reads, writes, wadd)
        evs = []
        for (q, out, in_) in parts:
            deps = dict(deps0)
            i = self.drr[q]
            self.drr[q] = (i + 1) % NDSEM
            prev = self.dtot[q][i]
            self.dtot[q][i] = prev + 16
            if prev > 0 and deps.get(("d", q, i), -1) < prev:
                deps[("d", q, i)] = prev
            self.ops[q].append([lambda e, out=out, in_=in_: e.dma_start(out=out, in_=in_), deps, (q, i), False])
            evs.append((("d", q, i), prev + 16))
        for b in reads:
            self.touched.add(b)
            for key, val in evs:
                if b.r.get(key, -1) < val:
                    b.r[key] = val
        for b in writes:
            self.touched.add(b)
            b.w = {key: val for key, val in evs}
            b.r = {}
        for b in wadd:
            self.touched.add(b)
            for key, val in evs:
                if b.w.get(key, -1) < val:
                    b.w[key] = val

    def end(self):
        nc = self.nc
        tail = {q: {("d", q, i): self.dtot[q][i] for i in range(NDSEM) if self.dtot[q][i] > 0} for q in DMAQ}
        for e in ENGS:
            for o in self.ops[e]:
                for k, v in o[1].items():
                    if k[0] == "c":
                        self.ops[k[1]][v][3] = True
        semval = {}
        for e in ENGS:
            c = self.base[e]
            vals = []
            for o in self.ops[e]:
                if o[3]:
                    c += 1
                vals.append(c)
            semval[e] = vals
            self.base[e] = c
            self.ninst += len(vals)
        ops, sem, dsem = self.ops, self.sem, self.dsem

        def emit(ename):
            def body(e):
                waited = {}
                for fn, deps, dma, sig in ops[ename]:
                    for k, v in deps.items():
                        if k[0] == "c":
                            s, val = sem[k[1]], semval[k[1]][v]
                        else:
                            s, val = dsem[k[1]][k[2]], v
                        if waited.get(k, -1) >= val:
                            continue
                        waited[k] = val
                        e.wait_ge(s, val)
                    ins = fn(e)
                    if dma is not None:
                        ins.then_inc(dsem[dma[0]][dma[1]], 16)
                    elif sig:
                        ins.then_inc(sem[ename], 1)
                if ename in tail:
                    for k, v in tail[ename].items():
                        if waited.get(k, -1) < v:
                            e.wait_ge(dsem[k[1]][k[2]], v)
            return body
        with nc.Block() as block:
            if ops["pe"]:
                block.tensor(emit("pe"))
            if ops["act"]:
                block.scalar(emit("act"))
            if ops["dve"]:
                block.vector(emit("dve"))
            if ops["pool"]:
                block.gpsimd(emit("pool"))
            if ops["sp"]:
                block.sync(emit("sp"))
        for b in self.touched:
            b.w = {}
            b.r = {}
        self.ops = None

    def close(self):
        self.es.close()


class T:
    __slots__ = ("t", "b")

    def __init__(self, t):
        self.t = t
        self.b = Buf()


class Ring:
    def __init__(self, items):
        self.items = items
        self.i = 0

    def next(self):
        it = self.items[self.i % len(self.items)]
        self.i += 1
        return it


def _consts():
    c = {}
    t = np.arange(NTOK)
    row = (t // GRID_W).astype(np.float32)
    col = (t % GRID_W).astype(np.float32)
    inv = (np.float32(10000.0) ** (-np.arange(0, 32, 2, dtype=np.float32) / np.float32(32))).astype(np.float32)
    p = np.arange(128)
    d = p % 64
    blk = d // 16
    f = d % 16
    pos = np.where((blk < 2)[:, None], row[None, :], col[None, :]).astype(np.float32)
    ang = (pos * inv[f][:, None]).astype(np.float32)
    c["rope_cos"] = np.cos(ang).astype(np.float32)
    sgn = np.where((blk % 2) == 0, -1.0, 1.0).astype(np.float32)
    c["rope_sin"] = (np.sin(ang) * sgn[:, None]).astype(np.float32)
    partner = np.where((blk % 2) == 0, p + 16, p - 16)
    perm = np.zeros((128, 128), np.float32)
    perm[partner, p] = 1.0
    c["perm"] = perm
    k1 = np.arange(64)
    a = 2 * np.pi * np.outer(k1, k1) / 64.0
    C, S = np.cos(a), np.sin(a)
    w64 = np.zeros((128, 128), np.float64)
    w64[0:64, 0:64] = C.T
    w64[64:128, 0:64] = S.T
    w64[0:64, 64:128] = -S.T
    w64[64:128, 64:128] = C.T
    c["w64"] = w64.astype(np.float32)
    n2 = np.arange(128, dtype=np.int64)
    kk = (np.arange(64)[:, None] + 64 * np.arange(128)[None, :]).astype(np.int64)
    ph = (n2[:, None, None] * kk[None]) % 8192
    a3 = 2 * np.pi * ph.astype(np.float64) / 8192.0
    c["cf"] = np.cos(a3).astype(np.float32).reshape(128, 64 * 128)
    c["sf"] = np.sin(a3).astype(np.float32).reshape(128, 64 * 128)
    cd = 2 * np.pi * np.outer(np.arange(64), np.arange(64)) / 64.0
    cs = np.zeros((256, 512), np.float64)
    for g in range(4):
        cs[g * 64:(g + 1) * 64, g * 64:(g + 1) * 64] = np.cos(cd)
        cs[g * 64:(g + 1) * 64, 256 + g * 64:256 + (g + 1) * 64] = -np.sin(cd)
    c["cs"] = cs.astype(np.float32)
    a256 = 2 * np.pi * (np.outer(np.arange(256), np.arange(256)) % 256) / 256.0
    c["c256"] = np.cos(a256).astype(np.float32)
    c["s256"] = np.sin(a256).astype(np.float32)
    pin = np.zeros((2, 128, 4, 512), np.float32)
    wins = (2, 4, 8, 16)

    def cnt(n, w):
        tt = np.arange(n)
        lo = w // 2
        hi = w - lo - 1
        st = np.clip(tt - lo, 0, n)
        en = np.clip(tt + hi + 1, 0, n)
        return (en - st).astype(np.float32)
    for ch in range(2):
        for half in range(2):
            w = wins[ch * 2 + half]
            cl = np.float32(1.0) / cnt(NTOK, w)
            cc = np.float32(1.0) / cnt(NCTX, w)
            sl = slice(half * 64, (half + 1) * 64)
            pin[ch, sl, 0, :] = cl[:512]
            pin[ch, sl, 1, :] = np.float32(1.0) / np.float32(w)
            pin[ch, sl, 2, :] = cl[-512:]
            pin[ch, sl, 3, :256] = cc
            pin[ch, sl, 3, 256:] = 1.0
    c["poolinv"] = pin
    c["ident"] = np.eye(128, dtype=np.float32)
    return c


ATT_VARIANTS = [("int", 10, [8, 9, 10, 11, 12]), ("t0", 0, [0, 1, 2, 3]), ("t1", 1, [0, 1, 2, 3]),
                ("t62", 62, [60, 61, 62, 63]), ("t63", 63, [60, 61, 62, 63])]


def _bias_tables(rpb_l):
    out = np.full((128, 8, NBT, 128), -30000.0, np.float32)
    kr_l = (np.arange(128) // 64)[:, None]
    kc = (np.arange(128) % 64)[:, None]
    qr_l = (np.arange(128) // 64)[None, :]
    qc = (np.arange(128) % 64)[None, :]
    ci = 0
    for _, tq, keys in ATT_VARIANTS:
        for u in keys:
            r = 2 * tq + qr_l
            rs = np.clip(r - 4, 0, 120)
            kr = 2 * u + kr_l
            vrow = (kr >= rs) & (kr < rs + 8)
            drow = np.clip(kr - r + 7, 0, 14)
            cs_ = np.clip(qc - 8, 0, 48)
            vcol = (kc >= cs_) & (kc < cs_ + 16)
            dcol = np.clip(kc - qc + 15, 0, 30)
            valid = vrow & vcol
            drow_b = np.broadcast_to(drow, (128, 128))
            dcol_b = np.broadcast_to(dcol, (128, 128))
            for h in range(8):
                g = rpb_l[h][drow_b, dcol_b]
                out[:, h, ci, :] = np.where(valid, g, np.float32(-30000.0))
            ci += 1
    return out


def _vec(inp, l):
    v = np.zeros((128, NV), np.float32)
    v[:, 0:8] = inp["g_mix"][l].reshape(8, 128).T
    v[:, 8:16] = inp["g_ff"][l].reshape(8, 128).T
    v[:, 16:64] = inp["b_mod"][l].reshape(48, 128).T
    v[:, 64:66] = inp["pool_scale"][l].reshape(2, 128).T
    v[:, 66:68] = inp["b_dw"][l].reshape(2, 128).T
    v[:, 68:70] = inp["conv_ln_g"][l].reshape(2, 128).T
    v[:, 70:72] = inp["conv_ln_b"][l].reshape(2, 128).T
    wd = inp["w_dw"][l]
    for c in range(2):
        v[:, 72 + c * 31:72 + (c + 1) * 31] = wd[:, c * 128:(c + 1) * 128].T
    return v


def _wpool_bd(wp):
    o = np.zeros((2, 128, 128), np.float32)
    for g in range(4):
        c, h = g // 2, g % 2
        o[c, h * 64:(h + 1) * 64, h * 64:(h + 1) * 64] = wp[g]
    return o


def build(NL, dbg=()):
    nc = bass.Bass("TRN2", target_bir_lowering=False)
    P = Prog(nc)
    uid = {"n": 0}

    def sbt(name, shape, dt):
        uid["n"] += 1
        return nc.sbuf_tensor("%s_u%d" % (name, uid["n"]), shape, dt)

    def pst(name, shape, dt):
        uid["n"] += 1
        return nc.psum_tensor("%s_u%d" % (name, uid["n"]), shape, dt)

    def din(name, shape, dt=F32):
        return nc.dram_tensor(name, list(shape), dt, kind="ExternalInput").ap()

    def dscr(name, shape, dt):
        if name in dbg:
            return nc.dram_tensor(name, list(shape), dt, kind="ExternalOutput").ap()
        return nc.dram_tensor(name, list(shape), dt).ap()

    xin = din("xT", [8, 128, NT])
    cT = din("cT", [128, 8, 2])
    gfin = din("gfin", [128, 8])
    L = []
    for l in range(NL):
        L.append(dict(
            wmod=din("wmod%d" % l, [D, 6 * D]), vec=din("vec%d" % l, [128, NV]), win=din("win%d" % l, [D, IN_WIDTH]),
            wbr=din("wbr%d" % l, [1280, D]), wout=din("wout%d" % l, [D, D]), wff1=din("wff1%d" % l, [D, 4 * D]),
            wff2=din("wff2%d" % l, [4 * D, D]), wpool=din("wpool%d" % l, [2, 128, 128]),
            btab=din("btab%d" % l, [128, 8, NBT, 128])))
    K = dict(rope_cos=din("rope_cos", [128, NTOK]), rope_sin=din("rope_sin", [128, NTOK]), perm=din("perm", [128, 128]),
             w64=din("w64", [128, 128]), cf=din("cf", [128, 8192]), sf=din("sf", [128, 8192]), cs=din("cs", [256, 512]),
             c256=din("c256", [256, 256]), s256=din("s256", [256, 256]), poolinv=din("poolinv", [2, 128, 4, 512]), ident=din("ident", [128, 128]))
    outT = nc.dram_tensor("outT", [8, 128, NTOK], F32, kind="ExternalOutput").ap()

    xs = dscr("xs", [8, 128, NT], F32)
    qk_d = dscr("qk_d", [2, 4, 128, NT], BF16)
    v_d = dscr("v_d", [NT, 512], BF16)
    poolu_d = dscr("poolu_d", [2, 128, NT], F32)
    z_d = dscr("z_d", [NT, 512], BF16)
    convz_d = dscr("convz_d", [2, 128, NT], BF16)
    br_d = dscr("br_d", [10, 128, NT], BF16)
    zs_d = dscr("zs_d", [128, 128, 256], BF16)
    cfb_d = dscr("cfb_d", [128, 8192], BF16)
    sfb_d = dscr("sfb_d", [128, 8192], BF16)
    hT_d = dscr("hT_d", [8, 128, NT], BF16)
    h2_d = dscr("h2_d", [8, 128, NT], BF16)
    B_hT, B_h2, B_m = Buf(), Buf(), Buf()
    m_d = dscr("m_d", [8, 128, NT], BF16)
    B_qk, B_v, B_poolu, B_z, B_convz, B_br, B_zs, B_cfb, B_out = [Buf() for _ in range(9)]
    B_xsg = [Buf() for _ in range(17)]

    alt = {"i": 0}

    def evac_copy(out_ap, in_ap, reads, writes):
        alt["i"] += 1
        if alt["i"] % 2:
            P.op("act", lambda e: e.activation(out=out_ap, in_=in_ap, func=AF.Identity), reads=reads, writes=writes)
        else:
            P.op("dve", lambda e: e.tensor_copy(out=out_ap, in_=in_ap), reads=reads, writes=writes)

    with ExitStack() as gs:
        def gsb(name, shape, dt):
            return T(gs.enter_context(sbt(name, list(shape), dt)))
        ones_bf = gsb("ones_bf", [128, 128], BF16)
        ones_f = gsb("ones_f", [128, 128], F32)
        perm_bf = gsb("perm_bf", [128, 128], BF16)
        cs_bf = gsb("cs_bf", [128, 2, 512], BF16)
        w64_bf = gsb("w64_bf", [128, 128], BF16)
        ident_bf = gsb("ident_bf", [128, 128], BF16)
        c256_bf = gsb("c256_bf", [128, 2, 256], BF16)
        s256_bf = gsb("s256_bf", [128, 2, 256], BF16)
        gfin_t = gsb("gfin_t", [128, 8], F32)
        sc_t = gsb("sc_t", [128, 8, 2], F32)
        vec_t = gsb("vec_t", [128, NV], F32)
        mod_t = gsb("mod_t", [128, 2, 48], F32)
        gs1_t = gsb("gs1_t", [128, 2, 8], F32)
        gs2_t = gsb("gs2_t", [128, 2, 8], F32)
        zero_t = gsb("zero_t", [128, 1], F32)
        eps_t = gsb("eps_t", [128, 1], F32)
        wpool_bf = gsb("wpool_bf", [128, 2, 128], BF16)

        P.begin()
        with ExitStack() as es:
            def sb(name, shape, dt):
                return T(es.enter_context(sbt(name, list(shape), dt)))
            st = sb("st_a", [128, 2048], F32)
            st2 = sb("st_b", [128, 2048], F32)
            stb = sb("st_c", [128, 2048], BF16)
            stb2 = sb("st_d", [128, 2048], BF16)
            P.op("pool", lambda e: e.memset(ones_bf.t[:], 1.0), writes=[ones_bf.b])
            P.op("pool", lambda e: e.memset(ones_f.t[:], 1.0), writes=[ones_f.b])
            P.op("pool", lambda e: e.memset(zero_t.t[:], 0.0), writes=[zero_t.b])
            P.op("pool", lambda e: e.memset(eps_t.t[:], EPS), writes=[eps_t.b])
            for k in range(8):
                P.dma(("sp", "act")[k % 2], xs[k], xin[k], wadd=B_xsg)
            P.dma("sp", gfin_t.t[:], gfin, writes=[gfin_t.b])
            P.dma("sp", sc_t.t[:], cT, writes=[sc_t.b])
            P.op("act", lambda e: e.activation(out=sc_t.t[:], in_=sc_t.t[:], func=AF.Silu), reads=[sc_t.b], writes=[sc_t.b])
            P.dma("sp", st.t[:, 0:128], K["perm"], writes=[st.b])
            P.dma("sp", st.t[:, 128:256], K["w64"], writes=[st.b])
            stc = sb("st_cs", [128, 2, 512], F32)
            stq = sb("st_c256", [128, 2, 256], F32)
            sts = sb("st_s256", [128, 2, 256], F32)
            P.dma("sp", stc.t[:], K["cs"].rearrange("(c p) n -> p c n", p=128), writes=[stc.b])
            P.op("dve", lambda e: e.tensor_copy(out=perm_bf.t[:], in_=st.t[:, 0:128]), reads=[st.b], writes=[perm_bf.b])
            P.op("dve", lambda e: e.tensor_copy(out=w64_bf.t[:], in_=st.t[:, 128:256]), reads=[st.b], writes=[w64_bf.b])
            P.dma("sp", st.t[:, 256:384], K["ident"], writes=[st.b])
            P.op("dve", lambda e: e.tensor_copy(out=ident_bf.t[:], in_=st.t[:, 256:384]), reads=[st.b], writes=[ident_bf.b])
            P.op("dve", lambda e: e.tensor_copy(out=cs_bf.t[:], in_=stc.t[:]), reads=[stc.b], writes=[cs_bf.b])
            P.dma("act", stq.t[:], K["c256"].rearrange("(c p) n -> p c n", p=128), writes=[stq.b])
            P.dma("act", sts.t[:], K["s256"].rearrange("(c p) n -> p c n", p=128), writes=[sts.b])
            P.op("dve", lambda e: e.tensor_copy(out=c256_bf.t[:], in_=stq.t[:]), reads=[stq.b], writes=[c256_bf.b])
            P.op("dve", lambda e: e.tensor_copy(out=s256_bf.t[:], in_=sts.t[:]), reads=[sts.b], writes=[s256_bf.b])
            i = 0
            for src, dst in ((K["cf"], cfb_d), (K["sf"], sfb_d)):
                for j in range(4):
                    s32, s16 = (st, stb) if i % 2 == 0 else (st2, stb2)
                    sl = slice(j * 2048, (j + 1) * 2048)
                    P.dma("sp", s32.t[:], src[:, sl], writes=[s32.b])
                    eng = "dve" if i % 2 == 0 else "pool"
                    P.op(eng, lambda e, s32=s32, s16=s16: e.tensor_copy(out=s16.t[:], in_=s32.t[:]), reads=[s32.b], writes=[s16.b])
                    P.dma("act", dst[:, sl], s16.t[:], reads=[s16.b], wadd=[B_cfb])
                    i += 1
        P.end()

        def norm_groups(glist, gsf, shf, out_fn, store_fn=None):
            with ExitStack() as es:
                def sb(name, shape, dt):
                    return T(es.enter_context(sbt(name, list(shape), dt)))
                xg = Ring([sb("n_xg%d" % i, [128, 8, 512], F32) for i in range(2)])
                sq = Ring([sb("n_sq%d" % i, [128, 8, 512], BF16) for i in range(2)])
                rs = Ring([sb("n_rs%d" % i, [128, 512], F32) for i in range(2)])
                tm = Ring([sb("n_tm%d" % i, [128, 512], F32) for i in range(4)])
                psn = Ring([T(es.enter_context(pst("n_ps%d" % i, [128, 512], F32))) for i in range(2)])
                st = {}

                def stage1(i):
                    g, (t0, W, mi) = glist[i]
                    x_, s_, r_, p_ = xg.next(), sq.next(), rs.next(), psn.next()
                    P.dma_multi([("sp", x_.t[:, 0:4, 0:W], xs[0:4, :, t0:t0 + W].rearrange("k p t -> p k t")),
                                 ("sp", x_.t[:, 4:8, 0:W], xs[4:8, :, t0:t0 + W].rearrange("k p t -> p k t"))], reads=[B_xsg[g]], writes=[x_.b])
                    P.op("act", lambda e: e.activation(out=s_.t[:, :, 0:W], in_=x_.t[:, :, 0:W], func=AF.Square), reads=[x_.b], writes=[s_.b])
                    for k in range(8):
                        P.op("pe", lambda e, k=k: e.matmul(p_.t[:, 0:W], lhsT=ones_bf.t[:], rhs=s_.t[:, k, 0:W], start=(k == 0), stop=(k == 7)),
                             reads=[ones_bf.b, s_.b], writes=[p_.b])
                    P.op("act", lambda e: e.activation(out=r_.t[:, 0:W], in_=p_.t[:, 0:W], func=AF.Ln, bias=eps_t.t[:, 0:1], scale=1.0 / D),
                         reads=[p_.b, eps_t.b], writes=[r_.b])
                    P.op("act", lambda e: e.activation(out=r_.t[:, 0:W], in_=r_.t[:, 0:W], func=AF.Exp, scale=-0.5), reads=[r_.b], writes=[r_.b])
                    st[i] = (x_, r_)

                def stage2(i):
                    g, (t0, W, mi) = glist[i]
                    x_, r_ = st.pop(i)
                    for k in range(8):
                        t_ = tm.next()
                        P.op("dve", lambda e, k=k, t_=t_: e.scalar_tensor_tensor(
                            out=t_.t[:, 0:W], in0=x_.t[:, k, 0:W], scalar=gsf(mi, k), in1=r_.t[:, 0:W], op0=ALU.mult, op1=ALU.mult),
                            reads=[x_.b, r_.b], writes=[t_.b])
                        oap, ob = out_fn(g, k)
                        P.op("act", lambda e, k=k, t_=t_, oap=oap: e.activation(out=oap, in_=t_.t[:, 0:W], func=AF.Identity, bias=shf(mi, k), scale=1.0),
                             reads=[t_.b], writes=[ob])
                    if store_fn is not None:
                        store_fn(g)
                stage1(0)
                for i in range(len(glist)):
                    if i + 1 < len(glist):
                        stage1(i + 1)
                    stage2(i)

        def load_w(src2d, r0, c0, nk, ncol, st_ring, dst, dk0=0, dc0=0):
            s_ = st_ring.next()
            h = max(1, nk // 2)
            parts = [("sp", s_.t[:, 0:h, 0:ncol], src2d[r0:r0 + h * 128, c0:c0 + ncol].rearrange("(k p) c -> p k c", p=128))]
            if nk > h:
                parts.append(("act", s_.t[:, h:nk, 0:ncol], src2d[r0 + h * 128:r0 + nk * 128, c0:c0 + ncol].rearrange("(k p) c -> p k c", p=128)))
            P.dma_multi(parts, writes=[s_.b])

            def cast():
                P.op("dve", lambda e: e.tensor_copy(out=dst.t[:, dk0:dk0 + h, dc0:dc0 + ncol], in_=s_.t[:, 0:h, 0:ncol]), reads=[s_.b], writes=[dst.b])
                if nk > h:
                    P.op("act", lambda e: e.activation(out=dst.t[:, dk0 + h:dk0 + nk, dc0:dc0 + ncol], in_=s_.t[:, h:nk, 0:ncol], func=AF.Identity), reads=[s_.b], writes=[dst.b])
            return cast

        def load_cast_w(src2d, r0, c0, nk, ncol, st_ring, dst, dk0=0, dc0=0):
            load_w(src2d, r0, c0, nk, ncol, st_ring, dst, dk0, dc0)()

        def gsf1(mi, k):
            return gs1_t.t[:, mi, k:k + 1]

        def gsf2(mi, k):
            return gs2_t.t[:, mi, k:k + 1]

        def shf1(mi, k):
            return mod_t.t[:, mi, k:k + 1]

        def shf2(mi, k):
            return mod_t.t[:, mi, 24 + k:25 + k]

        for l in range(NL):
            W_ = L[l]
            if "stopS" in dbg:
                break
            P.begin()
            with ExitStack() as es:
                def sb(name, shape, dt):
                    return T(es.enter_context(sbt(name, list(shape), dt)))
                wm = Ring([sb("m_w%d" % i, [128, 8, 512], F32) for i in range(2)])
                psm = T(es.enter_context(pst("m_ps", [128, 48, 2], F32)))
                wp32 = sb("m_wp", [128, 2, 128], F32)
                P.dma("sp", vec_t.t[:], W_["vec"], writes=[vec_t.b])
                P.dma("act", wp32.t[:], W_["wpool"].rearrange("c p m -> p c m"), writes=[wp32.b])
                P.op("dve", lambda e: e.tensor_copy(out=wpool_bf.t[:], in_=wp32.t[:]), reads=[wp32.b], writes=[wpool_bf.b])
                for blk in range(12):
                    w_ = wm.next()
                    P.dma_multi([("sp", w_.t[:, 0:4, :], W_["wmod"][0:512, blk * 512:(blk + 1) * 512].rearrange("(k p) c -> p k c", p=128)),
                                 ("act", w_.t[:, 4:8, :], W_["wmod"][512:1024, blk * 512:(blk + 1) * 512].rearrange("(k p) c -> p k c", p=128))], writes=[w_.b])
                    for jj in range(4):
                        j = blk * 4 + jj
                        for k in range(8):
                            if "M1" in dbg:
                                continue
                            P.op("pe", lambda e, w_=w_, jj=jj, j=j, k=k: e.matmul(psm.t[:, j, :], lhsT=w_.t[:, k, jj * 128:(jj + 1) * 128], rhs=sc_t.t[:, k, :], start=(k == 0), stop=(k == 7)),
                                 reads=[w_.b, sc_t.b], writes=[psm.b])
                for b in range(2):
                    if "M1" in dbg or "M2" in dbg:
                        continue
                    P.op("dve", lambda e, b=b: e.tensor_tensor(out=mod_t.t[:, b, :], in0=psm.t[:, :, b], in1=vec_t.t[:, 16:64], op=ALU.add),
                         reads=[psm.b, vec_t.b], writes=[mod_t.b])
                for b in range(2):
                    if "M1" in dbg or "M2" in dbg or "M3" in dbg:
                        continue
                    P.op("dve", lambda e, b=b: e.scalar_tensor_tensor(out=gs1_t.t[:, b, :], in0=mod_t.t[:, b, 8:16], scalar=1.0, in1=vec_t.t[:, 0:8], op0=ALU.add, op1=ALU.mult),
                         reads=[mod_t.b, vec_t.b], writes=[gs1_t.b])
                    P.op("dve", lambda e, b=b: e.scalar_tensor_tensor(out=gs2_t.t[:, b, :], in0=mod_t.t[:, b, 32:40], scalar=1.0, in1=vec_t.t[:, 8:16], op0=ALU.add, op1=ALU.mult),
                         reads=[mod_t.b, vec_t.b], writes=[gs2_t.b])
            P.end()

            if "stopM" in dbg:
                break
            for hf in range(2):
                gl = [(g, GROUPS[g]) for g in (range(0, 8) if hf == 0 else range(8, 17))]
                tbase = gl[0][1][0]
                with ExitStack() as hs:
                    hT = T(hs.enter_context(sbt("a_hT", [128, 8, 4352], BF16)))

                    def hout(g, k, hT=hT, tbase=tbase):
                        t0, W, _ = GROUPS[g]
                        return hT.t[:, k, t0 - tbase:t0 - tbase + W], hT.b
                    P.begin()
                    norm_groups(gl, gsf1, shf1, hout)
                    P.end()
                    if "A1only" in dbg:
                        break
                    P.begin()
                    ntok_h = sum(GROUPS[g][1] for g, _ in gl)
                    P.dma("pool", hT_d[0:4, :, tbase:tbase + ntok_h].rearrange("k p t -> p k t"), hT.t[:, 0:4, 0:ntok_h], reads=[hT.b], wadd=[B_hT])
                    P.dma("pool", hT_d[4:8, :, tbase:tbase + ntok_h].rearrange("k p t -> p k t"), hT.t[:, 4:8, 0:ntok_h], reads=[hT.b], wadd=[B_hT])
                    with ExitStack() as es:
                        def sb(name, shape, dt):
                            return T(es.enter_context(sbt(name, list(shape), dt)))

                        def ps(name):
                            return T(es.enter_context(pst(name, [128, 512], F32)))
                        wst = Ring([sb("a_wst%d" % i, [128, 8, 512], F32) for i in range(2)])
                        wbf = Ring([sb("a_wbf%d" % i, [128, 8, 512], BF16) for i in range(2)])
                        cosr = Ring([sb("a_cos%d" % i, [128, 512], F32) for i in range(3)])
                        sinr = Ring([sb("a_sin%d" % i, [128, 512], F32) for i in range(3)])
                        qsr = Ring([sb("a_qs%d" % i, [128, 512], BF16) for i in range(3)])
                        qfr = Ring([sb("a_qf%d" % i, [128, 512], F32) for i in range(3)])
                        t1r = Ring([sb("a_t1%d" % i, [128, 512], F32) for i in range(3)])
                        t2r = Ring([sb("a_t2%d" % i, [128, 512], F32) for i in range(3)])
                        obr = Ring([sb("a_ob%d" % i, [128, 512], BF16) for i in range(3)])
                        ofr = Ring([sb("a_of%d" % i, [128, 512], F32) for i in range(3)])
                        ubr = Ring([sb("a_ub%d" % i, [128, 2, 512], BF16) for i in range(2)])
                        afr = Ring([sb("a_af%d" % i, [128, 512], F32) for i in range(2)])
                        sgr = Ring([sb("a_sg%d" % i, [128, 512], F32) for i in range(2)])
                        psA = Ring([ps("a_psA%d" % i) for i in range(3)])
                        psP = Ring([ps("a_psP%d" % i) for i in range(2)])
                        psZ = Ring([ps("a_psZ%d" % i) for i in range(2)])
                        rope_pend = []
                        wb_next = None
                        for blk in range(5):
                            if any(x.startswith("Ablk") for x in dbg) and ("Ablk%d" % blk) not in dbg:
                                continue
                            if blk == 0 or any(x.startswith("Ablk") for x in dbg):
                                wb = wbf.next()
                                load_cast_w(W_["win"], 0, blk * 512, 8, 512, wst, wb)
                            else:
                                wb = wb_next
                            pend_cast = None
                            for gidx, (g, (t0, W, mi)) in enumerate(gl):
                                if blk < 4 and not any(x.startswith("Ablk") for x in dbg):
                                    if gidx == 0:
                                        wb_next = wbf.next()
                                        pend_cast = load_w(W_["win"], 0, (blk + 1) * 512, 8, 512, wst, wb_next)
                                    elif gidx == 2:
                                        pend_cast()
                                lo = t0 - tbase
                                if blk == 2:
                                    for tt in range(W // 128):
                                        pz = psZ.next()
                                        for k in range(8):
                                            P.op("pe", lambda e, pz=pz, k=k, lo=lo, tt=tt, wb=wb: e.matmul(pz.t[:, :], lhsT=hT.t[:, k, lo + tt * 128:lo + (tt + 1) * 128], rhs=wb.t[:, k, :], start=(k == 0), stop=(k == 7)),
                                                 reads=[hT.b, wb.b], writes=[pz.b])
                                        ob = obr.next()
                                        evac_copy(ob.t[:, :], pz.t[:, :], [pz.b], [ob.b])
                                        P.dma("pool", v_d[t0 + tt * 128:t0 + (tt + 1) * 128, :], ob.t[:, :], reads=[ob.b], wadd=[B_v])
                                    continue
                                if "norope" in dbg:
                                    mi = 1
                                if blk < 2 and mi == 0:
                                    cg, sg_ = cosr.next(), sinr.next()
                                    P.dma("sp", cg.t[:, 0:W], K["rope_cos"][:, t0:t0 + W], writes=[cg.b])
                                    P.dma("sp", sg_.t[:, 0:W], K["rope_sin"][:, t0:t0 + W], writes=[sg_.b])
                                order = (0, 2, 1, 3) if blk == 4 else (0, 1, 2, 3)
                                ub = ubr.next() if blk == 3 else None
                                af = None
                                for j in order:
                                    pa = psA.next()
                                    for k in range(8):
                                        P.op("pe", lambda e, pa=pa, k=k, j=j, lo=lo, W=W, wb=wb: e.matmul(pa.t[:, 0:W], lhsT=wb.t[:, k, j * 128:(j + 1) * 128], rhs=hT.t[:, k, lo:lo + W], start=(k == 0), stop=(k == 7)),
                                             reads=[hT.b, wb.b], writes=[pa.b])
                                    while rope_pend:
                                        rope_pend.pop(0)()
                                    if blk < 2:
                                        ob = obr.next()
                                        if mi == 0:
                                            qs, t1, t2, pp = qsr.next(), t1r.next(), t2r.next(), psP.next()
                                            qf = qfr.next()
                                            P.op("act", lambda e, qf=qf, pa=pa, W=W: e.activation(out=qf.t[:, 0:W], in_=pa.t[:, 0:W], func=AF.Identity), reads=[pa.b], writes=[qf.b])
                                            P.op("act", lambda e, qs=qs, pa=pa, W=W: e.activation(out=qs.t[:, 0:W], in_=pa.t[:, 0:W], func=AF.Identity), reads=[pa.b], writes=[qs.b])

                                            def rope_tail(qs=qs, qf=qf, t1=t1, t2=t2, pp=pp, ob=ob, cg=cg, sg_=sg_, W=W, blk=blk, j=j, t0=t0):
                                                P.op("pe", lambda e: e.matmul(pp.t[:, 0:W], lhsT=perm_bf.t[:], rhs=qs.t[:, 0:W], start=True, stop=True),
                                                     reads=[perm_bf.b, qs.b], writes=[pp.b])
                                                P.op("dve", lambda e: e.tensor_tensor(out=t1.t[:, 0:W], in0=qf.t[:, 0:W], in1=cg.t[:, 0:W], op=ALU.mult),
                                                     reads=[qf.b, cg.b], writes=[t1.b])
                                                P.op("dve", lambda e: e.tensor_tensor(out=t2.t[:, 0:W], in0=pp.t[:, 0:W], in1=sg_.t[:, 0:W], op=ALU.mult),
                                                     reads=[pp.b, sg_.b], writes=[t2.b])
                                                P.op("pool", lambda e: e.tensor_tensor(out=ob.t[:, 0:W], in0=t1.t[:, 0:W], in1=t2.t[:, 0:W], op=ALU.add),
                                                     reads=[t1.b, t2.b], writes=[ob.b])
                                                P.dma("pool", qk_d[blk, j, :, t0:t0 + W], ob.t[:, 0:W], reads=[ob.b], wadd=[B_qk])
                                            rope_pend.append(rope_tail)
                                            continue
                                        else:
                                            evac_copy(ob.t[:, 0:W], pa.t[:, 0:W], [pa.b], [ob.b])
                                        P.dma("pool", qk_d[blk, j, :, t0:t0 + W], ob.t[:, 0:W], reads=[ob.b], wadd=[B_qk])
                                    elif blk == 3:
                                        if j < 2:
                                            of = ofr.next()
                                            evac_copy(of.t[:, 0:W], pa.t[:, 0:W], [pa.b], [of.b])
                                            P.dma("pool", poolu_d[j, :, t0:t0 + W], of.t[:, 0:W], reads=[of.b], wadd=[B_poolu])
                                        else:
                                            evac_copy(ub.t[:, j - 2, 0:W], pa.t[:, 0:W], [pa.b], [ub.b])
                                            if j == 3:
                                                for tt in range(W // 128):
                                                    pz = psZ.next()
                                                    for c in range(2):
                                                        P.op("pe", lambda e, pz=pz, c=c, tt=tt, ub=ub: e.matmul(pz.t[:, :], lhsT=ub.t[:, c, tt * 128:(tt + 1) * 128], rhs=cs_bf.t[:, c, :], start=(c == 0), stop=(c == 1)),
                                                             reads=[ub.b, cs_bf.b], writes=[pz.b])
                                                    ob = obr.next()
                                                    evac_copy(ob.t[:, :], pz.t[:, :], [pz.b], [ob.b])
                                                    P.dma("pool", z_d[t0 + tt * 128:t0 + (tt + 1) * 128, :], ob.t[:, :], reads=[ob.b], wadd=[B_z])
                                    else:
                                        if j < 2:
                                            af = afr.next()
                                            P.op("dve", lambda e, af=af, pa=pa, W=W: e.tensor_copy(out=af.t[:, 0:W], in_=pa.t[:, 0:W]), reads=[pa.b], writes=[af.b])
                                        else:
                                            sg2, of = sgr.next(), obr.next()
                                            P.op("act", lambda e, sg2=sg2, pa=pa, W=W: e.activation(out=sg2.t[:, 0:W], in_=pa.t[:, 0:W], func=AF.Sigmoid), reads=[pa.b], writes=[sg2.b])
                                            P.op("pool", lambda e, of=of, af=af, sg2=sg2, W=W: e.tensor_tensor(out=of.t[:, 0:W], in0=af.t[:, 0:W], in1=sg2.t[:, 0:W], op=ALU.mult),
                                                 reads=[af.b, sg2.b], writes=[of.b])
                                            P.dma("pool", convz_d[j - 2, :, t0:t0 + W], of.t[:, 0:W], reads=[of.b], wadd=[B_convz])
                            while rope_pend:
                                rope_pend.pop(0)()
                    P.end()
            if "stopA" in dbg:
                break

            P.begin()
            with ExitStack() as es:
                def sb(name, shape, dt):
                    return T(es.enter_context(sbt(name, list(shape), dt)))

                def ps(name):
                    return T(es.enter_context(pst(name, [128, 512], F32)))
                NKR = 8
                kt = [sb("b_kt%d" % i, [128, 4, 128], BF16) for i in range(NKR)]
                vt = [sb("b_vt%d" % i, [128, 8, 65], BF16) for i in range(NKR)]
                kc = sb("b_kc", [128, 4, 256], BF16)
                vc = [sb("b_vc%d" % i, [128, 8, 65], BF16) for i in range(2)]
                for v_ in vt + vc:
                    P.op("pool", lambda e, v_=v_: e.memset(v_.t[:], 1.0), writes=[v_.b])
                qtr = Ring([sb("b_qt%d" % i, [128, 4, 128], BF16) for i in range(3)])
                bst = sb("b_bst", [128, 8, 640], F32)
                btab2 = W_["btab"].rearrange("p h c q -> p h (c q)")
                bt = {}
                off = 0
                for name, _, keys in ATT_VARIANTS:
                    n = len(keys)
                    bt[name] = sb("b_bt_" + name, [128, 8, n * 128], BF16)
                    P.dma("sp", bst.t[:, :, 0:n * 128], btab2[:, :, off * 128:(off + n) * 128], writes=[bst.b])
                    P.op("act", lambda e, n=n, name=name: e.activation(out=bt[name].t[:], in_=bst.t[:, :, 0:n * 128], func=AF.Exp), reads=[bst.b], writes=[bt[name].b])
                    off += n
                Er = Ring([sb("b_E%d" % i, [128, 896], BF16) for i in range(3)])
                Pr = Ring([sb("b_P%d" % i, [128, 896], BF16) for i in range(3)])
                recr = Ring([sb("b_rec%d" % i, [128, 8], F32) for i in range(3)])
                atr = Ring([sb("b_at%d" % i, [128, 4, 128], BF16) for i in range(3)])
                attr_ = Ring([sb("b_att%d" % i, [128, 512], BF16) for i in range(3)])
                psS = Ring([(ps("b_psSa%d" % i), ps("b_psSb%d" % i)) for i in range(2)])
                psO = [T(es.enter_context(pst("b_psO%d" % i, [128, 4, 65], F32))) for i in range(2)]
                psT = T(es.enter_context(pst("b_psT", [128, 512], BF16)))
                P.dma("sp", kc.t[:], qk_d[1, :, :, NTOK:NT].rearrange("j p t -> p j t"), reads=[B_qk], writes=[kc.b])
                for i in range(2):
                    P.dma("act", vc[i].t[:, :, 0:64], v_d[NTOK + i * 128:NTOK + (i + 1) * 128, :].rearrange("p (h d) -> p h d", h=8), reads=[B_v], writes=[vc[i].b])
                state = {"loaded": -1}
                units = []
                tiles = {}

                def tile_setup(qi):
                    if qi < 64:
                        if qi == 0:
                            var, keys = "t0", [0, 1, 2, 3]
                        elif qi == 1:
                            var, keys = "t1", [0, 1, 2, 3]
                        elif qi == 62:
                            var, keys = "t62", [60, 61, 62, 63]
                        elif qi == 63:
                            var, keys = "t63", [60, 61, 62, 63]
                        else:
                            var, keys = "int", [qi - 2, qi - 1, qi, qi + 1, qi + 2]
                        while state["loaded"] < keys[-1]:
                            state["loaded"] += 1
                            u = state["loaded"]
                            P.dma("sp", kt[u % NKR].t[:], qk_d[1, :, :, u * 128:(u + 1) * 128].rearrange("j p t -> p j t"), reads=[B_qk], writes=[kt[u % NKR].b])
                            P.dma("sp", vt[u % NKR].t[:, :, 0:64], v_d[u * 128:(u + 1) * 128, :].rearrange("p (h d) -> p h d", h=8), reads=[B_v], writes=[vt[u % NKR].b])
                        chunks = [(kt[u % NKR], None, vt[u % NKR], None) for u in keys] + [(kc, 0, vc[0], 0), (kc, 1, vc[1], 1)]
                        nwin = len(keys)
                    else:
                        var, nwin = None, 0
                        chunks = [(kc, 0, vc[0], 0), (kc, 1, vc[1], 1)]
                    q0 = qi * 128
                    qt = qtr.next()
                    P.dma("sp", qt.t[:], qk_d[0, :, :, q0:q0 + 128].rearrange("j p t -> p j t"), reads=[B_qk], writes=[qt.b])
                    tiles[qi] = dict(var=var, nwin=nwin, chunks=chunks, q0=q0, qt=qt, at=atr.next(), rec=recr.next(), att=attr_.next())

                def s_stage(qi, h):
                    if h == 0:
                        tile_setup(qi)
                    tl = tiles[qi]
                    chunks, qt, nwin, var = tl["chunks"], tl["qt"], tl["nwin"], tl["var"]
                    n = len(chunks)
                    hc, p0 = h // 2, (h % 2) * 64
                    sa, sb_ = psS.next()
                    for ci, (kT_, kci, _, _) in enumerate(chunks):
                        dst = sa if ci < 4 else sb_
                        lhs = kT_.t[p0:p0 + 64, hc, :] if kci is None else kT_.t[p0:p0 + 64, hc, kci * 128:(kci + 1) * 128]
                        P.op("pe", lambda e, dst=dst, ci=ci, lhs=lhs: e.matmul(dst.t[:, (ci % 4) * 128:(ci % 4 + 1) * 128], lhsT=lhs, rhs=qt.t[p0:p0 + 64, hc, :], start=True, stop=True),
                             reads=[kT_.b, qt.b], writes=[dst.b])
                    E = Er.next()
                    na = min(n, 4)
                    P.op("act", lambda e: e.activation(out=E.t[:, 0:na * 128], in_=sa.t[:, 0:na * 128], func=AF.Exp, scale=0.125), reads=[sa.b], writes=[E.b])
                    if n > 4:
                        P.op("act", lambda e: e.activation(out=E.t[:, 512:n * 128], in_=sb_.t[:, 0:(n - 4) * 128], func=AF.Exp, scale=0.125), reads=[sb_.b], writes=[E.b])
                    Pm = None
                    if nwin > 0:
                        Pm = Pr.next()
                        btv = bt[var]
                        P.op("dve", lambda e: e.tensor_tensor(out=Pm.t[:, 0:nwin * 128], in0=E.t[:, 0:nwin * 128], in1=btv.t[:, h, 0:nwin * 128], op=ALU.mult),
                             reads=[E.b, btv.b], writes=[Pm.b])
                    tl[("EP", h)] = (E, Pm)

                def pv_stage(qi, h):
                    tl = tiles[qi]
                    chunks, nwin, rec, att, at, q0 = tl["chunks"], tl["nwin"], tl["rec"], tl["att"], tl["at"], tl["q0"]
                    n = len(chunks)
                    E, Pm = tl.pop(("EP", h))
                    po = psO[h // 4]
                    h4 = h % 4
                    for ci, (_, _, v_, vci) in enumerate(chunks):
                        src = Pm if ci < nwin else E
                        P.op("pe", lambda e, v_=v_, src=src, ci=ci: e.matmul(po.t[:, h4, :], lhsT=src.t[:, ci * 128:(ci + 1) * 128], rhs=v_.t[:, h, :], start=(ci == 0), stop=(ci == n - 1)),
                             reads=[v_.b, src.b], writes=[po.b])
                    if h4 == 3:
                        hb = h - 3
                        P.op("dve", lambda e: e.reciprocal(out=rec.t[:, hb:hb + 4], in_=po.t[:, :, 64]), reads=[po.b], writes=[rec.b])
                        for hh in range(4):
                            P.op("dve", lambda e, hh=hh: e.tensor_scalar(out=att.t[:, (hb + hh) * 64:(hb + hh + 1) * 64], in0=po.t[:, hh, 0:64], scalar1=rec.t[:, hb + hh:hb + hh + 1], scalar2=0.0, op0=ALU.mult, op1=ALU.add),
                                 reads=[po.b, rec.b], writes=[att.b])
                    if h == 7:
                        for j in range(4):
                            P.op("pe", lambda e, j=j: e.transpose(out=psT.t[:, j * 128:(j + 1) * 128], in_=att.t[:, j * 128:(j + 1) * 128], identity=ident_bf.t[:]),
                                 reads=[att.b, ident_bf.b], writes=[psT.b])
                        P.op("act", lambda e: e.activation(out=at.t[:], in_=psT.t[:], func=AF.Identity), reads=[psT.b], writes=[at.b])
                        P.dma("pool", br_d[0:4, :, q0:q0 + 128].rearrange("j p t -> p j t"), at.t[:], reads=[at.b], wadd=[B_br])
                        del tiles[qi]
                units = [(qi, h) for qi in range(66) for h in range(8)]
                s_stage(*units[0])
                for i, u_ in enumerate(units):
                    if i + 1 < len(units):
                        s_stage(*units[i + 1])
                    pv_stage(*u_)
            P.end()

            P.begin()
            mix_es = ExitStack()
            def gen_pool(es=mix_es):
                def sb(name, shape, dt):
                    return T(es.enter_context(sbt(name, list(shape), dt)))

                def ps(name):
                    return T(es.enter_context(pst(name, [128, 512], F32)))
                Ur = Ring([sb("p_U%d" % i, [128, 528], F32) for i in range(2)])
                P2r = Ring([sb("p_P2%d" % i, [128, 528], F32) for i in range(2)])
                P4r = Ring([sb("p_P4%d" % i, [128, 528], F32) for i in range(2)])
                invr = Ring([sb("p_inv%d" % i, [128, 512], F32) for i in range(2)])
                tmr = Ring([sb("p_tm%d" % i, [128, 512], F32) for i in range(2)])
                dbr = Ring([sb("p_db%d" % i, [128, 512], BF16) for i in range(3)])
                obr = Ring([sb("p_ob%d" % i, [128, 512], BF16) for i in range(2)])
                psr = Ring([ps("p_ps%d" % i) for i in range(1)])
                pool_pend = []
                blocks = [(g * 512, 512, 0 if g == 0 else (2 if g == 15 else 1), g == 0, g == 15) for g in range(16)] + [(NTOK, 256, 3, True, True)]
                for c in range(2):
                    for (t0, W, var, first, last) in blocks:
                        U, A, Bq = Ur.next(), P2r.next(), P4r.next()
                        lo = 0 if not first else 8
                        hi = W + 16 if not last else W + 8
                        if first:
                            P.op("pool", lambda e, U=U: e.memset(U.t[:, 0:8], 0.0), writes=[U.b])
                        if last:
                            P.op("pool", lambda e, U=U, W=W: e.memset(U.t[:, W + 8:W + 16], 0.0), writes=[U.b])
                        P.dma("sp", U.t[:, lo:hi], poolu_d[c, :, t0 - 8 + lo:t0 - 8 + hi], reads=[B_poolu], writes=[U.b])
                        iv = invr.next()
                        P.dma("sp", iv.t[:, 0:W], K["poolinv"][c, :, var, 0:W], writes=[iv.b])
                        n = W + 16
                        P.op("pool", lambda e, U=U, A=A, n=n: e.tensor_tensor(out=A.t[:, 1:n], in0=U.t[:, 0:n - 1], in1=U.t[:, 1:n], op=ALU.add), reads=[U.b], writes=[A.b])
                        P.op("pool", lambda e, A=A, Bq=Bq, n=n: e.tensor_tensor(out=Bq.t[:, 2:n - 1], in0=A.t[:, 1:n - 2], in1=A.t[:, 3:n], op=ALU.add), reads=[A.b], writes=[Bq.b])
                        if c == 1:
                            A2, B2 = P2r.next(), P4r.next()
                            P.op("pool", lambda e, Bq=Bq, A2=A2, n=n: e.tensor_tensor(out=A2.t[:, 4:n - 3], in0=Bq.t[:, 2:n - 5], in1=Bq.t[:, 6:n - 1], op=ALU.add), reads=[Bq.b], writes=[A2.b])
                            P.op("pool", lambda e, A2=A2, B2=B2, n=n: e.tensor_tensor(out=B2.t[:, 8:n - 7], in0=A2.t[:, 4:n - 11], in1=A2.t[:, 12:n - 3], op=ALU.add), reads=[A2.b], writes=[B2.b])
                            lo_t, hi_t = A2, B2
                        else:
                            lo_t, hi_t = A, Bq
                        tm, db = tmr.next(), dbr.next()
                        for half, src in ((0, lo_t), (1, hi_t)):
                            pp = slice(half * 64, (half + 1) * 64)
                            P.op("dve", lambda e, tm=tm, src=src, iv=iv, pp=pp, W=W: e.tensor_tensor(out=tm.t[pp, 0:W], in0=src.t[pp, 8:8 + W], in1=iv.t[pp, 0:W], op=ALU.mult),
                                 reads=[src.b, iv.b], writes=[tm.b])
                        P.op("dve", lambda e, tm=tm, db=db, U=U, W=W: e.tensor_tensor(out=db.t[:, 0:W], in0=tm.t[:, 0:W], in1=U.t[:, 8:8 + W], op=ALU.subtract),
                             reads=[tm.b, U.b], writes=[db.b])
                        def pool_tail(db=db, c=c, W=W, t0=t0):
                            pp_ = psr.next()
                            P.op("pe", lambda e: e.matmul(pp_.t[:, 0:W], lhsT=wpool_bf.t[:, c, :], rhs=db.t[:, 0:W], start=True, stop=True),
                                 reads=[wpool_bf.b, db.b], writes=[pp_.b])
                            ob = obr.next()
                            P.op("act", lambda e: e.activation(out=ob.t[:, 0:W], in_=pp_.t[:, 0:W], func=AF.Identity, scale=vec_t.t[:, 64 + c:65 + c]),
                                 reads=[pp_.b, vec_t.b], writes=[ob.b])
                            P.dma("pool", br_d[4 + c, :, t0:t0 + W], ob.t[:, 0:W], reads=[ob.b], wadd=[B_br])
                        while pool_pend:
                            pool_pend.pop(0)()
                        pool_pend.append(pool_tail)
                        yield
                while pool_pend:
                    pool_pend.pop(0)()


            def gen_conv(es=mix_es):
                def sb(name, shape, dt):
                    return T(es.enter_context(sbt(name, list(shape), dt)))

                def ps(name):
                    return T(es.enter_context(pst(name, [128, 512], F32)))
                dg = sb("c_dg", [128, 62, 128], BF16)
                for cj in range(62):
                    P.op("dve", lambda e, cj=cj: e.tensor_scalar(out=dg.t[:, cj, :], in0=ident_bf.t[:], scalar1=vec_t.t[:, 72 + cj:73 + cj], scalar2=0.0, op0=ALU.mult, op1=ALU.add),
                         reads=[ident_bf.b, vec_t.b], writes=[dg.b])
                Zr_ = [Ring([sb("c_Z%d_%d" % (c, i), [128, 542], BF16) for i in range(2)]) for c in range(2)]
                accr = [Ring([sb("c_acc%d_%d" % (c, i), [128, 512], F32) for i in range(3)]) for c in range(2)]
                sqr = Ring([sb("c_sq%d" % i, [128, 512], F32) for i in range(2)])
                mr = Ring([sb("c_m%d" % i, [128, 512], F32) for i in range(2)])
                m2r = Ring([sb("c_m2%d" % i, [128, 512], F32) for i in range(2)])
                rsr = Ring([sb("c_rs%d" % i, [128, 512], F32) for i in range(2)])
                xcr = Ring([sb("c_xc%d" % i, [128, 512], F32) for i in range(2)])
                obr = Ring([sb("c_ob%d" % i, [128, 512], BF16) for i in range(2)])
                psC = Ring([ps("c_psC%d" % i) for i in range(2)])
                ps1 = Ring([ps("c_ps1%d" % i) for i in range(1)])
                ps2 = Ring([ps("c_ps2%d" % i) for i in range(1)])
                blocks = [(g * 512, 512, g == 0, g == 15) for g in range(16)] + [(NTOK, 256, True, True)]
                conv_pend = []
                for (t0, W, first, last) in blocks:
                    accs = []
                    for c in range(2):
                        Z = Zr_[c].next()
                        lo = 15 if first else 0
                        hi = W + 15 if last else W + 30
                        if first:
                            P.op("pool", lambda e, Z=Z: e.memset(Z.t[:, 0:15], 0.0), writes=[Z.b])
                        if last:
                            P.op("pool", lambda e, Z=Z, W=W: e.memset(Z.t[:, W + 15:W + 30], 0.0), writes=[Z.b])
                        P.dma("sp", Z.t[:, lo:hi], convz_d[c, :, t0 - 15 + lo:t0 - 15 + hi], reads=[B_convz], writes=[Z.b])
                        pc = psC.next()
                        for j in range(31):
                            P.op("pe", lambda e, pc=pc, Z=Z, c=c, j=j, W=W: e.matmul(pc.t[:, 0:W], lhsT=dg.t[:, c * 31 + j, :], rhs=Z.t[:, j:j + W], start=(j == 0), stop=(j == 30)),
                                 reads=[dg.b, Z.b], writes=[pc.b])
                        acc = accr[c].next()
                        P.op("act", lambda e, acc=acc, pc=pc, c=c, W=W: e.activation(out=acc.t[:, 0:W], in_=pc.t[:, 0:W], func=AF.Identity, bias=vec_t.t[:, 66 + c:67 + c], scale=1.0),
                             reads=[pc.b, vec_t.b], writes=[acc.b])
                        accs.append(acc)
                    def conv_tail(accs=accs, W=W, t0=t0):
                        p1, p2 = ps1.next(), ps2.next()
                        for c in range(2):
                            sq = sqr.next()
                            P.op("act", lambda e, sq=sq, a=accs[c], W=W: e.activation(out=sq.t[:, 0:W], in_=a.t[:, 0:W], func=AF.Square), reads=[accs[c].b], writes=[sq.b])
                            P.op("pe", lambda e, p1=p1, a=accs[c], c=c, W=W: e.matmul(p1.t[:, 0:W], lhsT=ones_f.t[:], rhs=a.t[:, 0:W], start=(c == 0), stop=(c == 1)), reads=[ones_f.b, accs[c].b], writes=[p1.b])
                            P.op("pe", lambda e, p2=p2, sq=sq, c=c, W=W: e.matmul(p2.t[:, 0:W], lhsT=ones_f.t[:], rhs=sq.t[:, 0:W], start=(c == 0), stop=(c == 1)), reads=[ones_f.b, sq.b], writes=[p2.b])
                        m, m2, rs = mr.next(), m2r.next(), rsr.next()
                        P.op("dve", lambda e, m=m, p1=p1, W=W: e.tensor_scalar(out=m.t[:, 0:W], in0=p1.t[:, 0:W], scalar1=1.0 / 256.0, scalar2=0.0, op0=ALU.mult, op1=ALU.add), reads=[p1.b], writes=[m.b])
                        P.op("dve", lambda e, m=m, m2=m2, W=W: e.tensor_tensor(out=m2.t[:, 0:W], in0=m.t[:, 0:W], in1=m.t[:, 0:W], op=ALU.mult), reads=[m.b], writes=[m2.b])
                        P.op("dve", lambda e, rs=rs, p2=p2, m2=m2, W=W: e.scalar_tensor_tensor(out=rs.t[:, 0:W], in0=p2.t[:, 0:W], scalar=1.0 / 256.0, in1=m2.t[:, 0:W], op0=ALU.mult, op1=ALU.subtract),
                             reads=[p2.b, m2.b], writes=[rs.b])
                        P.op("act", lambda e, rs=rs, W=W: e.activation(out=rs.t[:, 0:W], in_=rs.t[:, 0:W], func=AF.Ln, bias=eps_t.t[:, 0:1], scale=1.0), reads=[rs.b, eps_t.b], writes=[rs.b])
                        P.op("act", lambda e, rs=rs, W=W: e.activation(out=rs.t[:, 0:W], in_=rs.t[:, 0:W], func=AF.Exp, scale=-0.5), reads=[rs.b], writes=[rs.b])
                        for c in range(2):
                            xc, ob = xcr.next(), obr.next()
                            eng = "dve" if c == 0 else "pool"
                            P.op(eng, lambda e, xc=xc, a=accs[c], m=m, W=W: e.tensor_tensor(out=xc.t[:, 0:W], in0=a.t[:, 0:W], in1=m.t[:, 0:W], op=ALU.subtract), reads=[accs[c].b, m.b], writes=[xc.b])
                            P.op(eng, lambda e, xc=xc, rs=rs, W=W: e.tensor_tensor(out=xc.t[:, 0:W], in0=xc.t[:, 0:W], in1=rs.t[:, 0:W], op=ALU.mult), reads=[xc.b, rs.b], writes=[xc.b])
                            P.op("act", lambda e, xc=xc, ob=ob, c=c, W=W: e.activation(out=ob.t[:, 0:W], in_=xc.t[:, 0:W], func=AF.Silu, scale=vec_t.t[:, 68 + c:69 + c], bias=vec_t.t[:, 70 + c:71 + c]),
                                 reads=[xc.b, vec_t.b], writes=[ob.b])
                            P.dma("pool", br_d[8 + c, :, t0:t0 + W], ob.t[:, 0:W], reads=[ob.b], wadd=[B_br])
                    while conv_pend:
                        conv_pend.pop(0)()
                    conv_pend.append(conv_tail)
                    yield
                while conv_pend:
                    conv_pend.pop(0)()


            def gen_four(es=mix_es):
                def sb(name, shape, dt):
                    return T(es.enter_context(sbt(name, list(shape), dt)))

                def ps(name):
                    return T(es.enter_context(pst(name, [128, 512], F32)))
                zin = Ring([sb("f_zin%d" % i, [128, 8, 256], BF16) for i in range(2)])
                zso = Ring([sb("f_zso%d" % i, [128, 2048], BF16) for i in range(2)])
                zs2 = zs_d.rearrange("p n d -> p (n d)")
                psF = Ring([ps("f_ps%d" % i) for i in range(2)])
                zv = z_d[0:NTOK, :].rearrange("(a b) f -> a b f", b=128)
                for it in range(16):
                    zi_ = zin.next()
                    P.dma_multi([("sp", zi_.t[0:64, :, :], zv[:, it * 8:(it + 1) * 8, 0:256]),
                                 ("sp", zi_.t[64:128, :, :], zv[:, it * 8:(it + 1) * 8, 256:512])], reads=[B_z], writes=[zi_.b])
                    zo = zso.next()
                    for s in range(4):
                        pf = psF.next()
                        P.op("pe", lambda e, pf=pf, zi_=zi_, s=s: e.matmul(pf.t[:, :], lhsT=w64_bf.t[:], rhs=zi_.t[:, 2 * s:2 * s + 2, :], start=True, stop=True),
                             reads=[w64_bf.b, zi_.b], writes=[pf.b])
                        evac_copy(zo.t[:, s * 512:(s + 1) * 512], pf.t[:, :], [pf.b], [zo.b])
                    P.dma("pool", zs2[:, it * 2048:(it + 1) * 2048], zo.t[:], reads=[zo.b], wadd=[B_zs])
                    yield
                FT = sb("f_FT", [128, 2, 128, 64], BF16)
                zr_r = Ring([sb("f_zr%d" % i, [128, 8, 256], BF16) for i in range(2)])
                zi_r = Ring([sb("f_zi%d" % i, [128, 8, 256], BF16) for i in range(2)])
                cf_r = Ring([sb("f_cf%d" % i, [128, 8, 128], BF16) for i in range(2)])
                sf_r = Ring([sb("f_sf%d" % i, [128, 8, 128], BF16) for i in range(2)])
                FTv = FT.t
                for kb in range(8):
                    zr, zi2, cfb, sfb = zr_r.next(), zi_r.next(), cf_r.next(), sf_r.next()
                    P.dma("sp", zr.t[:], zs_d[kb * 8:(kb + 1) * 8, :, :].rearrange("k n d -> n k d"), reads=[B_zs], writes=[zr.b])
                    P.dma("sp", zi2.t[:], zs_d[64 + kb * 8:64 + (kb + 1) * 8, :, :].rearrange("k n d -> n k d"), reads=[B_zs], writes=[zi2.b])
                    P.dma("sp", cfb.t[:], cfb_d[:, kb * 1024:(kb + 1) * 1024].rearrange("p (k m) -> p k m", k=8), reads=[B_cfb], writes=[cfb.b])
                    P.dma("sp", sfb.t[:], sfb_d[:, kb * 1024:(kb + 1) * 1024].rearrange("p (k m) -> p k m", k=8), reads=[B_cfb], writes=[sfb.b])
                    for dc in range(2):
                        for q4 in range(2):
                            pf = psF.next()
                            for a in range(4):
                                k1l = q4 * 4 + a
                                P.op("pe", lambda e, pf=pf, zr=zr, cfb=cfb, k1l=k1l, a=a, dc=dc: e.matmul(pf.t[:, a * 128:(a + 1) * 128], lhsT=zr.t[:, k1l, dc * 128:(dc + 1) * 128], rhs=cfb.t[:, k1l, :], start=True, stop=False),
                                     reads=[zr.b, cfb.b], writes=[pf.b])
                                P.op("pe", lambda e, pf=pf, zi2=zi2, sfb=sfb, k1l=k1l, a=a, dc=dc: e.matmul(pf.t[:, a * 128:(a + 1) * 128], lhsT=zi2.t[:, k1l, dc * 128:(dc + 1) * 128], rhs=sfb.t[:, k1l, :], start=False, stop=True),
                                     reads=[zi2.b, sfb.b], writes=[pf.b])
                            for a in range(4):
                                k1 = kb * 8 + q4 * 4 + a
                                eng = "act" if (dc + q4) % 2 == 0 else "dve"
                                if eng == "act":
                                    P.op("act", lambda e, pf=pf, a=a, dc=dc, k1=k1: e.activation(out=FTv[:, dc, :, k1], in_=pf.t[:, a * 128:(a + 1) * 128], func=AF.Identity, scale=float(1.0 / np.sqrt(8192.0 * 64.0))),
                                         reads=[pf.b], writes=[FT.b])
                                else:
                                    P.op("dve", lambda e, pf=pf, a=a, dc=dc, k1=k1: e.tensor_scalar(out=FTv[:, dc, :, k1], in0=pf.t[:, a * 128:(a + 1) * 128], scalar1=float(1.0 / np.sqrt(8192.0 * 64.0)), scalar2=0.0, op0=ALU.mult, op1=ALU.add),
                                         reads=[pf.b], writes=[FT.b])
                    yield
                for dc in range(2):
                    P.dma(("sp", "act")[dc], br_d[6 + dc, :, 0:NTOK].rearrange("p (a b) -> p a b", b=64), FT.t[:, dc, :, :], reads=[FT.b], wadd=[B_br])
                zc = sb("f_zc", [128, 2, 512], BF16)
                obc = sb("f_obc", [128, 2, 256], BF16)
                P.dma("sp", zc.t[:], z_d[NTOK:NT, :].rearrange("(c p) f -> p c f", p=128), reads=[B_z], writes=[zc.b])
                for dc in range(2):
                    pf = psF.next()
                    i = 0
                    for tt in range(2):
                        for (o, tab) in ((0, c256_bf), (256, s256_bf)):
                            P.op("pe", lambda e, pf=pf, tt=tt, o=o, tab=tab, dc=dc, i=i: e.matmul(pf.t[:, 0:256], lhsT=zc.t[:, tt, o + dc * 128:o + (dc + 1) * 128], rhs=tab.t[:, tt, :], start=(i == 0), stop=(i == 3)),
                                 reads=[zc.b, tab.b], writes=[pf.b])
                            i += 1
                    P.op("act", lambda e, pf=pf, dc=dc: e.activation(out=obc.t[:, dc, :], in_=pf.t[:, 0:256], func=AF.Identity, scale=1.0 / 128.0), reads=[pf.b], writes=[obc.b])
                P.dma("sp", br_d[6:8, :, NTOK:NT].rearrange("c p t -> p c t"), obc.t[:], reads=[obc.b], wadd=[B_br])
            gens = [gen_conv(), gen_pool(), gen_four()]
            while gens:
                for g_ in list(gens):
                    try:
                        next(g_)
                    except StopIteration:
                        gens.remove(g_)
            P.end()
            mix_es.close()
            if "stopB" in dbg:
                break

            for qi in range(2):
                gl = [(g, GROUPS[g]) for g in (range(0, 8) if qi == 0 else range(8, 17))]
                tbase = gl[0][1][0]

                def lofs(t0, tbase=tbase):
                    return (t0 - tbase) if t0 < NTOK else 4096
                with ExitStack() as hs:
                    hts = ExitStack()
                    hT = T(hts.enter_context(sbt("c_hT", [128, 8, 4352], BF16)))
                    P.begin()
                    parts = [("sp", hT.t[:, 0:4, 0:4096], hT_d[0:4, :, tbase:tbase + 4096].rearrange("k p t -> p k t")),
                             ("act", hT.t[:, 4:8, 0:4096], hT_d[4:8, :, tbase:tbase + 4096].rearrange("k p t -> p k t"))]
                    if qi == 1:
                        parts.append(("sp", hT.t[:, :, 4096:4352], hT_d[:, :, NTOK:NT].rearrange("k p t -> p k t")))
                    P.dma_multi(parts, reads=[B_hT], writes=[hT.b])
                    with ExitStack() as es:
                        def sb(name, shape, dt):
                            return T(es.enter_context(sbt(name, list(shape), dt)))

                        def ps(name):
                            return T(es.enter_context(pst(name, [128, 512], F32)))
                        gst = Ring([sb("c2_gst%d" % i, [128, 8, 512], F32) for i in range(2)])
                        gbf = Ring([sb("c2_gbf%d" % i, [128, 8, 512], BF16) for i in range(2)])
                        bst = Ring([sb("c2_bst%d" % i, [128, 10, 128], F32) for i in range(2)])
                        bbf = Ring([sb("c2_bbf%d" % i, [128, 10, 128], BF16) for i in range(2)])
                        brg = Ring([sb("c2_brg%d" % i, [128, 10, 512], BF16) for i in range(4)])
                        sgr = Ring([sb("c2_sg%d" % i, [128, 512], F32) for i in range(2)])
                        tr = Ring([sb("c2_t%d" % i, [128, 512], F32) for i in range(2)])
                        mar = Ring([sb("c2_ma%d" % i, [128, 512], F32) for i in range(3)])
                        mor = Ring([sb("c2_mo%d" % i, [128, 512], BF16) for i in range(3)])
                        psG = Ring([ps("c2_psG%d" % i) for i in range(2)])
                        psY = Ring([ps("c2_psY%d" % i) for i in range(2)])
                        KB = [(0, 4), (4, 2), (6, 2), (8, 2)]
                        def c2_load(d):
                            gs_, gb = gst.next(), gbf.next()
                            P.dma_multi([(("sp", "act")[b % 2], gs_.t[:, :, b * 128:(b + 1) * 128], W_["win"][:, GATE_OFF + b * 1024 + d * 128:GATE_OFF + b * 1024 + (d + 1) * 128].rearrange("(k p) c -> p k c", p=128)) for b in range(4)], writes=[gs_.b])
                            bs_, bb = bst.next(), bbf.next()
                            P.dma("sp", bs_.t[:], W_["wbr"][:, d * 128:(d + 1) * 128].rearrange("(k p) c -> p k c", p=128), writes=[bs_.b])

                            def cast():
                                P.op("dve", lambda e: e.tensor_copy(out=gb.t[:, 0:4, :], in_=gs_.t[:, 0:4, :]), reads=[gs_.b], writes=[gb.b])
                                P.op("act", lambda e: e.activation(out=gb.t[:, 4:8, :], in_=gs_.t[:, 4:8, :], func=AF.Identity), reads=[gs_.b], writes=[gb.b])
                                P.op("act", lambda e: e.activation(out=bb.t[:], in_=bs_.t[:], func=AF.Identity), reads=[bs_.b], writes=[bb.b])
                            return gb, bb, cast
                        nxt = c2_load(0)
                        nxt[2]()
                        for d in range(8):
                            gb, bb = nxt[0], nxt[1]
                            nxt = None
                            for gidx, (g, (t0, W, mi)) in enumerate(gl):
                                if d < 7 and gidx == 0:
                                    nxt = c2_load(d + 1)
                                if d < 7 and gidx == 2:
                                    nxt[2]()
                                lo = lofs(t0)
                                bg = brg.next()
                                P.dma_multi([("sp", bg.t[:, 0:5, 0:W], br_d[0:5, :, t0:t0 + W].rearrange("j p t -> p j t")),
                                             ("sp", bg.t[:, 5:10, 0:W], br_d[5:10, :, t0:t0 + W].rearrange("j p t -> p j t"))], reads=[B_br], writes=[bg.b])
                                ma = None
                                for b in range(4):
                                    pg, py = psG.next(), psY.next()
                                    for k in range(8):
                                        P.op("pe", lambda e, pg=pg, gb=gb, b=b, k=k, lo=lo, W=W: e.matmul(pg.t[:, 0:W], lhsT=gb.t[:, k, b * 128:(b + 1) * 128], rhs=hT.t[:, k, lo:lo + W], start=(k == 0), stop=(k == 7)),
                                             reads=[gb.b, hT.b], writes=[pg.b])
                                    k0, nk = KB[b]
                                    for k in range(nk):
                                        P.op("pe", lambda e, py=py, bb=bb, bg=bg, k0=k0, k=k, nk=nk, W=W: e.matmul(py.t[:, 0:W], lhsT=bb.t[:, k0 + k, :], rhs=bg.t[:, k0 + k, 0:W], start=(k == 0), stop=(k == nk - 1)),
                                             reads=[bb.b, bg.b], writes=[py.b])
                                    sg_ = sgr.next()
                                    P.op("act", lambda e, sg_=sg_, pg=pg, W=W: e.activation(out=sg_.t[:, 0:W], in_=pg.t[:, 0:W], func=AF.Sigmoid), reads=[pg.b], writes=[sg_.b])
                                    if b == 0:
                                        ma = mar.next()
                                        P.op("dve", lambda e, ma=ma, py=py, sg_=sg_, W=W: e.tensor_tensor(out=ma.t[:, 0:W], in0=py.t[:, 0:W], in1=sg_.t[:, 0:W], op=ALU.mult), reads=[py.b, sg_.b], writes=[ma.b])
                                    else:
                                        t_ = tr.next()
                                        P.op("dve", lambda e, t_=t_, py=py, sg_=sg_, W=W: e.tensor_tensor(out=t_.t[:, 0:W], in0=py.t[:, 0:W], in1=sg_.t[:, 0:W], op=ALU.mult), reads=[py.b, sg_.b], writes=[t_.b])
                                        if b < 3:
                                            nma = mar.next()
                                            P.op("pool", lambda e, nma=nma, ma=ma, t_=t_, W=W: e.tensor_tensor(out=nma.t[:, 0:W], in0=ma.t[:, 0:W], in1=t_.t[:, 0:W], op=ALU.add), reads=[ma.b, t_.b], writes=[nma.b])
                                            ma = nma
                                        else:
                                            mo = mor.next()
                                            P.op("pool", lambda e, ma=ma, t_=t_, mo=mo, W=W: e.tensor_tensor(out=mo.t[:, 0:W], in0=ma.t[:, 0:W], in1=t_.t[:, 0:W], op=ALU.add), reads=[ma.b, t_.b], writes=[mo.b])
                                            P.dma("pool", m_d[d, :, t0:t0 + W], mo.t[:, 0:W], reads=[mo.b], wadd=[B_m])
                    P.end()
                    hts.close()
            P.begin()
            with ExitStack() as es:
                def sb(name, shape, dt):
                    return T(es.enter_context(sbt(name, list(shape), dt)))

                def ps(name):
                    return T(es.enter_context(pst(name, [128, 512], F32)))
                wst = Ring([sb("c3_wst%d" % i, [128, 8, 512], F32) for i in range(2)])
                wo = sb("c3_wo", [128, 8, 1024], BF16)
                mgr = Ring([sb("c3_mg%d" % i, [128, 8, 512], BF16) for i in range(2)])
                gl = [(g, GROUPS[g]) for g in range(17)]
                xgr = Ring([sb("c3_xg%d" % i, [128, 8, 512], F32) for i in range(3)])
                psO = Ring([ps("c3_ps%d" % i) for i in range(3)])
                n_sq = Ring([sb("c3_sq%d" % i, [128, 8, 512], BF16) for i in range(2)])
                n_rs = Ring([sb("c3_rs%d" % i, [128, 512], F32) for i in range(2)])
                n_tm = Ring([sb("c3_tm%d" % i, [128, 512], F32) for i in range(4)])
                n_ps = Ring([ps("c3_psn%d" % i) for i in range(2)])
                h2r = Ring([sb("c3_h2g%d" % i, [128, 8, 512], BF16) for i in range(2)])
                npend = []

                def norm2_tile(x_, g, W, mi):
                    s_, r_, p_ = n_sq.next(), n_rs.next(), n_ps.next()
                    P.op("act", lambda e: e.activation(out=s_.t[:, :, 0:W], in_=x_.t[:, :, 0:W], func=AF.Square), reads=[x_.b], writes=[s_.b])
                    for k in range(8):
                        P.op("pe", lambda e, k=k: e.matmul(p_.t[:, 0:W], lhsT=ones_bf.t[:], rhs=s_.t[:, k, 0:W], start=(k == 0), stop=(k == 7)),
                             reads=[ones_bf.b, s_.b], writes=[p_.b])
                    P.op("act", lambda e: e.activation(out=r_.t[:, 0:W], in_=p_.t[:, 0:W], func=AF.Ln, bias=eps_t.t[:, 0:1], scale=1.0 / D),
                         reads=[p_.b, eps_t.b], writes=[r_.b])
                    P.op("act", lambda e: e.activation(out=r_.t[:, 0:W], in_=r_.t[:, 0:W], func=AF.Exp, scale=-0.5), reads=[r_.b], writes=[r_.b])
                    h2g = h2r.next()
                    for k in range(8):
                        t_ = n_tm.next()
                        P.op("dve", lambda e, k=k, t_=t_: e.scalar_tensor_tensor(
                            out=t_.t[:, 0:W], in0=x_.t[:, k, 0:W], scalar=gsf2(mi, k), in1=r_.t[:, 0:W], op0=ALU.mult, op1=ALU.mult),
                            reads=[x_.b, r_.b], writes=[t_.b])
                        P.op("act", lambda e, k=k, t_=t_: e.activation(out=h2g.t[:, k, 0:W], in_=t_.t[:, 0:W], func=AF.Identity, bias=shf2(mi, k), scale=1.0),
                             reads=[t_.b], writes=[h2g.b])
                    t0_ = GROUPS[g][0]
                    P.dma("pool", h2_d[:, :, t0_:t0_ + W].rearrange("k p t -> p k t"), h2g.t[:, :, 0:W], reads=[h2g.b], wadd=[B_h2])
                for hcol in range(2):
                    load_cast_w(W_["wout"], 0, hcol * 512, 8, 512, wst, wo, 0, hcol * 512)
                for g, (t0, W, mi) in gl:
                    mg = mgr.next()
                    P.dma_multi([("sp", mg.t[:, 0:4, 0:W], m_d[0:4, :, t0:t0 + W].rearrange("k p t -> p k t")),
                                 ("sp", mg.t[:, 4:8, 0:W], m_d[4:8, :, t0:t0 + W].rearrange("k p t -> p k t"))], reads=[B_m], writes=[mg.b])
                    xg = xgr.next()
                    P.dma_multi([("sp", xg.t[:, 0:4, 0:W], xs[0:4, :, t0:t0 + W].rearrange("k p t -> p k t")),
                                 ("sp", xg.t[:, 4:8, 0:W], xs[4:8, :, t0:t0 + W].rearrange("k p t -> p k t"))], reads=[B_xsg[g]], writes=[xg.b])
                    for d in range(8):
                        po = psO.next()
                        for k in range(8):
                            P.op("pe", lambda e, po=po, d=d, k=k, mg=mg, W=W: e.matmul(po.t[:, 0:W], lhsT=wo.t[:, k, d * 128:(d + 1) * 128], rhs=mg.t[:, k, 0:W], start=(k == 0), stop=(k == 7)),
                                 reads=[wo.b, mg.b], writes=[po.b])
                        if d == 3:
                            while npend:
                                npend.pop(0)()
                        P.op("dve", lambda e, po=po, xg=xg, d=d, W=W, mi=mi: e.scalar_tensor_tensor(out=xg.t[:, d, 0:W], in0=po.t[:, 0:W], scalar=mod_t.t[:, mi, 16 + d:17 + d], in1=xg.t[:, d, 0:W], op0=ALU.mult, op1=ALU.add),
                             reads=[po.b, xg.b, mod_t.b], writes=[xg.b])
                    P.dma("pool", xs[:, :, t0:t0 + W].rearrange("k p t -> p k t"), xg.t[:, :, 0:W], reads=[xg.b], writes=[B_xsg[g]])
                    npend.append(lambda xg=xg, g=g, W=W, mi=mi: norm2_tile(xg, g, W, mi))
                while npend:
                    npend.pop(0)()
            P.end()

            P.begin()
            with ExitStack() as es:
                def sb(name, shape, dt):
                    return T(es.enter_context(sbt(name, list(shape), dt)))

                def ps(name):
                    return T(es.enter_context(pst(name, [128, 512], F32)))
                wst = Ring([sb("d_wst%d" % i, [128, 8, 512], F32) for i in range(2)])
                w1r = [sb("d_w1b%d" % i, [128, 8, 1024], BF16) for i in range(2)]
                w2r = [sb("d_w2b%d" % i, [128, 8, 1024], BF16) for i in range(2)]
                hid = Ring([sb("d_hid%d" % i, [128, 8, 512], BF16) for i in range(2)])
                rl = Ring([sb("d_rl%d" % i, [128, 512], BF16) for i in range(3)])
                xgr = Ring([sb("d_xg%d" % i, [128, 8, 512], F32) for i in range(2)])
                h2l = Ring([sb("d_h2%d" % i, [128, 8, 512], BF16) for i in range(2)])
                gl = [(g, GROUPS[g]) for g in range(17)]
                psH = Ring([ps("d_psH%d" % i) for i in range(3)])
                psO = Ring([ps("d_psO%d" % i) for i in range(3)])

                def loads_for(p):
                    a1, a2 = w1r[p % 2], w2r[p % 2]
                    return [lambda: load_w(W_["wff1"], 0, p * 1024, 8, 512, wst, a1, 0, 0),
                            lambda: load_w(W_["wff1"], 0, p * 1024 + 512, 8, 512, wst, a1, 0, 512),
                            lambda: load_w(W_["wff2"], p * 1024, 0, 8, 512, wst, a2, 0, 0),
                            lambda: load_w(W_["wff2"], p * 1024, 512, 8, 512, wst, a2, 0, 512)]
                ld = loads_for(0)
                for i0 in (0, 2):
                    ca, cb = ld[i0](), ld[i0 + 1]()
                    ca()
                    cb()
                for p in range(4):
                    w1b, w2b = w1r[p % 2], w2r[p % 2]
                    nxt = loads_for(p + 1) if p < 3 else None
                    pend = []
                    for gi, (g, (t0, W, mi)) in enumerate(gl):
                        if nxt is not None and gi < 3:
                            for cfn in pend:
                                cfn()
                            pend = [nxt[2 * gi](), nxt[2 * gi + 1]()] if gi < 2 else []
                        h2 = h2l.next()
                        P.dma_multi([("sp", h2.t[:, 0:4, 0:W], h2_d[0:4, :, t0:t0 + W].rearrange("k p t -> p k t")),
                                     ("sp", h2.t[:, 4:8, 0:W], h2_d[4:8, :, t0:t0 + W].rearrange("k p t -> p k t"))], reads=[B_h2], writes=[h2.b])
                        hd = hid.next()
                        for hcn in range(8):
                            ph = psH.next()
                            for k in range(8):
                                P.op("pe", lambda e, ph=ph, hcn=hcn, k=k, h2=h2, W=W, w1b=w1b: e.matmul(ph.t[:, 0:W], lhsT=w1b.t[:, k, hcn * 128:(hcn + 1) * 128], rhs=h2.t[:, k, 0:W], start=(k == 0), stop=(k == 7)),
                                     reads=[w1b.b, h2.b], writes=[ph.b])
                            r_ = rl.next()
                            P.op("act", lambda e, r_=r_, ph=ph, W=W: e.activation(out=r_.t[:, 0:W], in_=ph.t[:, 0:W], func=AF.Relu), reads=[ph.b], writes=[r_.b])
                            P.op("dve", lambda e, r_=r_, hd=hd, hcn=hcn, W=W: e.tensor_tensor(out=hd.t[:, hcn, 0:W], in0=r_.t[:, 0:W], in1=r_.t[:, 0:W], op=ALU.mult), reads=[r_.b], writes=[hd.b])
                        xg = xgr.next()
                        P.dma_multi([("sp", xg.t[:, 0:4, 0:W], xs[0:4, :, t0:t0 + W].rearrange("k p t -> p k t")),
                                     ("sp", xg.t[:, 4:8, 0:W], xs[4:8, :, t0:t0 + W].rearrange("k p t -> p k t"))], reads=[B_xsg[g]], writes=[xg.b])
                        for d in range(8):
                            po = psO.next()
                            for k in range(8):
                                P.op("pe", lambda e, po=po, d=d, k=k, hd=hd, W=W, w2b=w2b: e.matmul(po.t[:, 0:W], lhsT=w2b.t[:, k, d * 128:(d + 1) * 128], rhs=hd.t[:, k, 0:W], start=(k == 0), stop=(k == 7)),
                                     reads=[w2b.b, hd.b], writes=[po.b])
                            P.op("dve", lambda e, po=po, xg=xg, d=d, W=W, mi=mi: e.scalar_tensor_tensor(out=xg.t[:, d, 0:W], in0=po.t[:, 0:W], scalar=mod_t.t[:, mi, 40 + d:41 + d], in1=xg.t[:, d, 0:W], op0=ALU.mult, op1=ALU.add),
                                 reads=[po.b, xg.b, mod_t.b], writes=[xg.b])
                        P.dma("pool", xs[:, :, t0:t0 + W].rearrange("k p t -> p k t"), xg.t[:, :, 0:W], reads=[xg.b], writes=[B_xsg[g]])
            P.end()

        if not any(s in dbg for s in ("stopS", "stopM", "stopA", "stopB")):
            P.begin()
            with ExitStack() as es:
                fo = Ring([T(es.enter_context(sbt("e_fo%d" % i, [128, 8, 512], F32))) for i in range(2)])
                cur = {}

                def fout(g, k):
                    if k == 0:
                        cur["t"] = fo.next()
                    return cur["t"].t[:, k, :], cur["t"].b

                def fstore(g):
                    t0 = GROUPS[g][0]
                    P.dma("pool", outT[:, :, t0:t0 + 512].rearrange("k p t -> p k t"), cur["t"].t[:], reads=[cur["t"].b], wadd=[B_out])
                norm_groups([(g, GROUPS[g]) for g in range(16)], lambda mi, k: gfin_t.t[:, k:k + 1], lambda mi, k: zero_t.t[:, 0:1], fout, fstore)
            P.end()
    P.close()
    return nc, P


_CACHE = {}


def prep_inputs(inp, NL=4, batches=(0, 1, 2, 3, 0, 1, 2, 3)):
    shared = dict(_consts())
    shared["gfin"] = np.ascontiguousarray(inp["g_final"].reshape(8, 128).T)
    for l in range(NL):
        shared["wmod%d" % l] = np.ascontiguousarray(inp["w_mod"][l])
        shared["vec%d" % l] = _vec(inp, l)
        shared["win%d" % l] = np.ascontiguousarray(inp["w_in"][l])
        shared["wbr%d" % l] = np.ascontiguousarray(np.concatenate([inp["w_br_attn"][l], inp["w_br_pool"][l], inp["w_br_fourier"][l], inp["w_br_conv"][l]], 0))
        shared["wout%d" % l] = np.ascontiguousarray(inp["w_out"][l])
        shared["wff1%d" % l] = np.ascontiguousarray(inp["w_ff1"][l])
        shared["wff2%d" % l] = np.ascontiguousarray(inp["w_ff2"][l])
        shared["wpool%d" % l] = _wpool_bd(inp["w_pool"][l])
        shared["btab%d" % l] = _bias_tables(inp["rpb"][l])
    maps = []
    for b in batches:
        m = dict(shared)
        xt = np.concatenate([inp["x"][b], inp["ctx"][b]], 0)
        m["xT"] = np.ascontiguousarray(xt.T.reshape(8, 128, NT))
        ct = np.stack([inp["c"][b], inp["c_ctx"]], -1)
        m["cT"] = np.ascontiguousarray(ct.reshape(8, 128, 2).transpose(1, 0, 2))
        maps.append(m)
    return maps


def kernel(**inputs):
    inp = {k: np.asarray(v) for k, v in inputs.items()}
    if "nc" not in _CACHE:
        _CACHE["nc"] = build(4)[0]
    nc = _CACHE["nc"]
    maps = prep_inputs(inp, 4)
    res = run_bass_kernel_spmd(nc, maps, core_ids=list(range(8)))
    out = np.empty((4, NTOK, D), np.float32)
    for b in range(4):
        o = res.results[b]["outT"]
        out[b] = o.reshape(D, NTOK).T
    return out
```

```python
import numpy as np
from contextlib import ExitStack
import concourse.bass as bass
import concourse.mybir as mybir
from concourse.bass_utils import run_bass_kernel_spmd

F32 = mybir.dt.float32
BF16 = mybir.dt.bfloat16
AF = mybir.ActivationFunctionType
ALU = mybir.AluOpType

D = 1024
NTOK = 8192
NCTX = 256
NT = NTOK + NCTX
GRID_W = 64
EPS = 1e-6
Q_OFF, K_OFF, V_OFF, POOL_OFF, FOUR_OFF, CONV_OFF, GATE_OFF = 0, 512, 1024, 1536, 1792, 2048, 2560
IN_WIDTH = 6656
NV = 134
GROUPS = [(g * 512, 512, 0) for g in range(16)] + [(NTOK, 256, 1)]
NBT = 21

ENGS = ("pe", "act", "dve", "pool", "sp")
DMAQ = ("sp", "act", "pool")
NDSEM = 10


class Buf:
    __slots__ = ("w", "r")

    def __init__(self):
        self.w = {}
        self.r = {}


class Prog:
    def __init__(self, nc):
        self.nc = nc
        self.es = ExitStack()
        self.sem = {e: self.es.enter_context(nc.semaphore("c_" + e)) for e in ENGS}
        self.base = {e: 0 for e in ENGS}
        self.dsem = {q: [self.es.enter_context(nc.semaphore("d_%s%d" % (q, i))) for i in range(NDSEM)] for q in DMAQ}
        self.dtot = {q: [0] * NDSEM for q in DMAQ}
        self.drr = {q: 0 for q in DMAQ}
        self.ops = None
        self.touched = None
        self.ninst = 0

    def begin(self):
        self.ops = {e: [] for e in ENGS}
        self.touched = set()

    def _deps(self, eng, reads, writes, wadd=()):
        deps = {}

        def add(evs, raw):
            for k, v in evs.items():
                if k[0] == "c" and k[1] == eng and (eng == "pe" or not raw):
                    continue
                if deps.get(k, -1) < v:
                    deps[k] = v
        for b in reads:
            add(b.w, True)
        for b in writes:
            add(b.w, False)
            add(b.r, False)
        for b in wadd:
            add(b.r, False)
        return deps

    def _mark(self, key, val, reads, writes, wadd=()):
        for b in wadd:
            self.touched.add(b)
            if b.w.get(key, -1) < val:
                b.w[key] = val
        for b in reads:
            self.touched.add(b)
            if b.r.get(key, -1) < val:
                b.r[key] = val
        for b in writes:
            self.touched.add(b)
            b.w = {key: val}
            b.r = {}

    def op(self, eng, fn, reads=(), writes=()):
        deps = self._deps(eng, reads, writes)
        idx = len(self.ops[eng])
        self.ops[eng].append([fn, deps, None, False])
        self._mark(("c", eng), idx, reads, writes)

    def dma(self, q, out, in_, reads=(), writes=(), wadd=()):
        self.dma_multi([(q, out, in_)], reads, writes, wadd)

    def dma_multi(self, parts, reads=(), writes=(), wadd=()):
        deps0 = self._deps(None, reads, writes, wadd)
        evs = []
        for (q, out, in_) in parts:
            deps = dict(deps0)
            i = self.drr[q]
            self.drr[q] = (i + 1) % NDSEM
            prev = self.dtot[q][i]
            self.dtot[q][i] = prev + 16
            if prev > 0 and deps.get(("d", q, i), -1) < prev:
                deps[("d", q, i)] = prev
            self.ops[q].append([lambda e, out=out, in_=in_: e.dma_start(out=out, in_=in_), deps, (q, i), False])
            evs.append((("d", q, i), prev + 16))
        for b in reads:
            self.touched.add(b)
            for key, val in evs:
                if b.r.get(key, -1) < val:
                    b.r[key] = val
        for b in writes:
            self.touched.add(b)
            b.w = {key: val for key, val in evs}
            b.r = {}
        for b in wadd:
            self.touched.add(b)
            for key, val in evs:
                if b.w.get(key, -1) < val:
                    b.w[key] = val

    def end(self):
        nc = self.nc
        tail = {q: {("d", q, i): self.dtot[q][i] for i in range(NDSEM) if self.dtot[q][i] > 0} for q in DMAQ}
        for e in ENGS:
            for o in self.ops[e]:
                for k, v in o[1].items():
                    if k[0] == "c":
                        self.ops[k[1]][v][3] = True
        semval = {}
        for e in ENGS:
            c = self.base[e]
            vals = []
            for o in self.ops[e]:
                if o[3]:
                    c += 1
                vals.append(c)
            semval[e] = vals
            self.base[e] = c
            self.ninst += len(vals)
        ops, sem, dsem = self.ops, self.sem, self.dsem

        def emit(ename):
            def body(e):
                waited = {}
                for fn, deps, dma, sig in ops[ename]:
                    for k, v in deps.items():
                        if k[0] == "c":
                            s, val = sem[k[1]], semval[k[1]][v]
                        else:
                            s, val = dsem[k[1]][k[2]], v
                        if waited.get(k, -1) >= val:
                            continue
                        waited[k] = val
                        e.wait_ge(s, val)
                    ins = fn(e)
                    if dma is not None:
                        ins.then_inc(dsem[dma[0]][dma[1]], 16)
                    elif sig:
                        ins.then_inc(sem[ename], 1)
                if ename in tail:
                    for k, v in tail[ename].items():
                        if waited.get(k, -1) < v:
                            e.wait_ge(dsem[k[1]][k[2]], v)
            return body
        with nc.Block() as block:
            if ops["pe"]:
                block.tensor(emit("pe"))
            if ops["act"]:
                block.scalar(emit("act"))
            if ops["dve"]:
                block.vector(emit("dve"))
            if ops["pool"]:
                block.gpsimd(emit("pool"))
            if ops["sp"]:
                block.sync(emit("sp"))
        for b in self.touched:
            b.w = {}
            b.r = {}
        self.ops = None

    def close(self):
        self.es.close()


class T:
    __slots__ = ("t", "b")

    def __init__(self, t):
        self.t = t
        self.b = Buf()


class Ring:
    def __init__(self, items):
        self.items = items
        self.i = 0

    def next(self):
        it = self.items[self.i % len(self.items)]
        self.i += 1
        return it


def _consts():
    c = {}
    t = np.arange(NTOK)
    row = (t // GRID_W).astype(np.float32)
    col = (t % GRID_W).astype(np.float32)
    inv = (np.float32(10000.0) ** (-np.arange(0, 32, 2, dtype=np.float32) / np.float32(32))).astype(np.float32)
    p = np.arange(128)
    d = p % 64
    blk = d // 16
    f = d % 16
    pos = np.where((blk < 2)[:, None], row[None, :], col[None, :]).astype(np.float32)
    ang = (pos * inv[f][:, None]).astype(np.float32)
    c["rope_cos"] = np.cos(ang).astype(np.float32)
    sgn = np.where((blk % 2) == 0, -1.0, 1.0).astype(np.float32)
    c["rope_sin"] = (np.sin(ang) * sgn[:, None]).astype(np.float32)
    partner = np.where((blk % 2) == 0, p + 16, p - 16)
    perm = np.zeros((128, 128), np.float32)
    perm[partner, p] = 1.0
    c["perm"] = perm
    k1 = np.arange(64)
    a = 2 * np.pi * np.outer(k1, k1) / 64.0
    C, S = np.cos(a), np.sin(a)
    w64 = np.zeros((128, 128), np.float64)
    w64[0:64, 0:64] = C.T
    w64[64:128, 0:64] = S.T
    w64[0:64, 64:128] = -S.T
    w64[64:128, 64:128] = C.T
    c["w64"] = w64.astype(np.float32)
    n2 = np.arange(128, dtype=np.int64)
    kk = (np.arange(64)[:, None] + 64 * np.arange(128)[None, :]).astype(np.int64)
    ph = (n2[:, None, None] * kk[None]) % 8192
    a3 = 2 * np.pi * ph.astype(np.float64) / 8192.0
    c["cf"] = np.cos(a3).astype(np.float32).reshape(128, 64 * 128)
    c["sf"] = np.sin(a3).astype(np.float32).reshape(128, 64 * 128)
    cd = 2 * np.pi * np.outer(np.arange(64), np.arange(64)) / 64.0
    cs = np.zeros((256, 512), np.float64)
    for g in range(4):
        cs[g * 64:(g + 1) * 64, g * 64:(g + 1) * 64] = np.cos(cd)
        cs[g * 64:(g + 1) * 64, 256 + g * 64:256 + (g + 1) * 64] = -np.sin(cd)
    c["cs"] = cs.astype(np.float32)
    a256 = 2 * np.pi * (np.outer(np.arange(256), np.arange(256)) % 256) / 256.0
    c["c256"] = np.cos(a256).astype(np.float32)
    c["s256"] = np.sin(a256).astype(np.float32)
    pin = np.zeros((2, 128, 4, 512), np.float32)
    wins = (2, 4, 8, 16)

    def cnt(n, w):
        tt = np.arange(n)
        lo = w // 2
        hi = w - lo - 1
        st = np.clip(tt - lo, 0, n)
        en = np.clip(tt + hi + 1, 0, n)
        return (en - st).astype(np.float32)
    for ch in range(2):
        for half in range(2):
            w = wins[ch * 2 + half]
            cl = np.float32(1.0) / cnt(NTOK, w)
            cc = np.float32(1.0) / cnt(NCTX, w)
            sl = slice(half * 64, (half + 1) * 64)
            pin[ch, sl, 0, :] = cl[:512]
            pin[ch, sl, 1, :] = np.float32(1.0) / np.float32(w)
            pin[ch, sl, 2, :] = cl[-512:]
            pin[ch, sl, 3, :256] = cc
            pin[ch, sl, 3, 256:] = 1.0
    c["poolinv"] = pin
    c["ident"] = np.eye(128, dtype=np.float32)
    return c


ATT_VARIANTS = [("int", 10, [8, 9, 10, 11, 12]), ("t0", 0, [0, 1, 2, 3]), ("t1", 1, [0, 1, 2, 3]),
                ("t62", 62, [60, 61, 62, 63]), ("t63", 63, [60, 61, 62, 63])]


def _bias_tables(rpb_l):
    out = np.full((128, 8, NBT, 128), -30000.0, np.float32)
    kr_l = (np.arange(128) // 64)[:, None]
    kc = (np.arange(128) % 64)[:, None]
    qr_l = (np.arange(128) // 64)[None, :]
    qc = (np.arange(128) % 64)[None, :]
    ci = 0
    for _, tq, keys in ATT_VARIANTS:
        for u in keys:
            r = 2 * tq + qr_l
            rs = np.clip(r - 4, 0, 120)
            kr = 2 * u + kr_l
            vrow = (kr >= rs) & (kr < rs + 8)
            drow = np.clip(kr - r + 7, 0, 14)
            cs_ = np.clip(qc - 8, 0, 48)
            vcol = (kc >= cs_) & (kc < cs_ + 16)
            dcol = np.clip(kc - qc + 15, 0, 30)
            valid = vrow & vcol
            drow_b = np.broadcast_to(drow, (128, 128))
            dcol_b = np.broadcast_to(dcol, (128, 128))
            for h in range(8):
                g = rpb_l[h][drow_b, dcol_b]
                out[:, h, ci, :] = np.where(valid, g, np.float32(-30000.0))
            ci += 1
    return out


def _vec(inp, l):
    v = np.zeros((128, NV), np.float32)
    v[:, 0:8] = inp["g_mix"][l].reshape(8, 128).T
    v[:, 8:16] = inp["g_ff"][l].reshape(8, 128).T
    v[:, 16:64] = inp["b_mod"][l].reshape(48, 128).T
    v[:, 64:66] = inp["pool_scale"][l].reshape(2, 128).T
    v[:, 66:68] = inp["b_dw"][l].reshape(2, 128).T
    v[:, 68:70] = inp["conv_ln_g"][l].reshape(2, 128).T
    v[:, 70:72] = inp["conv_ln_b"][l].reshape(2, 128).T
    wd = inp["w_dw"][l]
    for c in range(2):
        v[:, 72 + c * 31:72 + (c + 1) * 31] = wd[:, c * 128:(c + 1) * 128].T
    return v


def _wpool_bd(wp):
    o = np.zeros((2, 128, 128), np.float32)
    for g in range(4):
        c, h = g // 2, g % 2
        o[c, h * 64:(h + 1) * 64, h * 64:(h + 1) * 64] = wp[g]
    return o


def build(NL, dbg=()):
    nc = bass.Bass("TRN2", target_bir_lowering=False)
    P = Prog(nc)
    uid = {"n": 0}

    def sbt(name, shape, dt):
        uid["n"] += 1
        return nc.sbuf_tensor("%s_u%d" % (name, uid["n"]), shape, dt)

    def pst(name, shape, dt):
        uid["n"] += 1
        return nc.psum_tensor("%s_u%d" % (name, uid["n"]), shape, dt)

    def din(name, shape, dt=F32):
        return nc.dram_tensor(name, list(shape), dt, kind="ExternalInput").ap()

    def dscr(name, shape, dt):
        if name in dbg:
            return nc.dram_tensor(name, list(shape), dt, kind="ExternalOutput").ap()
        return nc.dram_tensor(name, list(shape), dt).ap()

    xin = din("xT", [8, 128, NT])
    cT = din("cT", [128, 8, 2])
    gfin = din("gfin", [128, 8])
    L = []
    for l in range(NL):
        L.append(dict(
            wmod=din("wmod%d" % l, [D, 6 * D]), vec=din("vec%d" % l, [128, NV]), win=din("win%d" % l, [D, IN_WIDTH]),
            wbr=din("wbr%d" % l, [1280, D]), wout=din("wout%d" % l, [D, D]), wff1=din("wff1%d" % l, [D, 4 * D]),
            wff2=din("wff2%d" % l, [4 * D, D]), wpool=din("wpool%d" % l, [2, 128, 128]),
            btab=din("btab%d" % l, [128, 8, NBT, 128])))
    K = dict(rope_cos=din("rope_cos", [128, NTOK]), rope_sin=din("rope_sin", [128, NTOK]), perm=din("perm", [128, 128]),
             w64=din("w64", [128, 128]), cf=din("cf", [128, 8192]), sf=din("sf", [128, 8192]), cs=din("cs", [256, 512]),
             c256=din("c256", [256, 256]), s256=din("s256", [256, 256]), poolinv=din("poolinv", [2, 128, 4, 512]), ident=din("ident", [128, 128]))
    outT = nc.dram_tensor("outT", [8, 128, NTOK], F32, kind="ExternalOutput").ap()

    xs = dscr("xs", [8, 128, NT], F32)
    qk_d = dscr("qk_d", [2, 4, 128, NT], BF16)
    v_d = dscr("v_d", [NT, 512], BF16)
    poolu_d = dscr("poolu_d", [2, 128, NT], F32)
    z_d = dscr("z_d", [NT, 512], BF16)
    convz_d = dscr("convz_d", [2, 128, NT], BF16)
    br_d = dscr("br_d", [10, 128, NT], BF16)
    zs_d = dscr("zs_d", [128, 128, 256], BF16)
    cfb_d = dscr("cfb_d", [128, 8192], BF16)
    sfb_d = dscr("sfb_d", [128, 8192], BF16)
    hT_d = dscr("hT_d", [8, 128, NT], BF16)
    h2_d = dscr("h2_d", [8, 128, NT], BF16)
    B_hT, B_h2, B_m = Buf(), Buf(), Buf()
    m_d = dscr("m_d", [8, 128, NT], BF16)
    B_qk, B_v, B_poolu, B_z, B_convz, B_br, B_zs, B_cfb, B_out = [Buf() for _ in range(9)]
    B_xsg = [Buf() for _ in range(17)]

    alt = {"i": 0}

    def evac_copy(out_ap, in_ap, reads, writes):
        alt["i"] += 1
        if alt["i"] % 2:
            P.op("act", lambda e: e.activation(out=out_ap, in_=in_ap, func=AF.Identity), reads=reads, writes=writes)
        else:
            P.op("dve", lambda e: e.tensor_copy(out=out_ap, in_=in_ap), reads=reads, writes=writes)

    with ExitStack() as gs:
        def gsb(name, shape, dt):
            return T(gs.enter_context(sbt(name, list(shape), dt)))
        ones_bf = gsb("ones_bf", [128, 128], BF16)
        ones_f = gsb("ones_f", [128, 128], F32)
        perm_bf = gsb("perm_bf", [128, 128], BF16)
        cs_bf = gsb("cs_bf", [128, 2, 512], BF16)
        w64_bf = gsb("w64_bf", [128, 128], BF16)
        ident_bf = gsb("ident_bf", [128, 128], BF16)
        c256_bf = gsb("c256_bf", [128, 2, 256], BF16)
        s256_bf = gsb("s256_bf", [128, 2, 256], BF16)
        gfin_t = gsb("gfin_t", [128, 8], F32)
        sc_t = gsb("sc_t", [128, 8, 2], F32)
        vec_t = gsb("vec_t", [128, NV], F32)
        mod_t = gsb("mod_t", [128, 2, 48], F32)
        gs1_t = gsb("gs1_t", [128, 2, 8], F32)
        gs2_t = gsb("gs2_t", [128, 2, 8], F32)
        zero_t = gsb("zero_t", [128, 1], F32)
        eps_t = gsb("eps_t", [128, 1], F32)
        wpool_bf = gsb("wpool_bf", [128, 2, 128], BF16)

        P.begin()
        with ExitStack() as es:
            def sb(name, shape, dt):
                return T(es.enter_context(sbt(name, list(shape), dt)))
            st = sb("st_a", [128, 2048], F32)
            st2 = sb("st_b", [128, 2048], F32)
            stb = sb("st_c", [128, 2048], BF16)
            stb2 = sb("st_d", [128, 2048], BF16)
            P.op("pool", lambda e: e.memset(ones_bf.t[:], 1.0), writes=[ones_bf.b])
            P.op("pool", lambda e: e.memset(ones_f.t[:], 1.0), writes=[ones_f.b])
            P.op("pool", lambda e: e.memset(zero_t.t[:], 0.0), writes=[zero_t.b])
            P.op("pool", lambda e: e.memset(eps_t.t[:], EPS), writes=[eps_t.b])
            for k in range(8):
                P.dma(("sp", "act")[k % 2], xs[k], xin[k], wadd=B_xsg)
            P.dma("sp", gfin_t.t[:], gfin, writes=[gfin_t.b])
            P.dma("sp", sc_t.t[:], cT, writes=[sc_t.b])
            P.op("act", lambda e: e.activation(out=sc_t.t[:], in_=sc_t.t[:], func=AF.Silu), reads=[sc_t.b], writes=[sc_t.b])
            P.dma("sp", st.t[:, 0:128], K["perm"], writes=[st.b])
            P.dma("sp", st.t[:, 128:256], K["w64"], writes=[st.b])
            stc = sb("st_cs", [128, 2, 512], F32)
            stq = sb("st_c256", [128, 2, 256], F32)
            sts = sb("st_s256", [128, 2, 256], F32)
            P.dma("sp", stc.t[:], K["cs"].rearrange("(c p) n -> p c n", p=128), writes=[stc.b])
            P.op("dve", lambda e: e.tensor_copy(out=perm_bf.t[:], in_=st.t[:, 0:128]), reads=[st.b], writes=[perm_bf.b])
            P.op("dve", lambda e: e.tensor_copy(out=w64_bf.t[:], in_=st.t[:, 128:256]), reads=[st.b], writes=[w64_bf.b])
            P.dma("sp", st.t[:, 256:384], K["ident"], writes=[st.b])
            P.op("dve", lambda e: e.tensor_copy(out=ident_bf.t[:], in_=st.t[:, 256:384]), reads=[st.b], writes=[ident_bf.b])
            P.op("dve", lambda e: e.tensor_copy(out=cs_bf.t[:], in_=stc.t[:]), reads=[stc.b], writes=[cs_bf.b])
            P.dma("act", stq.t[:], K["c256"].rearrange("(c p) n -> p c n", p=128), writes=[stq.b])
            P.dma("act", sts.t[:], K["s256"].rearrange("(c p) n -> p c n", p=128), writes=[sts.b])
            P.op("dve", lambda e: e.tensor_copy(out=c256_bf.t[:], in_=stq.t[:]), reads=[stq.b], writes=[c256_bf.b])
            P.op("dve", lambda e: e.tensor_copy(out=s256_bf.t[:], in_=sts.t[:]), reads=[sts.b], writes=[s256_bf.b])
            i = 0
            for src, dst in ((K["cf"], cfb_d), (K["sf"], sfb_d)):
                for j in range(4):
                    s32, s16 = (st, stb) if i % 2 == 0 else (st2, stb2)
                    sl = slice(j * 2048, (j + 1) * 2048)
                    P.dma("sp", s32.t[:], src[:, sl], writes=[s32.b])
                    eng = "dve" if i % 2 == 0 else "pool"
                    P.op(eng, lambda e, s32=s32, s16=s16: e.tensor_copy(out=s16.t[:], in_=s32.t[:]), reads=[s32.b], writes=[s16.b])
                    P.dma("act", dst[:, sl], s16.t[:], reads=[s16.b], wadd=[B_cfb])
                    i += 1
        P.end()

        def norm_groups(glist, gsf, shf, out_fn, store_fn=None):
            with ExitStack() as es:
                def sb(name, shape, dt):
                    return T(es.enter_context(sbt(name, list(shape), dt)))
                xg = Ring([sb("n_xg%d" % i, [128, 8, 512], F32) for i in range(2)])
                sq = Ring([sb("n_sq%d" % i, [128, 8, 512], BF16) for i in range(2)])
                rs = Ring([sb("n_rs%d" % i, [128, 512], F32) for i in range(2)])
                tm = Ring([sb("n_tm%d" % i, [128, 512], F32) for i in range(4)])
                psn = Ring([T(es.enter_context(pst("n_ps%d" % i, [128, 512], F32))) for i in range(2)])
                st = {}

                def stage1(i):
                    g, (t0, W, mi) = glist[i]
                    x_, s_, r_, p_ = xg.next(), sq.next(), rs.next(), psn.next()
                    P.dma_multi([("sp", x_.t[:, 0:4, 0:W], xs[0:4, :, t0:t0 + W].rearrange("k p t -> p k t")),
                                 ("sp", x_.t[:, 4:8, 0:W], xs[4:8, :, t0:t0 + W].rearrange("k p t -> p k t"))], reads=[B_xsg[g]], writes=[x_.b])
                    P.op("act", lambda e: e.activation(out=s_.t[:, :, 0:W], in_=x_.t[:, :, 0:W], func=AF.Square), reads=[x_.b], writes=[s_.b])
                    for k in range(8):
                        P.op("pe", lambda e, k=k: e.matmul(p_.t[:, 0:W], lhsT=ones_bf.t[:], rhs=s_.t[:, k, 0:W], start=(k == 0), stop=(k == 7)),
                             reads=[ones_bf.b, s_.b], writes=[p_.b])
                    P.op("act", lambda e: e.activation(out=r_.t[:, 0:W], in_=p_.t[:, 0:W], func=AF.Ln, bias=eps_t.t[:, 0:1], scale=1.0 / D),
                         reads=[p_.b, eps_t.b], writes=[r_.b])
                    P.op("act", lambda e: e.activation(out=r_.t[:, 0:W], in_=r_.t[:, 0:W], func=AF.Exp, scale=-0.5), reads=[r_.b], writes=[r_.b])
                    st[i] = (x_, r_)

                def stage2(i):
                    g, (t0, W, mi) = glist[i]
                    x_, r_ = st.pop(i)
                    for k in range(8):
                        t_ = tm.next()
                        P.op("dve", lambda e, k=k, t_=t_: e.scalar_tensor_tensor(
                            out=t_.t[:, 0:W], in0=x_.t[:, k, 0:W], scalar=gsf(mi, k), in1=r_.t[:, 0:W], op0=ALU.mult, op1=ALU.mult),
                            reads=[x_.b, r_.b], writes=[t_.b])
                        oap, ob = out_fn(g, k)
                        P.op("act", lambda e, k=k, t_=t_, oap=oap: e.activation(out=oap, in_=t_.t[:, 0:W], func=AF.Identity, bias=shf(mi, k), scale=1.0),
                             reads=[t_.b], writes=[ob])
                    if store_fn is not None:
                        store_fn(g)
                stage1(0)
                for i in range(len(glist)):
                    if i + 1 < len(glist):
                        stage1(i + 1)
                    stage2(i)

        def load_w(src2d, r0, c0, nk, ncol, st_ring, dst, dk0=0, dc0=0):
            s_ = st_ring.next()
            h = max(1, nk // 2)
            parts = [("sp", s_.t[:, 0:h, 0:ncol], src2d[r0:r0 + h * 128, c0:c0 + ncol].rearrange("(k p) c -> p k c", p=128))]
            if nk > h:
                parts.append(("act", s_.t[:, h:nk, 0:ncol], src2d[r0 + h * 128:r0 + nk * 128, c0:c0 + ncol].rearrange("(k p) c -> p k c", p=128)))
            P.dma_multi(parts, writes=[s_.b])

            def cast():
                P.op("dve", lambda e: e.tensor_copy(out=dst.t[:, dk0:dk0 + h, dc0:dc0 + ncol], in_=s_.t[:, 0:h, 0:ncol]), reads=[s_.b], writes=[dst.b])
                if nk > h:
                    P.op("act", lambda e: e.activation(out=dst.t[:, dk0 + h:dk0 + nk, dc0:dc0 + ncol], in_=s_.t[:, h:nk, 0:ncol], func=AF.Identity), reads=[s_.b], writes=[dst.b])
            return cast

        def load_cast_w(src2d, r0, c0, nk, ncol, st_ring, dst, dk0=0, dc0=0):
            load_w(src2d, r0, c0, nk, ncol, st_ring, dst, dk0, dc0)()

        def gsf1(mi, k):
            return gs1_t.t[:, mi, k:k + 1]

        def gsf2(mi, k):
            return gs2_t.t[:, mi, k:k + 1]

        def shf1(mi, k):
            return mod_t.t[:, mi, k:k + 1]

        def shf2(mi, k):
            return mod_t.t[:, mi, 24 + k:25 + k]

        for l in range(NL):
            W_ = L[l]
            if "stopS" in dbg:
                break
            P.begin()
            with ExitStack() as es:
                def sb(name, shape, dt):
                    return T(es.enter_context(sbt(name, list(shape), dt)))
                wm = Ring([sb("m_w%d" % i, [128, 8, 512], F32) for i in range(2)])
                psm = T(es.enter_context(pst("m_ps", [128, 48, 2], F32)))
                wp32 = sb("m_wp", [128, 2, 128], F32)
                P.dma("sp", vec_t.t[:], W_["vec"], writes=[vec_t.b])
                P.dma("act", wp32.t[:], W_["wpool"].rearrange("c p m -> p c m"), writes=[wp32.b])
                P.op("dve", lambda e: e.tensor_copy(out=wpool_bf.t[:], in_=wp32.t[:]), reads=[wp32.b], writes=[wpool_bf.b])
                for blk in range(12):
                    w_ = wm.next()
                    P.dma_multi([("sp", w_.t[:, 0:4, :], W_["wmod"][0:512, blk * 512:(blk + 1) * 512].rearrange("(k p) c -> p k c", p=128)),
                                 ("act", w_.t[:, 4:8, :], W_["wmod"][512:1024, blk * 512:(blk + 1) * 512].rearrange("(k p) c -> p k c", p=128))], writes=[w_.b])
                    for jj in range(4):
                        j = blk * 4 + jj
                        for k in range(8):
                            if "M1" in dbg:
                                continue
                            P.op("pe", lambda e, w_=w_, jj=jj, j=j, k=k: e.matmul(psm.t[:, j, :], lhsT=w_.t[:, k, jj * 128:(jj + 1) * 128], rhs=sc_t.t[:, k, :], start=(k == 0), stop=(k == 7)),
                                 reads=[w_.b, sc_t.b], writes=[psm.b])
                for b in range(2):
                    if "M1" in dbg or "M2" in dbg:
                        continue
                    P.op("dve", lambda e, b=b: e.tensor_tensor(out=mod_t.t[:, b, :], in0=psm.t[:, :, b], in1=vec_t.t[:, 16:64], op=ALU.add),
                         reads=[psm.b, vec_t.b], writes=[mod_t.b])
                for b in range(2):
                    if "M1" in dbg or "M2" in dbg or "M3" in dbg:
                        continue
                    P.op("dve", lambda e, b=b: e.scalar_tensor_tensor(out=gs1_t.t[:, b, :], in0=mod_t.t[:, b, 8:16], scalar=1.0, in1=vec_t.t[:, 0:8], op0=ALU.add, op1=ALU.mult),
                         reads=[mod_t.b, vec_t.b], writes=[gs1_t.b])
                    P.op("dve", lambda e, b=b: e.scalar_tensor_tensor(out=gs2_t.t[:, b, :], in0=mod_t.t[:, b, 32:40], scalar=1.0, in1=vec_t.t[:, 8:16], op0=ALU.add, op1=ALU.mult),
                         reads=[mod_t.b, vec_t.b], writes=[gs2_t.b])
            P.end()

            if "stopM" in dbg:
                break
            for hf in range(2):
                gl = [(g, GROUPS[g]) for g in (range(0, 8) if hf == 0 else range(8, 17))]
                tbase = gl[0][1][0]
                with ExitStack() as hs:
                    hT = T(hs.enter_context(sbt("a_hT", [128, 8, 4352], BF16)))

                    def hout(g, k, hT=hT, tbase=tbase):
                        t0, W, _ = GROUPS[g]
                        return hT.t[:, k, t0 - tbase:t0 - tbase + W], hT.b
                    P.begin()
                    norm_groups(gl, gsf1, shf1, hout)
                    P.end()
                    if "A1only" in dbg:
                        break
                    P.begin()
                    ntok_h = sum(GROUPS[g][1] for g, _ in gl)
                    P.dma("pool", hT_d[0:4, :, tbase:tbase + ntok_h].rearrange("k p t -> p k t"), hT.t[:, 0:4, 0:ntok_h], reads=[hT.b], wadd=[B_hT])
                    P.dma("pool", hT_d[4:8, :, tbase:tbase + ntok_h].rearrange("k p t -> p k t"), hT.t[:, 4:8, 0:ntok_h], reads=[hT.b], wadd=[B_hT])
                    with ExitStack() as es:
                        def sb(name, shape, dt):
                            return T(es.enter_context(sbt(name, list(shape), dt)))

                        def ps(name):
                            return T(es.enter_context(pst(name, [128, 512], F32)))
                        wst = Ring([sb("a_wst%d" % i, [128, 8, 512], F32) for i in range(2)])
                        wbf = Ring([sb("a_wbf%d" % i, [128, 8, 512], BF16) for i in range(2)])
                        cosr = Ring([sb("a_cos%d" % i, [128, 512], F32) for i in range(3)])
                        sinr = Ring([sb("a_sin%d" % i, [128, 512], F32) for i in range(3)])
                        qsr = Ring([sb("a_qs%d" % i, [128, 512], BF16) for i in range(3)])
                        qfr = Ring([sb("a_qf%d" % i, [128, 512], F32) for i in range(3)])
                        t1r = Ring([sb("a_t1%d" % i, [128, 512], F32) for i in range(3)])
                        t2r = Ring([sb("a_t2%d" % i, [128, 512], F32) for i in range(3)])
                        obr = Ring([sb("a_ob%d" % i, [128, 512], BF16) for i in range(3)])
                        ofr = Ring([sb("a_of%d" % i, [128, 512], F32) for i in range(3)])
                        ubr = Ring([sb("a_ub%d" % i, [128, 2, 512], BF16) for i in range(2)])
                        afr = Ring([sb("a_af%d" % i, [128, 512], F32) for i in range(2)])
                        sgr = Ring([sb("a_sg%d" % i, [128, 512], F32) for i in range(2)])
                        psA = Ring([ps("a_psA%d" % i) for i in range(3)])
                        psP = Ring([ps("a_psP%d" % i) for i in range(2)])
                        psZ = Ring([ps("a_psZ%d" % i) for i in range(2)])
                        rope_pend = []
                        wb_next = None
                        for blk in range(5):
                            if any(x.startswith("Ablk") for x in dbg) and ("Ablk%d" % blk) not in dbg:
                                continue
                            if blk == 0 or any(x.startswith("Ablk") for x in dbg):
                                wb = wbf.next()
                                load_cast_w(W_["win"], 0, blk * 512, 8, 512, wst, wb)
                            else:
                                wb = wb_next
                            pend_cast = None
                            for gidx, (g, (t0, W, mi)) in enumerate(gl):
                                if blk < 4 and not any(x.startswith("Ablk") for x in dbg):
                                    if gidx == 0:
                                        wb_next = wbf.next()
                                        pend_cast = load_w(W_["win"], 0, (blk + 1) * 512, 8, 512, wst, wb_next)
                                    elif gidx == 2:
                                        pend_cast()
                                lo = t0 - tbase
                                if blk == 2:
                                    for tt in range(W // 128):
                                        pz = psZ.next()
                                        for k in range(8):
                                            P.op("pe", lambda e, pz=pz, k=k, lo=lo, tt=tt, wb=wb: e.matmul(pz.t[:, :], lhsT=hT.t[:, k, lo + tt * 128:lo + (tt + 1) * 128], rhs=wb.t[:, k, :], start=(k == 0), stop=(k == 7)),
                                                 reads=[hT.b, wb.b], writes=[pz.b])
                                        ob = obr.next()
                                        evac_copy(ob.t[:, :], pz.t[:, :], [pz.b], [ob.b])
                                        P.dma("pool", v_d[t0 + tt * 128:t0 + (tt + 1) * 128, :], ob.t[:, :], reads=[ob.b], wadd=[B_v])
                                    continue
                                if "norope" in dbg:
                                    mi = 1
                                if blk < 2 and mi == 0:
                                    cg, sg_ = cosr.next(), sinr.next()
                                    P.dma("sp", cg.t[:, 0:W], K["rope_cos"][:, t0:t0 + W], writes=[cg.b])
                                    P.dma("sp", sg_.t[:, 0:W], K["rope_sin"][:, t0:t0 + W], writes=[sg_.b])
                                order = (0, 2, 1, 3) if blk == 4 else (0, 1, 2, 3)
                                ub = ubr.next() if blk == 3 else None
                                af = None
                                for j in order:
                                    pa = psA.next()
                                    for k in range(8):
                                        P.op("pe", lambda e, pa=pa, k=k, j=j, lo=lo, W=W, wb=wb: e.matmul(pa.t[:, 0:W], lhsT=wb.t[:, k, j * 128:(j + 1) * 128], rhs=hT.t[:, k, lo:lo + W], start=(k == 0), stop=(k == 7)),
                                             reads=[hT.b, wb.b], writes=[pa.b])
                                    while rope_pend:
                                        rope_pend.pop(0)()
                                    if blk < 2:
                                        ob = obr.next()
                                        if mi == 0:
                                            qs, t1, t2, pp = qsr.next(), t1r.next(), t2r.next(), psP.next()
                                            qf = qfr.next()
                                            P.op("act", lambda e, qf=qf, pa=pa, W=W: e.activation(out=qf.t[:, 0:W], in_=pa.t[:, 0:W], func=AF.Identity), reads=[pa.b], writes=[qf.b])
                                            P.op("act", lambda e, qs=qs, pa=pa, W=W: e.activation(out=qs.t[:, 0:W], in_=pa.t[:, 0:W], func=AF.Identity), reads=[pa.b], writes=[qs.b])

                                            def rope_tail(qs=qs, qf=qf, t1=t1, t2=t2, pp=pp, ob=ob, cg=cg, sg_=sg_, W=W, blk=blk, j=j, t0=t0):
                                                P.op("pe", lambda e: e.matmul(pp.t[:, 0:W], lhsT=perm_bf.t[:], rhs=qs.t[:, 0:W], start=True, stop=True),
                                                     reads=[perm_bf.b, qs.b], writes=[pp.b])
                                                P.op("dve", lambda e: e.tensor_tensor(out=t1.t[:, 0:W], in0=qf.t[:, 0:W], in1=cg.t[:, 0:W], op=ALU.mult),
                                                     reads=[qf.b, cg.b], writes=[t1.b])
                                                P.op("dve", lambda e: e.tensor_tensor(out=t2.t[:, 0:W], in0=pp.t[:, 0:W], in1=sg_.t[:, 0:W], op=ALU.mult),
                                                     reads=[pp.b, sg_.b], writes=[t2.b])
                                                P.op("pool", lambda e: e.tensor_tensor(out=ob.t[:, 0:W], in0=t1.t[:, 0:W], in1=t2.t[:, 0:W], op=ALU.add),
                                                     reads=[t1.b, t2.b], writes=[ob.b])
                                                P.dma("pool", qk_d[blk, j, :, t0:t0 + W], ob.t[:, 0:W], reads=[ob.b], wadd=[B_qk])
                                            rope_pend.append(rope_tail)
                                            continue
                                        else:
                                            evac_copy(ob.t[:, 0:W], pa.t[:, 0:W], [pa.b], [ob.b])
                                        P.dma("pool", qk_d[blk, j, :, t0:t0 + W], ob.t[:, 0:W], reads=[ob.b], wadd=[B_qk])
                                    elif blk == 3:
                                        if j < 2:
                                            of = ofr.next()
                                            evac_copy(of.t[:, 0:W], pa.t[:, 0:W], [pa.b], [of.b])
                                            P.dma("pool", poolu_d[j, :, t0:t0 + W], of.t[:, 0:W], reads=[of.b], wadd=[B_poolu])
                                        else:
                                            evac_copy(ub.t[:, j - 2, 0:W], pa.t[:, 0:W], [pa.b], [ub.b])
                                            if j == 3:
                                                for tt in range(W // 128):
                                                    pz = psZ.next()
                                                    for c in range(2):
                                                        P.op("pe", lambda e, pz=pz, c=c, tt=tt, ub=ub: e.matmul(pz.t[:, :], lhsT=ub.t[:, c, tt * 128:(tt + 1) * 128], rhs=cs_bf.t[:, c, :], start=(c == 0), stop=(c == 1)),
                                                             reads=[ub.b, cs_bf.b], writes=[pz.b])
                                                    ob = obr.next()
                                                    evac_copy(ob.t[:, :], pz.t[:, :], [pz.b], [ob.b])
                                                    P.dma("pool", z_d[t0 + tt * 128:t0 + (tt + 1) * 128, :], ob.t[:, :], reads=[ob.b], wadd=[B_z])
                                    else:
                                        if j < 2:
                                            af = afr.next()
                                            P.op("dve", lambda e, af=af, pa=pa, W=W: e.tensor_copy(out=af.t[:, 0:W], in_=pa.t[:, 0:W]), reads=[pa.b], writes=[af.b])
                                        else:
                                            sg2, of = sgr.next(), obr.next()
                                            P.op("act", lambda e, sg2=sg2, pa=pa, W=W: e.activation(out=sg2.t[:, 0:W], in_=pa.t[:, 0:W], func=AF.Sigmoid), reads=[pa.b], writes=[sg2.b])
                                            P.op("pool", lambda e, of=of, af=af, sg2=sg2, W=W: e.tensor_tensor(out=of.t[:, 0:W], in0=af.t[:, 0:W], in1=sg2.t[:, 0:W], op=ALU.mult),
                                                 reads=[af.b, sg2.b], writes=[of.b])
                                            P.dma("pool", convz_d[j - 2, :, t0:t0 + W], of.t[:, 0:W], reads=[of.b], wadd=[B_convz])
                            while rope_pend:
                                rope_pend.pop(0)()
                    P.end()
            if "stopA" in dbg:
                break

            P.begin()
            with ExitStack() as es:
                def sb(name, shape, dt):
                    return T(es.enter_context(sbt(name, list(shape), dt)))

                def ps(name):
                    return T(es.enter_context(pst(name, [128, 512], F32)))
                NKR = 8
                kt = [sb("b_kt%d" % i, [128, 4, 128], BF16) for i in range(NKR)]
                vt = [sb("b_vt%d" % i, [128, 8, 65], BF16) for i in range(NKR)]
                kc = sb("b_kc", [128, 4, 256], BF16)
                vc = [sb("b_vc%d" % i, [128, 8, 65], BF16) for i in range(2)]
                for v_ in vt + vc:
                    P.op("pool", lambda e, v_=v_: e.memset(v_.t[:], 1.0), writes=[v_.b])
                qtr = Ring([sb("b_qt%d" % i, [128, 4, 128], BF16) for i in range(3)])
                bst = sb("b_bst", [128, 8, 640], F32)
                btab2 = W_["btab"].rearrange("p h c q -> p h (c q)")
                bt = {}
                off = 0
                for name, _, keys in ATT_VARIANTS:
                    n = len(keys)
                    bt[name] = sb("b_bt_" + name, [128, 8, n * 128], BF16)
                    P.dma("sp", bst.t[:, :, 0:n * 128], btab2[:, :, off * 128:(off + n) * 128], writes=[bst.b])
                    P.op("act", lambda e, n=n, name=name: e.activation(out=bt[name].t[:], in_=bst.t[:, :, 0:n * 128], func=AF.Exp), reads=[bst.b], writes=[bt[name].b])
                    off += n
                Er = Ring([sb("b_E%d" % i, [128, 896], BF16) for i in range(3)])
                Pr = Ring([sb("b_P%d" % i, [128, 896], BF16) for i in range(3)])
                recr = Ring([sb("b_rec%d" % i, [128, 8], F32) for i in range(3)])
                atr = Ring([sb("b_at%d" % i, [128, 4, 128], BF16) for i in range(3)])
                attr_ = Ring([sb("b_att%d" % i, [128, 512], BF16) for i in range(3)])
                psS = Ring([(ps("b_psSa%d" % i), ps("b_psSb%d" % i)) for i in range(2)])
                psO = [T(es.enter_context(pst("b_psO%d" % i, [128, 4, 65], F32))) for i in range(2)]
                psT = T(es.enter_context(pst("b_psT", [128, 512], BF16)))
                P.dma("sp", kc.t[:], qk_d[1, :, :, NTOK:NT].rearrange("j p t -> p j t"), reads=[B_qk], writes=[kc.b])
                for i in range(2):
                    P.dma("act", vc[i].t[:, :, 0:64], v_d[NTOK + i * 128:NTOK + (i + 1) * 128, :].rearrange("p (h d) -> p h d", h=8), reads=[B_v], writes=[vc[i].b])
                state = {"loaded": -1}
                units = []
                tiles = {}

                def tile_setup(qi):
                    if qi < 64:
                        if qi == 0:
                            var, keys = "t0", [0, 1, 2, 3]
                        elif qi == 1:
                            var, keys = "t1", [0, 1, 2, 3]
                        elif qi == 62:
                            var, keys = "t62", [60, 61, 62, 63]
                        elif qi == 63:
                            var, keys = "t63", [60, 61, 62, 63]
                        else:
                            var, keys = "int", [qi - 2, qi - 1, qi, qi + 1, qi + 2]
                        while state["loaded"] < keys[-1]:
                            state["loaded"] += 1
                            u = state["loaded"]
                            P.dma("sp", kt[u % NKR].t[:], qk_d[1, :, :, u * 128:(u + 1) * 128].rearrange("j p t -> p j t"), reads=[B_qk], writes=[kt[u % NKR].b])
                            P.dma("sp", vt[u % NKR].t[:, :, 0:64], v_d[u * 128:(u + 1) * 128, :].rearrange("p (h d) -> p h d", h=8), reads=[B_v], writes=[vt[u % NKR].b])
                        chunks = [(kt[u % NKR], None, vt[u % NKR], None) for u in keys] + [(kc, 0, vc[0], 0), (kc, 1, vc[1], 1)]
                        nwin = len(keys)
                    else:
                        var, nwin = None, 0
                        chunks = [(kc, 0, vc[0], 0), (kc, 1, vc[1], 1)]
                    q0 = qi * 128
                    qt = qtr.next()
                    P.dma("sp", qt.t[:], qk_d[0, :, :, q0:q0 + 128].rearrange("j p t -> p j t"), reads=[B_qk], writes=[qt.b])
                    tiles[qi] = dict(var=var, nwin=nwin, chunks=chunks, q0=q0, qt=qt, at=atr.next(), rec=recr.next(), att=attr_.next())

                def s_stage(qi, h):
                    if h == 0:
                        tile_setup(qi)
                    tl = tiles[qi]
                    chunks, qt, nwin, var = tl["chunks"], tl["qt"], tl["nwin"], tl["var"]
                    n = len(chunks)
                    hc, p0 = h // 2, (h % 2) * 64
                    sa, sb_ = psS.next()
                    for ci, (kT_, kci, _, _) in enumerate(chunks):
                        dst = sa if ci < 4 else sb_
                        lhs = kT_.t[p0:p0 + 64, hc, :] if kci is None else kT_.t[p0:p0 + 64, hc, kci * 128:(kci + 1) * 128]
                        P.op("pe", lambda e, dst=dst, ci=ci, lhs=lhs: e.matmul(dst.t[:, (ci % 4) * 128:(ci % 4 + 1) * 128], lhsT=lhs, rhs=qt.t[p0:p0 + 64, hc, :], start=True, stop=True),
                             reads=[kT_.b, qt.b], writes=[dst.b])
                    E = Er.next()
                    na = min(n, 4)
                    P.op("act", lambda e: e.activation(out=E.t[:, 0:na * 128], in_=sa.t[:, 0:na * 128], func=AF.Exp, scale=0.125), reads=[sa.b], writes=[E.b])
                    if n > 4:
                        P.op("act", lambda e: e.activation(out=E.t[:, 512:n * 128], in_=sb_.t[:, 0:(n - 4) * 128], func=AF.Exp, scale=0.125), reads=[sb_.b], writes=[E.b])
                    Pm = None
                    if nwin > 0:
                        Pm = Pr.next()
                        btv = bt[var]
                        P.op("dve", lambda e: e.tensor_tensor(out=Pm.t[:, 0:nwin * 128], in0=E.t[:, 0:nwin * 128], in1=btv.t[:, h, 0:nwin * 128], op=ALU.mult),
                             reads=[E.b, btv.b], writes=[Pm.b])
                    tl[("EP", h)] = (E, Pm)

                def pv_stage(qi, h):
                    tl = tiles[qi]
                    chunks, nwin, rec, att, at, q0 = tl["chunks"], tl["nwin"], tl["rec"], tl["att"], tl["at"], tl["q0"]
                    n = len(chunks)
                    E, Pm = tl.pop(("EP", h))
                    po = psO[h // 4]
                    h4 = h % 4
                    for ci, (_, _, v_, vci) in enumerate(chunks):
                        src = Pm if ci < nwin else E
                        P.op("pe", lambda e, v_=v_, src=src, ci=ci: e.matmul(po.t[:, h4, :], lhsT=src.t[:, ci * 128:(ci + 1) * 128], rhs=v_.t[:, h, :], start=(ci == 0), stop=(ci == n - 1)),
                             reads=[v_.b, src.b], writes=[po.b])
                    if h4 == 3:
                        hb = h - 3
                        P.op("dve", lambda e: e.reciprocal(out=rec.t[:, hb:hb + 4], in_=po.t[:, :, 64]), reads=[po.b], writes=[rec.b])
                        for hh in range(4):
                            P.op("dve", lambda e, hh=hh: e.tensor_scalar(out=att.t[:, (hb + hh) * 64:(hb + hh + 1) * 64], in0=po.t[:, hh, 0:64], scalar1=rec.t[:, hb + hh:hb + hh + 1], scalar2=0.0, op0=ALU.mult, op1=ALU.add),
                                 reads=[po.b, rec.b], writes=[att.b])
                    if h == 7:
                        for j in range(4):
                            P.op("pe", lambda e, j=j: e.transpose(out=psT.t[:, j * 128:(j + 1) * 128], in_=att.t[:, j * 128:(j + 1) * 128], identity=ident_bf.t[:]),
                                 reads=[att.b, ident_bf.b], writes=[psT.b])
                        P.op("act", lambda e: e.activation(out=at.t[:], in_=psT.t[:], func=AF.Identity), reads=[psT.b], writes=[at.b])
                        P.dma("pool", br_d[0:4, :, q0:q0 + 128].rearrange("j p t -> p j t"), at.t[:], reads=[at.b], wadd=[B_br])
                        del tiles[qi]
                units = [(qi, h) for qi in range(66) for h in range(8)]
                s_stage(*units[0])
                for i, u_ in enumerate(units):
                    if i + 1 < len(units):
                        s_stage(*units[i + 1])
                    pv_stage(*u_)
            P.end()

            P.begin()
            mix_es = ExitStack()
            def gen_pool(es=mix_es):
                def sb(name, shape, dt):
                    return T(es.enter_context(sbt(name, list(shape), dt)))

                def ps(name):
                    return T(es.enter_context(pst(name, [128, 512], F32)))
                Ur = Ring([sb("p_U%d" % i, [128, 528], F32) for i in range(2)])
                P2r = Ring([sb("p_P2%d" % i, [128, 528], F32) for i in range(2)])
                P4r = Ring([sb("p_P4%d" % i, [128, 528], F32) for i in range(2)])
                invr = Ring([sb("p_inv%d" % i, [128, 512], F32) for i in range(2)])
                tmr = Ring([sb("p_tm%d" % i, [128, 512], F32) for i in range(2)])
                dbr = Ring([sb("p_db%d" % i, [128, 512], BF16) for i in range(3)])
                obr = Ring([sb("p_ob%d" % i, [128, 512], BF16) for i in range(2)])
                psr = Ring([ps("p_ps%d" % i) for i in range(1)])
                pool_pend = []
                blocks = [(g * 512, 512, 0 if g == 0 else (2 if g == 15 else 1), g == 0, g == 15) for g in range(16)] + [(NTOK, 256, 3, True, True)]
                for c in range(2):
                    for (t0, W, var, first, last) in blocks:
                        U, A, Bq = Ur.next(), P2r.next(), P4r.next()
                        lo = 0 if not first else 8
                        hi = W + 16 if not last else W + 8
                        if first:
                            P.op("pool", lambda e, U=U: e.memset(U.t[:, 0:8], 0.0), writes=[U.b])
                        if last:
                            P.op("pool", lambda e, U=U, W=W: e.memset(U.t[:, W + 8:W + 16], 0.0), writes=[U.b])
                        P.dma("sp", U.t[:, lo:hi], poolu_d[c, :, t0 - 8 + lo:t0 - 8 + hi], reads=[B_poolu], writes=[U.b])
                        iv = invr.next()
                        P.dma("sp", iv.t[:, 0:W], K["poolinv"][c, :, var, 0:W], writes=[iv.b])
                        n = W + 16
                        P.op("pool", lambda e, U=U, A=A, n=n: e.tensor_tensor(out=A.t[:, 1:n], in0=U.t[:, 0:n - 1], in1=U.t[:, 1:n], op=ALU.add), reads=[U.b], writes=[A.b])
                        P.op("pool", lambda e, A=A, Bq=Bq, n=n: e.tensor_tensor(out=Bq.t[:, 2:n - 1], in0=A.t[:, 1:n - 2], in1=A.t[:, 3:n], op=ALU.add), reads=[A.b], writes=[Bq.b])
                        if c == 1:
                            A2, B2 = P2r.next(), P4r.next()
                            P.op("pool", lambda e, Bq=Bq, A2=A2, n=n: e.tensor_tensor(out=A2.t[:, 4:n - 3], in0=Bq.t[:, 2:n - 5], in1=Bq.t[:, 6:n - 1], op=ALU.add), reads=[Bq.b], writes=[A2.b])
                            P.op("pool", lambda e, A2=A2, B2=B2, n=n: e.tensor_tensor(out=B2.t[:, 8:n - 7], in0=A2.t[:, 4:n - 11], in1=A2.t[:, 12:n - 3], op=ALU.add), reads=[A2.b], writes=[B2.b])
                            lo_t, hi_t = A2, B2
                        else:
                            lo_t, hi_t = A, Bq
                        tm, db = tmr.next(), dbr.next()
                        for half, src in ((0, lo_t), (1, hi_t)):
                            pp = slice(half * 64, (half + 1) * 64)
                            P.op("dve", lambda e, tm=tm, src=src, iv=iv, pp=pp, W=W: e.tensor_tensor(out=tm.t[pp, 0:W], in0=src.t[pp, 8:8 + W], in1=iv.t[pp, 0:W], op=ALU.mult),
                                 reads=[src.b, iv.b], writes=[tm.b])
                        P.op("dve", lambda e, tm=tm, db=db, U=U, W=W: e.tensor_tensor(out=db.t[:, 0:W], in0=tm.t[:, 0:W], in1=U.t[:, 8:8 + W], op=ALU.subtract),
                             reads=[tm.b, U.b], writes=[db.b])
                        def pool_tail(db=db, c=c, W=W, t0=t0):
                            pp_ = psr.next()
                            P.op("pe", lambda e: e.matmul(pp_.t[:, 0:W], lhsT=wpool_bf.t[:, c, :], rhs=db.t[:, 0:W], start=True, stop=True),
                                 reads=[wpool_bf.b, db.b], writes=[pp_.b])
                            ob = obr.next()
                            P.op("act", lambda e: e.activation(out=ob.t[:, 0:W], in_=pp_.t[:, 0:W], func=AF.Identity, scale=vec_t.t[:, 64 + c:65 + c]),
                                 reads=[pp_.b, vec_t.b], writes=[ob.b])
                            P.dma("act", br_d[4 + c, :, t0:t0 + W], ob.t[:, 0:W], reads=[ob.b], wadd=[B_br])
                        while pool_pend:
                            pool_pend.pop(0)()
                        pool_pend.append(pool_tail)
                        yield
                while pool_pend:
                    pool_pend.pop(0)()


            def gen_conv(es=mix_es):
                def sb(name, shape, dt):
                    return T(es.enter_context(sbt(name, list(shape), dt)))

                def ps(name):
                    return T(es.enter_context(pst(name, [128, 512], F32)))
                dg = sb("c_dg", [128, 62, 128], BF16)
                for cj in range(62):
                    P.op("dve", lambda e, cj=cj: e.tensor_scalar(out=dg.t[:, cj, :], in0=ident_bf.t[:], scalar1=vec_t.t[:, 72 + cj:73 + cj], scalar2=0.0, op0=ALU.mult, op1=ALU.add),
                         reads=[ident_bf.b, vec_t.b], writes=[dg.b])
                Zr_ = [Ring([sb("c_Z%d_%d" % (c, i), [128, 542], BF16) for i in range(2)]) for c in range(2)]
                accr = [Ring([sb("c_acc%d_%d" % (c, i), [128, 512], F32) for i in range(3)]) for c in range(2)]
                sqr = Ring([sb("c_sq%d" % i, [128, 512], F32) for i in range(2)])
                mr = Ring([sb("c_m%d" % i, [128, 512], F32) for i in range(2)])
                m2r = Ring([sb("c_m2%d" % i, [128, 512], F32) for i in range(2)])
                rsr = Ring([sb("c_rs%d" % i, [128, 512], F32) for i in range(2)])
                xcr = Ring([sb("c_xc%d" % i, [128, 512], F32) for i in range(2)])
                obr = Ring([sb("c_ob%d" % i, [128, 512], BF16) for i in range(2)])
                psC = Ring([ps("c_psC%d" % i) for i in range(2)])
                ps1 = Ring([ps("c_ps1%d" % i) for i in range(1)])
                ps2 = Ring([ps("c_ps2%d" % i) for i in range(1)])
                blocks = [(g * 512, 512, g == 0, g == 15) for g in range(16)] + [(NTOK, 256, True, True)]
                conv_pend = []
                for (t0, W, first, last) in blocks:
                    accs = []
                    for c in range(2):
                        Z = Zr_[c].next()
                        lo = 15 if first else 0
                        hi = W + 15 if last else W + 30
                        if first:
                            P.op("pool", lambda e, Z=Z: e.memset(Z.t[:, 0:15], 0.0), writes=[Z.b])
                        if last:
                            P.op("pool", lambda e, Z=Z, W=W: e.memset(Z.t[:, W + 15:W + 30], 0.0), writes=[Z.b])
                        P.dma("sp", Z.t[:, lo:hi], convz_d[c, :, t0 - 15 + lo:t0 - 15 + hi], reads=[B_convz], writes=[Z.b])
                        pc = psC.next()
                        for j in range(31):
                            P.op("pe", lambda e, pc=pc, Z=Z, c=c, j=j, W=W: e.matmul(pc.t[:, 0:W], lhsT=dg.t[:, c * 31 + j, :], rhs=Z.t[:, j:j + W], start=(j == 0), stop=(j == 30)),
                                 reads=[dg.b, Z.b], writes=[pc.b])
                        acc = accr[c].next()
                        P.op("act", lambda e, acc=acc, pc=pc, c=c, W=W: e.activation(out=acc.t[:, 0:W], in_=pc.t[:, 0:W], func=AF.Identity, bias=vec_t.t[:, 66 + c:67 + c], scale=1.0),
                             reads=[pc.b, vec_t.b], writes=[acc.b])
                        accs.append(acc)
                    def conv_tail(accs=accs, W=W, t0=t0):
                        p1, p2 = ps1.next(), ps2.next()
                        for c in range(2):
                            sq = sqr.next()
                            P.op("act", lambda e, sq=sq, a=accs[c], W=W: e.activation(out=sq.t[:, 0:W], in_=a.t[:, 0:W], func=AF.Square), reads=[accs[c].b], writes=[sq.b])
                            P.op("pe", lambda e, p1=p1, a=accs[c], c=c, W=W: e.matmul(p1.t[:, 0:W], lhsT=ones_f.t[:], rhs=a.t[:, 0:W], start=(c == 0), stop=(c == 1)), reads=[ones_f.b, accs[c].b], writes=[p1.b])
                            P.op("pe", lambda e, p2=p2, sq=sq, c=c, W=W: e.matmul(p2.t[:, 0:W], lhsT=ones_f.t[:], rhs=sq.t[:, 0:W], start=(c == 0), stop=(c == 1)), reads=[ones_f.b, sq.b], writes=[p2.b])
                        m, m2, rs = mr.next(), m2r.next(), rsr.next()
                        P.op("dve", lambda e, m=m, p1=p1, W=W: e.tensor_scalar(out=m.t[:, 0:W], in0=p1.t[:, 0:W], scalar1=1.0 / 256.0, scalar2=0.0, op0=ALU.mult, op1=ALU.add), reads=[p1.b], writes=[m.b])
                        P.op("dve", lambda e, m=m, m2=m2, W=W: e.tensor_tensor(out=m2.t[:, 0:W], in0=m.t[:, 0:W], in1=m.t[:, 0:W], op=ALU.mult), reads=[m.b], writes=[m2.b])
                        P.op("dve", lambda e, rs=rs, p2=p2, m2=m2, W=W: e.scalar_tensor_tensor(out=rs.t[:, 0:W], in0=p2.t[:, 0:W], scalar=1.0 / 256.0, in1=m2.t[:, 0:W], op0=ALU.mult, op1=ALU.subtract),
                             reads=[p2.b, m2.b], writes=[rs.b])
                        P.op("act", lambda e, rs=rs, W=W: e.activation(out=rs.t[:, 0:W], in_=rs.t[:, 0:W], func=AF.Ln, bias=eps_t.t[:, 0:1], scale=1.0), reads=[rs.b, eps_t.b], writes=[rs.b])
                        P.op("act", lambda e, rs=rs, W=W: e.activation(out=rs.t[:, 0:W], in_=rs.t[:, 0:W], func=AF.Exp, scale=-0.5), reads=[rs.b], writes=[rs.b])
                        for c in range(2):
                            xc, ob = xcr.next(), obr.next()
                            eng = "dve" if c == 0 else "pool"
                            P.op(eng, lambda e, xc=xc, a=accs[c], m=m, W=W: e.tensor_tensor(out=xc.t[:, 0:W], in0=a.t[:, 0:W], in1=m.t[:, 0:W], op=ALU.subtract), reads=[accs[c].b, m.b], writes=[xc.b])
                            P.op(eng, lambda e, xc=xc, rs=rs, W=W: e.tensor_tensor(out=xc.t[:, 0:W], in0=xc.t[:, 0:W], in1=rs.t[:, 0:W], op=ALU.mult), reads=[xc.b, rs.b], writes=[xc.b])
                            P.op("act", lambda e, xc=xc, ob=ob, c=c, W=W: e.activation(out=ob.t[:, 0:W], in_=xc.t[:, 0:W], func=AF.Silu, scale=vec_t.t[:, 68 + c:69 + c], bias=vec_t.t[:, 70 + c:71 + c]),
                                 reads=[xc.b, vec_t.b], writes=[ob.b])
                            P.dma("act", br_d[8 + c, :, t0:t0 + W], ob.t[:, 0:W], reads=[ob.b], wadd=[B_br])
                    while conv_pend:
                        conv_pend.pop(0)()
                    conv_pend.append(conv_tail)
                    yield
                while conv_pend:
                    conv_pend.pop(0)()


            def gen_four(es=mix_es):
                def sb(name, shape, dt):
                    return T(es.enter_context(sbt(name, list(shape), dt)))

                def ps(name):
                    return T(es.enter_context(pst(name, [128, 512], F32)))
                zin = Ring([sb("f_zin%d" % i, [128, 8, 256], BF16) for i in range(2)])
                zso = Ring([sb("f_zso%d" % i, [128, 2048], BF16) for i in range(2)])
                zs2 = zs_d.rearrange("p n d -> p (n d)")
                psF = Ring([ps("f_ps%d" % i) for i in range(2)])
                zv = z_d[0:NTOK, :].rearrange("(a b) f -> a b f", b=128)
                for it in range(16):
                    zi_ = zin.next()
                    P.dma_multi([("sp", zi_.t[0:64, :, :], zv[:, it * 8:(it + 1) * 8, 0:256]),
                                 ("sp", zi_.t[64:128, :, :], zv[:, it * 8:(it + 1) * 8, 256:512])], reads=[B_z], writes=[zi_.b])
                    zo = zso.next()
                    for s in range(4):
                        pf = psF.next()
                        P.op("pe", lambda e, pf=pf, zi_=zi_, s=s: e.matmul(pf.t[:, :], lhsT=w64_bf.t[:], rhs=zi_.t[:, 2 * s:2 * s + 2, :], start=True, stop=True),
                             reads=[w64_bf.b, zi_.b], writes=[pf.b])
                        evac_copy(zo.t[:, s * 512:(s + 1) * 512], pf.t[:, :], [pf.b], [zo.b])
                    P.dma("pool", zs2[:, it * 2048:(it + 1) * 2048], zo.t[:], reads=[zo.b], wadd=[B_zs])
                    yield
                FT = sb("f_FT", [128, 2, 128, 64], BF16)
                zr_r = Ring([sb("f_zr%d" % i, [128, 8, 256], BF16) for i in range(2)])
                zi_r = Ring([sb("f_zi%d" % i, [128, 8, 256], BF16) for i in range(2)])
                cf_r = Ring([sb("f_cf%d" % i, [128, 8, 128], BF16) for i in range(2)])
                sf_r = Ring([sb("f_sf%d" % i, [128, 8, 128], BF16) for i in range(2)])
                FTv = FT.t
                for kb in range(8):
                    zr, zi2, cfb, sfb = zr_r.next(), zi_r.next(), cf_r.next(), sf_r.next()
                    P.dma("sp", zr.t[:], zs_d[kb * 8:(kb + 1) * 8, :, :].rearrange("k n d -> n k d"), reads=[B_zs], writes=[zr.b])
                    P.dma("sp", zi2.t[:], zs_d[64 + kb * 8:64 + (kb + 1) * 8, :, :].rearrange("k n d -> n k d"), reads=[B_zs], writes=[zi2.b])
                    P.dma("sp", cfb.t[:], cfb_d[:, kb * 1024:(kb + 1) * 1024].rearrange("p (k m) -> p k m", k=8), reads=[B_cfb], writes=[cfb.b])
                    P.dma("sp", sfb.t[:], sfb_d[:, kb * 1024:(kb + 1) * 1024].rearrange("p (k m) -> p k m", k=8), reads=[B_cfb], writes=[sfb.b])
                    for dc in range(2):
                        for q4 in range(2):
                            pf = psF.next()
                            for a in range(4):
                                k1l = q4 * 4 + a
                                P.op("pe", lambda e, pf=pf, zr=zr, cfb=cfb, k1l=k1l, a=a, dc=dc: e.matmul(pf.t[:, a * 128:(a + 1) * 128], lhsT=zr.t[:, k1l, dc * 128:(dc + 1) * 128], rhs=cfb.t[:, k1l, :], start=True, stop=False),
                                     reads=[zr.b, cfb.b], writes=[pf.b])
                                P.op("pe", lambda e, pf=pf, zi2=zi2, sfb=sfb, k1l=k1l, a=a, dc=dc: e.matmul(pf.t[:, a * 128:(a + 1) * 128], lhsT=zi2.t[:, k1l, dc * 128:(dc + 1) * 128], rhs=sfb.t[:, k1l, :], start=False, stop=True),
                                     reads=[zi2.b, sfb.b], writes=[pf.b])
                            for a in range(4):
                                k1 = kb * 8 + q4 * 4 + a
                                eng = "act" if (dc + q4) % 2 == 0 else "dve"
                                if eng == "act":
                                    P.op("act", lambda e, pf=pf, a=a, dc=dc, k1=k1: e.activation(out=FTv[:, dc, :, k1], in_=pf.t[:, a * 128:(a + 1) * 128], func=AF.Identity, scale=float(1.0 / np.sqrt(8192.0 * 64.0))),
                                         reads=[pf.b], writes=[FT.b])
                                else:
                                    P.op("dve", lambda e, pf=pf, a=a, dc=dc, k1=k1: e.tensor_scalar(out=FTv[:, dc, :, k1], in0=pf.t[:, a * 128:(a + 1) * 128], scalar1=float(1.0 / np.sqrt(8192.0 * 64.0)), scalar2=0.0, op0=ALU.mult, op1=ALU.add),
                                         reads=[pf.b], writes=[FT.b])
                    yield
                for dc in range(2):
                    P.dma(("sp", "act")[dc], br_d[6 + dc, :, 0:NTOK].rearrange("p (a b) -> p a b", b=64), FT.t[:, dc, :, :], reads=[FT.b], wadd=[B_br])
                zc = sb("f_zc", [128, 2, 512], BF16)
                obc = sb("f_obc", [128, 2, 256], BF16)
                P.dma("sp", zc.t[:], z_d[NTOK:NT, :].rearrange("(c p) f -> p c f", p=128), reads=[B_z], writes=[zc.b])
                for dc in range(2):
                    pf = psF.next()
                    i = 0
                    for tt in range(2):
                        for (o, tab) in ((0, c256_bf), (256, s256_bf)):
                            P.op("pe", lambda e, pf=pf, tt=tt, o=o, tab=tab, dc=dc, i=i: e.matmul(pf.t[:, 0:256], lhsT=zc.t[:, tt, o + dc * 128:o + (dc + 1) * 128], rhs=tab.t[:, tt, :], start=(i == 0), stop=(i == 3)),
                                 reads=[zc.b, tab.b], writes=[pf.b])
                            i += 1
                    P.op("act", lambda e, pf=pf, dc=dc: e.activation(out=obc.t[:, dc, :], in_=pf.t[:, 0:256], func=AF.Identity, scale=1.0 / 128.0), reads=[pf.b], writes=[obc.b])
                P.dma("sp", br_d[6:8, :, NTOK:NT].rearrange("c p t -> p c t"), obc.t[:], reads=[obc.b], wadd=[B_br])
            gens = [gen_conv(), gen_pool(), gen_four()]
            while gens:
                for g_ in list(gens):
                    try:
                        next(g_)
                    except StopIteration:
                        gens.remove(g_)
            P.end()
            mix_es.close()
            if "stopB" in dbg:
                break

            for qi in range(2):
                gl = [(g, GROUPS[g]) for g in (range(0, 8) if qi == 0 else range(8, 17))]
                tbase = gl[0][1][0]

                def lofs(t0, tbase=tbase):
                    return (t0 - tbase) if t0 < NTOK else 4096
                with ExitStack() as hs:
                    hts = ExitStack()
                    hT = T(hts.enter_context(sbt("c_hT", [128, 8, 4352], BF16)))
                    P.begin()
                    parts = [("sp", hT.t[:, 0:4, 0:4096], hT_d[0:4, :, tbase:tbase + 4096].rearrange("k p t -> p k t")),
                             ("act", hT.t[:, 4:8, 0:4096], hT_d[4:8, :, tbase:tbase + 4096].rearrange("k p t -> p k t"))]
                    if qi == 1:
                        parts.append(("sp", hT.t[:, :, 4096:4352], hT_d[:, :, NTOK:NT].rearrange("k p t -> p k t")))
                    P.dma_multi(parts, reads=[B_hT], writes=[hT.b])
                    with ExitStack() as es:
                        def sb(name, shape, dt):
                            return T(es.enter_context(sbt(name, list(shape), dt)))

                        def ps(name):
                            return T(es.enter_context(pst(name, [128, 512], F32)))
                        gst = Ring([sb("c2_gst%d" % i, [128, 8, 512], F32) for i in range(2)])
                        gbf = Ring([sb("c2_gbf%d" % i, [128, 8, 512], BF16) for i in range(2)])
                        bst = Ring([sb("c2_bst%d" % i, [128, 10, 128], F32) for i in range(2)])
                        bbf = Ring([sb("c2_bbf%d" % i, [128, 10, 128], BF16) for i in range(2)])
                        brg = Ring([sb("c2_brg%d" % i, [128, 10, 512], BF16) for i in range(4)])
                        sgr = Ring([sb("c2_sg%d" % i, [128, 512], F32) for i in range(2)])
                        tr = Ring([sb("c2_t%d" % i, [128, 512], F32) for i in range(2)])
                        mar = Ring([sb("c2_ma%d" % i, [128, 512], F32) for i in range(3)])
                        mor = Ring([sb("c2_mo%d" % i, [128, 512], BF16) for i in range(3)])
                        psG = Ring([ps("c2_psG%d" % i) for i in range(2)])
                        psY = Ring([ps("c2_psY%d" % i) for i in range(2)])
                        KB = [(0, 4), (4, 2), (6, 2), (8, 2)]
                        def c2_load(d):
                            gs_, gb = gst.next(), gbf.next()
                            P.dma_multi([(("sp", "act")[b % 2], gs_.t[:, :, b * 128:(b + 1) * 128], W_["win"][:, GATE_OFF + b * 1024 + d * 128:GATE_OFF + b * 1024 + (d + 1) * 128].rearrange("(k p) c -> p k c", p=128)) for b in range(4)], writes=[gs_.b])
                            bs_, bb = bst.next(), bbf.next()
                            P.dma("sp", bs_.t[:], W_["wbr"][:, d * 128:(d + 1) * 128].rearrange("(k p) c -> p k c", p=128), writes=[bs_.b])

                            def cast():
                                P.op("dve", lambda e: e.tensor_copy(out=gb.t[:, 0:4, :], in_=gs_.t[:, 0:4, :]), reads=[gs_.b], writes=[gb.b])
                                P.op("act", lambda e: e.activation(out=gb.t[:, 4:8, :], in_=gs_.t[:, 4:8, :], func=AF.Identity), reads=[gs_.b], writes=[gb.b])
                                P.op("act", lambda e: e.activation(out=bb.t[:], in_=bs_.t[:], func=AF.Identity), reads=[bs_.b], writes=[bb.b])
                            return gb, bb, cast
                        nxt = c2_load(0)
                        nxt[2]()
                        for d in range(8):
                            gb, bb = nxt[0], nxt[1]
                            nxt = None
                            for gidx, (g, (t0, W, mi)) in enumerate(gl):
                                if d < 7 and gidx == 0:
                                    nxt = c2_load(d + 1)
                                if d < 7 and gidx == 2:
                                    nxt[2]()
                                lo = lofs(t0)
                                bg = brg.next()
                                P.dma_multi([("sp", bg.t[:, 0:5, 0:W], br_d[0:5, :, t0:t0 + W].rearrange("j p t -> p j t")),
                                             ("sp", bg.t[:, 5:10, 0:W], br_d[5:10, :, t0:t0 + W].rearrange("j p t -> p j t"))], reads=[B_br], writes=[bg.b])
                                ma = None
                                for b in range(4):
                                    pg, py = psG.next(), psY.next()
                                    for k in range(8):
                                        P.op("pe", lambda e, pg=pg, gb=gb, b=b, k=k, lo=lo, W=W: e.matmul(pg.t[:, 0:W], lhsT=gb.t[:, k, b * 128:(b + 1) * 128], rhs=hT.t[:, k, lo:lo + W], start=(k == 0), stop=(k == 7)),
                                             reads=[gb.b, hT.b], writes=[pg.b])
                                    k0, nk = KB[b]
                                    for k in range(nk):
                                        P.op("pe", lambda e, py=py, bb=bb, bg=bg, k0=k0, k=k, nk=nk, W=W: e.matmul(py.t[:, 0:W], lhsT=bb.t[:, k0 + k, :], rhs=bg.t[:, k0 + k, 0:W], start=(k == 0), stop=(k == nk - 1)),
                                             reads=[bb.b, bg.b], writes=[py.b])
                                    sg_ = sgr.next()
                                    P.op("act", lambda e, sg_=sg_, pg=pg, W=W: e.activation(out=sg_.t[:, 0:W], in_=pg.t[:, 0:W], func=AF.Sigmoid), reads=[pg.b], writes=[sg_.b])
                                    if b == 0:
                                        ma = mar.next()
                                        P.op("dve", lambda e, ma=ma, py=py, sg_=sg_, W=W: e.tensor_tensor(out=ma.t[:, 0:W], in0=py.t[:, 0:W], in1=sg_.t[:, 0:W], op=ALU.mult), reads=[py.b, sg_.b], writes=[ma.b])
                                    else:
                                        t_ = tr.next()
                                        P.op("dve", lambda e, t_=t_, py=py, sg_=sg_, W=W: e.tensor_tensor(out=t_.t[:, 0:W], in0=py.t[:, 0:W], in1=sg_.t[:, 0:W], op=ALU.mult), reads=[py.b, sg_.b], writes=[t_.b])
                                        if b < 3:
                                            nma = mar.next()
                                            P.op("pool", lambda e, nma=nma, ma=ma, t_=t_, W=W: e.tensor_tensor(out=nma.t[:, 0:W], in0=ma.t[:, 0:W], in1=t_.t[:, 0:W], op=ALU.add), reads=[ma.b, t_.b], writes=[nma.b])
                                            ma = nma
                                        else:
                                            mo = mor.next()
                                            P.op("pool", lambda e, ma=ma, t_=t_, mo=mo, W=W: e.tensor_tensor(out=mo.t[:, 0:W], in0=ma.t[:, 0:W], in1=t_.t[:, 0:W], op=ALU.add), reads=[ma.b, t_.b], writes=[mo.b])
                                            P.dma("pool", m_d[d, :, t0:t0 + W], mo.t[:, 0:W], reads=[mo.b], wadd=[B_m])
                    P.end()
                    hts.close()
            P.begin()
            with ExitStack() as es:
                def sb(name, shape, dt):
                    return T(es.enter_context(sbt(name, list(shape), dt)))

                def ps(name):
                    return T(es.enter_context(pst(name, [128, 512], F32)))
                wst = Ring([sb("c3_wst%d" % i, [128, 8, 512], F32) for i in range(2)])
                wo = sb("c3_wo", [128, 8, 1024], BF16)
                mgr = Ring([sb("c3_mg%d" % i, [128, 8, 512], BF16) for i in range(2)])
                gl = [(g, GROUPS[g]) for g in range(17)]
                xgr = Ring([sb("c3_xg%d" % i, [128, 8, 512], F32) for i in range(3)])
                psO = Ring([ps("c3_ps%d" % i) for i in range(3)])
                n_sq = Ring([sb("c3_sq%d" % i, [128, 8, 512], BF16) for i in range(2)])
                n_rs = Ring([sb("c3_rs%d" % i, [128, 512], F32) for i in range(2)])
                n_tm = Ring([sb("c3_tm%d" % i, [128, 512], F32) for i in range(4)])
                n_ps = Ring([ps("c3_psn%d" % i) for i in range(2)])
                h2r = Ring([sb("c3_h2g%d" % i, [128, 8, 512], BF16) for i in range(2)])
                npend = []

                def norm2_tile(x_, g, W, mi):
                    s_, r_, p_ = n_sq.next(), n_rs.next(), n_ps.next()
                    P.op("act", lambda e: e.activation(out=s_.t[:, :, 0:W], in_=x_.t[:, :, 0:W], func=AF.Square), reads=[x_.b], writes=[s_.b])
                    for k in range(8):
                        P.op("pe", lambda e, k=k: e.matmul(p_.t[:, 0:W], lhsT=ones_bf.t[:], rhs=s_.t[:, k, 0:W], start=(k == 0), stop=(k == 7)),
                             reads=[ones_bf.b, s_.b], writes=[p_.b])
                    P.op("act", lambda e: e.activation(out=r_.t[:, 0:W], in_=p_.t[:, 0:W], func=AF.Ln, bias=eps_t.t[:, 0:1], scale=1.0 / D),
                         reads=[p_.b, eps_t.b], writes=[r_.b])
                    P.op("act", lambda e: e.activation(out=r_.t[:, 0:W], in_=r_.t[:, 0:W], func=AF.Exp, scale=-0.5), reads=[r_.b], writes=[r_.b])
                    h2g = h2r.next()
                    for k in range(8):
                        t_ = n_tm.next()
                        P.op("dve", lambda e, k=k, t_=t_: e.scalar_tensor_tensor(
                            out=t_.t[:, 0:W], in0=x_.t[:, k, 0:W], scalar=gsf2(mi, k), in1=r_.t[:, 0:W], op0=ALU.mult, op1=ALU.mult),
                            reads=[x_.b, r_.b], writes=[t_.b])
                        P.op("act", lambda e, k=k, t_=t_: e.activation(out=h2g.t[:, k, 0:W], in_=t_.t[:, 0:W], func=AF.Identity, bias=shf2(mi, k), scale=1.0),
                             reads=[t_.b], writes=[h2g.b])
                    t0_ = GROUPS[g][0]
                    P.dma("pool", h2_d[:, :, t0_:t0_ + W].rearrange("k p t -> p k t"), h2g.t[:, :, 0:W], reads=[h2g.b], wadd=[B_h2])
                for hcol in range(2):
                    load_cast_w(W_["wout"], 0, hcol * 512, 8, 512, wst, wo, 0, hcol * 512)
                for g, (t0, W, mi) in gl:
                    mg = mgr.next()
                    P.dma_multi([("sp", mg.t[:, 0:4, 0:W], m_d[0:4, :, t0:t0 + W].rearrange("k p t -> p k t")),
                                 ("sp", mg.t[:, 4:8, 0:W], m_d[4:8, :, t0:t0 + W].rearrange("k p t -> p k t"))], reads=[B_m], writes=[mg.b])
                    xg = xgr.next()
                    P.dma_multi([("sp", xg.t[:, 0:4, 0:W], xs[0:4, :, t0:t0 + W].rearrange("k p t -> p k t")),
                                 ("sp", xg.t[:, 4:8, 0:W], xs[4:8, :, t0:t0 + W].rearrange("k p t -> p k t"))], reads=[B_xsg[g]], writes=[xg.b])
                    for d in range(8):
                        po = psO.next()
                        for k in range(8):
                            P.op("pe", lambda e, po=po, d=d, k=k, mg=mg, W=W: e.matmul(po.t[:, 0:W], lhsT=wo.t[:, k, d * 128:(d + 1) * 128], rhs=mg.t[:, k, 0:W], start=(k == 0), stop=(k == 7)),
                                 reads=[wo.b, mg.b], writes=[po.b])
                        if d == 3:
                            while npend:
                                npend.pop(0)()
                        P.op("dve", lambda e, po=po, xg=xg, d=d, W=W, mi=mi: e.scalar_tensor_tensor(out=xg.t[:, d, 0:W], in0=po.t[:, 0:W], scalar=mod_t.t[:, mi, 16 + d:17 + d], in1=xg.t[:, d, 0:W], op0=ALU.mult, op1=ALU.add),
                             reads=[po.b, xg.b, mod_t.b], writes=[xg.b])
                    P.dma("pool", xs[:, :, t0:t0 + W].rearrange("k p t -> p k t"), xg.t[:, :, 0:W], reads=[xg.b], writes=[B_xsg[g]])
                    npend.append(lambda xg=xg, g=g, W=W, mi=mi: norm2_tile(xg, g, W, mi))
                while npend:
                    npend.pop(0)()
            P.end()

            P.begin()
            with ExitStack() as es:
                def sb(name, shape, dt):
                    return T(es.enter_context(sbt(name, list(shape), dt)))

                def ps(name):
                    return T(es.enter_context(pst(name, [128, 512], F32)))
                wst = Ring([sb("d_wst%d" % i, [128, 8, 512], F32) for i in range(2)])
                w1r = [sb("d_w1b%d" % i, [128, 8, 1024], BF16) for i in range(2)]
                w2r = [sb("d_w2b%d" % i, [128, 8, 1024], BF16) for i in range(2)]
                hid = Ring([sb("d_hid%d" % i, [128, 8, 512], BF16) for i in range(2)])
                rl = Ring([sb("d_rl%d" % i, [128, 512], BF16) for i in range(3)])
                xgr = Ring([sb("d_xg%d" % i, [128, 8, 512], F32) for i in range(2)])
                h2l = Ring([sb("d_h2%d" % i, [128, 8, 512], BF16) for i in range(2)])
                gl = [(g, GROUPS[g]) for g in range(17)]
                psH = Ring([ps("d_psH%d" % i) for i in range(3)])
                psO = Ring([ps("d_psO%d" % i) for i in range(3)])

                def loads_for(p):
                    a1, a2 = w1r[p % 2], w2r[p % 2]
                    return [lambda: load_w(W_["wff1"], 0, p * 1024, 8, 512, wst, a1, 0, 0),
                            lambda: load_w(W_["wff1"], 0, p * 1024 + 512, 8, 512, wst, a1, 0, 512),
                            lambda: load_w(W_["wff2"], p * 1024, 0, 8, 512, wst, a2, 0, 0),
                            lambda: load_w(W_["wff2"], p * 1024, 512, 8, 512, wst, a2, 0, 512)]
                ld = loads_for(0)
                for i0 in (0, 2):
                    ca, cb = ld[i0](), ld[i0 + 1]()
                    ca()
                    cb()
                for p in range(4):
                    w1b, w2b = w1r[p % 2], w2r[p % 2]
                    nxt = loads_for(p + 1) if p < 3 else None
                    pend = []
                    for gi, (g, (t0, W, mi)) in enumerate(gl):
                        if nxt is not None and gi < 3:
                            for cfn in pend:
                                cfn()
                            pend = [nxt[2 * gi](), nxt[2 * gi + 1]()] if gi < 2 else []
                        h2 = h2l.next()
                        P.dma_multi([("sp", h2.t[:, 0:4, 0:W], h2_d[0:4, :, t0:t0 + W].rearrange("k p t -> p k t")),
                                     ("sp", h2.t[:, 4:8, 0:W], h2_d[4:8, :, t0:t0 + W].rearrange("k p t -> p k t"))], reads=[B_h2], writes=[h2.b])
                        hd = hid.next()
                        for hcn in range(8):
                            ph = psH.next()
                            for k in range(8):
                                P.op("pe", lambda e, ph=ph, hcn=hcn, k=k, h2=h2, W=W, w1b=w1b: e.matmul(ph.t[:, 0:W], lhsT=w1b.t[:, k, hcn * 128:(hcn + 1) * 128], rhs=h2.t[:, k, 0:W], start=(k == 0), stop=(k == 7)),
                                     reads=[w1b.b, h2.b], writes=[ph.b])
                            r_ = rl.next()
                            P.op("act", lambda e, r_=r_, ph=ph, W=W: e.activation(out=r_.t[:, 0:W], in_=ph.t[:, 0:W], func=AF.Relu), reads=[ph.b], writes=[r_.b])
                            P.op("pool", lambda e, r_=r_, hd=hd, hcn=hcn, W=W: e.tensor_tensor(out=hd.t[:, hcn, 0:W], in0=r_.t[:, 0:W], in1=r_.t[:, 0:W], op=ALU.mult), reads=[r_.b], writes=[hd.b])
                        xg = xgr.next()
                        P.dma_multi([("sp", xg.t[:, 0:4, 0:W], xs[0:4, :, t0:t0 + W].rearrange("k p t -> p k t")),
                                     ("sp", xg.t[:, 4:8, 0:W], xs[4:8, :, t0:t0 + W].rearrange("k p t -> p k t"))], reads=[B_xsg[g]], writes=[xg.b])
                        for d in range(8):
                            po = psO.next()
                            for k in range(8):
                                P.op("pe", lambda e, po=po, d=d, k=k, hd=hd, W=W, w2b=w2b: e.matmul(po.t[:, 0:W], lhsT=w2b.t[:, k, d * 128:(d + 1) * 128], rhs=hd.t[:, k, 0:W], start=(k == 0), stop=(k == 7)),
                                     reads=[w2b.b, hd.b], writes=[po.b])
                            P.op("dve", lambda e, po=po, xg=xg, d=d, W=W, mi=mi: e.scalar_tensor_tensor(out=xg.t[:, d, 0:W], in0=po.t[:, 0:W], scalar=mod_t.t[:, mi, 40 + d:41 + d], in1=xg.t[:, d, 0:W], op0=ALU.mult, op1=ALU.add),
                                 reads=[po.b, xg.b, mod_t.b], writes=[xg.b])
                        P.dma("pool", xs[:, :, t0:t0 + W].rearrange("k p t -> p k t"), xg.t[:, :, 0:W], reads=[xg.b], writes=[B_xsg[g]])
            P.end()

        if not any(s in dbg for s in ("stopS", "stopM", "stopA", "stopB")):
            P.begin()
            with ExitStack() as es:
                fo = Ring([T(es.enter_context(sbt("e_fo%d" % i, [128, 8, 512], F32))) for i in range(2)])
                cur = {}

                def fout(g, k):
                    if k == 0:
                        cur["t"] = fo.next()
                    return cur["t"].t[:, k, :], cur["t"].b

                def fstore(g):
                    t0 = GROUPS[g][0]
                    P.dma("pool", outT[:, :, t0:t0 + 512].rearrange("k p t -> p k t"), cur["t"].t[:], reads=[cur["t"].b], wadd=[B_out])
                norm_groups([(g, GROUPS[g]) for g in range(16)], lambda mi, k: gfin_t.t[:, k:k + 1], lambda mi, k: zero_t.t[:, 0:1], fout, fstore)
            P.end()
    P.close()
    return nc, P


_CACHE = {}


def prep_inputs(inp, NL=4, batches=(0, 1, 2, 3, 0, 1, 2, 3)):
    shared = dict(_consts())
    shared["gfin"] = np.ascontiguousarray(inp["g_final"].reshape(8, 128).T)
    for l in range(NL):
        shared["wmod%d" % l] = np.ascontiguousarray(inp["w_mod"][l])
        shared["vec%d" % l] = _vec(inp, l)
        shared["win%d" % l] = np.ascontiguousarray(inp["w_in"][l])
        shared["wbr%d" % l] = np.ascontiguousarray(np.concatenate([inp["w_br_attn"][l], inp["w_br_pool"][l], inp["w_br_fourier"][l], inp["w_br_conv"][l]], 0))
        shared["wout%d" % l] = np.ascontiguousarray(inp["w_out"][l])
        shared["wff1%d" % l] = np.ascontiguousarray(inp["w_ff1"][l])
        shared["wff2%d" % l] = np.ascontiguousarray(inp["w_ff2"][l])
        shared["wpool%d" % l] = _wpool_bd(inp["w_pool"][l])
        shared["btab%d" % l] = _bias_tables(inp["rpb"][l])
    maps = []
    for b in batches:
        m = dict(shared)
        xt = np.concatenate([inp["x"][b], inp["ctx"][b]], 0)
        m["xT"] = np.ascontiguousarray(xt.T.reshape(8, 128, NT))
        ct = np.stack([inp["c"][b], inp["c_ctx"]], -1)
        m["cT"] = np.ascontiguousarray(ct.reshape(8, 128, 2).transpose(1, 0, 2))
        maps.append(m)
    return maps


def kernel(**inputs):
    inp = {k: np.asarray(v) for k, v in inputs.items()}
    if "nc" not in _CACHE:
        _CACHE["nc"] = build(4)[0]
    nc = _CACHE["nc"]
    maps = prep_inputs(inp, 4)
    res = run_bass_kernel_spmd(nc, maps, core_ids=list(range(8)))
    out = np.empty((4, NTOK, D), np.float32)
    for b in range(4):
        o = res.results[b]["outT"]
        out[b] = o.reshape(D, NTOK).T
    return out
```
